# Optimizing a Trainium2 kernel written in Bass

```python
import jax, jax.numpy as jnp
from jax import lax
import numpy as np

D_MODEL = 1024
BATCH = 16
SEQ = 2048
DEPTH = 1
DEC_BATCH = 8
DEC_SEQ = 8192
PAST_LEN = 128

N_META = 16
GRID_W = 64
HEAD_DIM = 64
ATTN_WIDTH = D_MODEL // 2
N_Q_HEADS = ATTN_WIDTH // HEAD_DIM
N_KV_HEADS = N_Q_HEADS // 4
Q_PER_KV = N_Q_HEADS // N_KV_HEADS
KV_WIDTH = N_KV_HEADS * HEAD_DIM
FOURIER_WIDTH = D_MODEL // 2
N_FOURIER_GROUPS = 4
FOURIER_GROUP = FOURIER_WIDTH // N_FOURIER_GROUPS
D_FF = -(-(8 * D_MODEL) // (3 * 256)) * 256
IN_WIDTH = ATTN_WIDTH + 2 * KV_WIDTH + FOURIER_WIDTH + 2 * D_MODEL
ROPE_AXIS_DIM = HEAD_DIM // 2
ROPE_THETA = 10000.0
Q_BLOCK = 128
EPS = 1e-6

kernel_name = "hybrid_fnet_gqa_axial_rope_encoder"


def rms_norm(x, g):
    xf = x.astype(jnp.float32)
    y = xf * lax.rsqrt(jnp.mean(xf * xf, axis=-1, keepdims=True) + EPS) * g.astype(jnp.float32)
    return y.astype(x.dtype)


def grid_positions(n_tokens):
    rows = n_tokens // GRID_W
    meta_row = jnp.full((N_META,), -1.0, jnp.float32)
    meta_col = jnp.arange(N_META, dtype=jnp.float32)
    tok_row = jnp.repeat(jnp.arange(rows, dtype=jnp.float32), GRID_W)
    tok_col = jnp.tile(jnp.arange(GRID_W, dtype=jnp.float32), rows)
    return jnp.concatenate([meta_row, tok_row]), jnp.concatenate([meta_col, tok_col])


def rope_1d(x, ang):
    half = ROPE_AXIS_DIM // 2
    x1, x2 = x[..., :half], x[..., half:]
    c = jnp.cos(ang)[None, :, None, :]
    s = jnp.sin(ang)[None, :, None, :]
    return jnp.concatenate([x1 * c - x2 * s, x2 * c + x1 * s], axis=-1)


def axial_rope(x, row, col):
    inv_freq = 1.0 / (ROPE_THETA ** (jnp.arange(0, ROPE_AXIS_DIM, 2, dtype=jnp.float32) / ROPE_AXIS_DIM))
    ang_r = row[:, None] * inv_freq[None, :]
    ang_c = col[:, None] * inv_freq[None, :]
    xf = x.astype(jnp.float32)
    out = jnp.concatenate([rope_1d(xf[..., :ROPE_AXIS_DIM], ang_r),
                           rope_1d(xf[..., ROPE_AXIS_DIM:], ang_c)], axis=-1)
    return out.astype(x.dtype)


def head_rms_norm(x, g):
    xf = x.astype(jnp.float32)
    y = xf * lax.rsqrt(jnp.mean(xf * xf, axis=-1, keepdims=True) + EPS) * g.astype(jnp.float32)
    return y.astype(x.dtype)


def attention_block(qb, k, v):
    scale = HEAD_DIM ** -0.5
    s = jnp.einsum('bqhgd,bshd->bhgqs', qb, k).astype(jnp.float32) * scale
    p = jax.nn.softmax(s, axis=-1).astype(v.dtype)
    return jnp.einsum('bhgqs,bshd->bqhgd', p, v)


def bidirectional_gqa(q, k, v):
    b = q.shape[0]
    out_meta = attention_block(q[:, :N_META], k, v)
    q_real = q[:, N_META:]
    n = q_real.shape[1]
    nb = n // Q_BLOCK
    q_blocks = q_real.reshape(b, nb, Q_BLOCK, N_KV_HEADS, Q_PER_KV, HEAD_DIM).swapaxes(0, 1)
    out_blocks = lax.map(lambda qb: attention_block(qb, k, v), q_blocks)
    out_real = out_blocks.swapaxes(0, 1).reshape(b, n, N_KV_HEADS, Q_PER_KV, HEAD_DIM)
    out = jnp.concatenate([out_meta, out_real], axis=1)
    return out.reshape(b, out.shape[1], ATTN_WIDTH)


def fourier_mix(f):
    b, l, _ = f.shape
    fg = f.astype(jnp.float32).reshape(b, l, N_FOURIER_GROUPS, FOURIER_GROUP)
    y = jnp.real(jnp.fft.fftn(fg, axes=(1, 3), norm='ortho'))
    return y.reshape(b, l, FOURIER_WIDTH).astype(f.dtype)


def encoder_layer(x, row, col, norm_mix_g, w_in, q_norm_g, k_norm_g, w_attn_o, w_four_o, w_out,
                  norm_ffn_g, w_ffn_in, w_ffn_out):
    b, l, _ = x.shape
    h = rms_norm(x, norm_mix_g)
    z = h @ w_in
    o0 = ATTN_WIDTH
    o1 = o0 + KV_WIDTH
    o2 = o1 + KV_WIDTH
    o3 = o2 + FOURIER_WIDTH
    o4 = o3 + D_MODEL
    q = z[..., :o0].reshape(b, l, N_Q_HEADS, HEAD_DIM)
    k = z[..., o0:o1].reshape(b, l, N_KV_HEADS, HEAD_DIM)
    v = z[..., o1:o2].reshape(b, l, N_KV_HEADS, HEAD_DIM)
    f = z[..., o2:o3]
    g_attn = jax.nn.sigmoid(z[..., o3:o4])
    g_four = jax.nn.sigmoid(z[..., o4:])
    q = axial_rope(head_rms_norm(q, q_norm_g), row, col)
    k = axial_rope(head_rms_norm(k, k_norm_g), row, col)
    q = q.reshape(b, l, N_KV_HEADS, Q_PER_KV, HEAD_DIM)
    a = bidirectional_gqa(q, k, v) @ w_attn_o
    fo = fourier_mix(f) @ w_four_o
    merged = g_attn * a + g_four * fo
    x = x + merged @ w_out
    h2 = rms_norm(x, norm_ffn_g)
    u = h2 @ w_ffn_in
    gate, up = u[..., :D_FF], u[..., D_FF:]
    x = x + (jax.nn.silu(gate) * up) @ w_ffn_out
    return x


def encode(x, meta_tokens, norm_mix_g, w_in, q_norm_g, k_norm_g, w_attn_o, w_four_o, w_out,
           norm_ffn_g, w_ffn_in, w_ffn_out, final_norm_g):
    b, n, d = x.shape
    row, col = grid_positions(n)
    meta = jnp.broadcast_to(meta_tokens.astype(x.dtype)[None], (b, N_META, d))
    h = jnp.concatenate([meta, x], axis=1)
    for layer in range(DEPTH):
        h = encoder_layer(h, row, col, norm_mix_g[layer], w_in[layer], q_norm_g[layer], k_norm_g[layer],
                          w_attn_o[layer], w_four_o[layer], w_out[layer], norm_ffn_g[layer],
                          w_ffn_in[layer], w_ffn_out[layer])
    h = rms_norm(h, final_norm_g)
    return h[:, N_META:]


def setup_inputs(seed: int = 0) -> dict:
    key = jax.random.key(seed)
    ks = jax.random.split(key, 14)

    def w(k, shape, fan_in, extra=1.0):
        return jax.random.normal(k, shape, jnp.float32) * (fan_in ** -0.5) * extra

    def gain(k, shape):
        return 1.0 + 0.02 * jax.random.normal(k, shape, jnp.float32)

    return {
        "x_prompt": jax.random.normal(ks[0], (BATCH, SEQ, D_MODEL), jnp.float32),
        "x_sample": jax.random.normal(ks[1], (DEC_BATCH, DEC_SEQ, D_MODEL), jnp.float32),
        "meta_tokens": jax.random.normal(ks[2], (N_META, D_MODEL), jnp.float32),
        "norm_mix_g": gain(ks[3], (DEPTH, D_MODEL)),
        "w_in": w(ks[4], (DEPTH, D_MODEL, IN_WIDTH), D_MODEL),
        "q_norm_g": gain(ks[5], (DEPTH, HEAD_DIM)),
        "k_norm_g": gain(ks[6], (DEPTH, HEAD_DIM)),
        "w_attn_o": w(ks[7], (DEPTH, ATTN_WIDTH, D_MODEL), ATTN_WIDTH),
        "w_four_o": w(ks[8], (DEPTH, FOURIER_WIDTH, D_MODEL), FOURIER_WIDTH),
        "w_out": w(ks[9], (DEPTH, D_MODEL, D_MODEL), D_MODEL),
        "norm_ffn_g": gain(ks[10], (DEPTH, D_MODEL)),
        "w_ffn_in": w(ks[11], (DEPTH, D_MODEL, 2 * D_FF), D_MODEL),
        "w_ffn_out": w(ks[12], (DEPTH, D_FF, D_MODEL), D_FF),
        "final_norm_g": gain(ks[13], (D_MODEL,)),
    }


def reference(x_prompt, x_sample, meta_tokens, norm_mix_g, w_in, q_norm_g, k_norm_g, w_attn_o, w_four_o,
              w_out, norm_ffn_g, w_ffn_in, w_ffn_out, final_norm_g):
    y_prompt = encode(x_prompt, meta_tokens, norm_mix_g, w_in, q_norm_g, k_norm_g, w_attn_o, w_four_o,
                      w_out, norm_ffn_g, w_ffn_in, w_ffn_out, final_norm_g)
    y_sample = encode(x_sample, meta_tokens, norm_mix_g, w_in, q_norm_g, k_norm_g, w_attn_o, w_four_o,
                      w_out, norm_ffn_g, w_ffn_in, w_ffn_out, final_norm_g)
    return (y_prompt, y_sample)
```

```python
import bisect
from contextlib import ExitStack
import numpy as np
import ml_dtypes
import concourse.bass as bass
import concourse.mybir as mybir
from concourse.bass_utils import run_bass_kernel_spmd

F32 = mybir.dt.float32
BF16 = mybir.dt.bfloat16
AF = mybir.ActivationFunctionType
ALU = mybir.AluOpType
AX = mybir.AxisListType

D = 1024
DFF = 2816
NJ = DFF // 128
WIN_COLS = 3840
C_K, C_V, C_F, C_G = 512, 640, 768, 1792
FACT = {528: (24, 22), 1040: (40, 26), 2064: (86, 24), 8208: (114, 72)}
EPS = 1e-6
EXP_SHIFT = -10.0

ENGS = ["pe", "act", "dve", "pool"]
EIDX = {e: i for i, e in enumerate(ENGS)}
EPOCH = 30000


class DSem:
    def __init__(self, h, name):
        self.h = h
        self.count = 0
        self.name = name


class _Rec:
    def __init__(self):
        self.call = None

    def __getattr__(self, name):
        def f(*a, **k):
            self.call = (name, a, k)
            return self
        return f


class Sched:
    def __init__(self, nc, stack):
        self.nc = nc
        self.stack = stack
        self.streams = {e: [] for e in ENGS + ["sp"]}
        self.nops = {e: 0 for e in ENGS}
        self.snap = {e: [] for e in ENGS}
        self.signaled = {e: [] for e in ENGS}
        self.clock = {e: [-1, -1, -1, -1] for e in ENGS + ["sp"]}
        self.known_sem = {e: {} for e in ENGS + ["sp"]}
        self.res = {}
        self.dsems = []

    def dsem(self, name):
        h = self.stack.enter_context(self.nc.semaphore(name))
        s = DSem(h, name)
        self.dsems.append(s)
        return s

    def _need(self, eng, tok):
        if tok is None:
            return
        if tok[0] == "eng":
            _, E, idx = tok
            if E == "pe" and eng == "pe":
                return
            if self.clock[eng][EIDX[E]] >= idx:
                return
            sl = self.signaled[E]
            j = bisect.bisect_left(sl, idx)
            if j < len(sl) and sl[j] - idx <= (3 if E == "pe" else 0):
                tgt = sl[j]
            else:
                bisect.insort(sl, idx)
                tgt = idx
            self.streams[eng].append(("we", E, tgt))
            sn = self.snap[E][tgt]
            c = self.clock[eng]
            for i in range(4):
                if sn[i] > c[i]:
                    c[i] = sn[i]
            if c[EIDX[E]] < tgt:
                c[EIDX[E]] = tgt
        else:
            _, s, cnt = tok
            if self.known_sem[eng].get(s, 0) >= cnt:
                return
            self.streams[eng].append(("ws", s, cnt))
            self.known_sem[eng][s] = cnt

    def op(self, eng, fn, reads=(), writes=(), dsem=None):
        rec = _Rec()
        fn(rec)
        fn = rec.call
        deps = []
        for r in reads:
            st = self.res.get(r)
            if st is not None:
                deps.append(st[0])
        for w in writes:
            st = self.res.get(w)
            if st is not None:
                deps.append(st[0])
                deps.extend(st[1])
        for d in deps:
            self._need(eng, d)
        if dsem is not None:
            dsem.count += 16
            tok = ("dma", dsem, dsem.count)
            self.streams[eng].append(("dma", fn, dsem))
        else:
            idx = self.nops[eng]
            self.nops[eng] = idx + 1
            tok = ("eng", eng, idx)
            self.snap[eng].append(tuple(self.clock[eng]))
            if eng == "pe":
                self.clock[eng][0] = idx
            self.streams[eng].append(("op", fn, idx))
        for w in writes:
            self.res[w] = [tok, []]
        for r in reads:
            st = self.res.get(r)
            if st is None:
                st = self.res[r] = [None, []]
            rl = st[1]
            for i, t in enumerate(rl):
                if t[0] == tok[0] and t[1] == tok[1]:
                    rl[i] = tok
                    break
            else:
                rl.append(tok)
        return tok

    def fence(self):
        toks = [("eng", E, self.nops[E] - 1) for E in ENGS if self.nops[E] > 0]
        toks += [("dma", s, s.count) for s in self.dsems if s.count]
        for eng in ENGS + ["sp"]:
            for t in toks:
                self._need(eng, t)
        self.res = {}

    def emit(self, block):
        nc = self.nc
        esem = {}
        for e in ENGS:
            self.signaled[e].sort()
            n = len(self.signaled[e])
            ne = max(1, (n + EPOCH - 1) // EPOCH)
            esem[e] = [self.stack.enter_context(nc.semaphore("es_%s_%d" % (e, k))) for k in range(ne)]
        rank = {e: {idx: i for i, idx in enumerate(self.signaled[e])} for e in ENGS}

        def run(eng, eo):
            for ent in self.streams[eng]:
                k = ent[0]
                if k == "we":
                    r = rank[ent[1]][ent[2]]
                    eo.wait_ge(esem[ent[1]][r // EPOCH], r % EPOCH + 1)
                elif k == "ws":
                    eo.wait_ge(ent[1].h, ent[2])
                elif k == "dma":
                    nm, a, kw = ent[1]
                    getattr(eo, nm)(*a, **kw).then_inc(ent[2].h, 16)
                else:
                    nm, a, kw = ent[1]
                    ins = getattr(eo, nm)(*a, **kw)
                    r = rank[eng].get(ent[2])
                    if r is not None:
                        ins.then_inc(esem[eng][r // EPOCH], 1)

        @block.tensor
        def _(t):
            run("pe", t)

        @block.scalar
        def _(t):
            run("act", t)

        @block.vector
        def _(t):
            run("dve", t)

        @block.gpsimd
        def _(t):
            run("pool", t)

        @block.sync
        def _(t):
            run("sp", t)


class Arena:
    def __init__(self, base, nwords):
        self.base = base
        self.n = nwords
        self.top = 0

    def alloc(self, shape, dtype):
        nel = int(np.prod(shape))
        words = nel if dtype == F32 else (nel + 1) // 2
        a = self.base[:, self.top:self.top + words]
        self.top += words
        assert self.top <= self.n, ("SBUF arena overflow", self.top, self.n)
        if dtype != F32:
            a = a.bitcast(dtype)[:, 0:nel]
        if len(shape) == 2:
            a = a.rearrange("p (a b) -> p a b", a=shape[0], b=shape[1])
        elif len(shape) == 3:
            a = a.rearrange("p (a b c) -> p a b c", a=shape[0], b=shape[1], c=shape[2])
        return a


def host_consts(seq_ns):
    c = {}
    c["ident"] = np.eye(128, dtype=np.float32).astype(ml_dtypes.bfloat16)
    c["identf"] = np.eye(128, dtype=np.float32)
    k = np.arange(128, dtype=np.float64)
    ang = 2 * np.pi * np.outer(k, k) / 128.0
    c["cs128"] = (np.concatenate([np.cos(ang), -np.sin(ang)], 1) / np.sqrt(128.0)).astype(np.float32)
    inv_freq = 1.0 / (10000.0 ** (np.arange(0, 32, 2, dtype=np.float64) / 32.0))
    for n in sorted(set(seq_ns)):
        L = n + 16
        N1, N2 = FACT[L]
        nkb = n // 128 + 1
        t = np.arange(n)
        row = np.concatenate([t // 64, np.full(16, -1)]).astype(np.float64)
        col = np.concatenate([t % 64, np.arange(16)]).astype(np.float64)
        ar = row[:, None] * inv_freq[None]
        ac = col[:, None] * inv_freq[None]
        tab = np.zeros((nkb * 128, 128), np.float32)
        tab[:L, 0:64] = np.concatenate([np.cos(ar), np.cos(ar), np.cos(ac), np.cos(ac)], 1)
        tab[:L, 64:128] = np.concatenate([-np.sin(ar), np.sin(ar), -np.sin(ac), np.sin(ac)], 1)
        c["rope%d" % n] = tab
        n1 = np.arange(N1, dtype=np.float64)
        n2 = np.arange(N2, dtype=np.float64)
        a1 = 2 * np.pi * (n1[:, None, None] * n1[None, None, :] / N1 + n2[None, :, None] * n1[None, None, :] / L)
        t1 = np.stack([np.cos(a1), np.sin(a1), -np.sin(a1)], 2)
        c["t1_%d" % n] = t1.astype(np.float32).astype(ml_dtypes.bfloat16)
        a2 = 2 * np.pi * np.outer(n2, n2) / N2
        t2 = np.stack([np.cos(a2), np.sin(a2)], 1) / np.sqrt(float(L))
        c["t2_%d" % n] = t2.astype(np.float32).astype(ml_dtypes.bfloat16)
    return c


def build(seq_ns):
    nc = bass.Bass("TRN2", target_bir_lowering=False)
    nseq = len(seq_ns)

    def din(name, shape, dt=F32):
        return nc.dram_tensor(name, list(shape), dt, kind="ExternalInput")

    xs = [din("x%d" % i, [n, D]).ap() for i, n in enumerate(seq_ns)]
    ys = [nc.dram_tensor("y%d" % i, [n, D], F32, kind="ExternalOutput").ap() for i, n in enumerate(seq_ns)]
    meta = din("meta_tokens", [16, D]).ap()
    g_mix_t = din("norm_mix_g", [D])
    g_q_t = din("q_norm_g", [64])
    g_k_t = din("k_norm_g", [64])
    g_ffn_t = din("norm_ffn_g", [D])
    g_fin_t = din("final_norm_g", [D])
    w_in = din("w_in", [D, 3328]).ap()
    w_ao = din("w_attn_o", [512, D]).ap()
    w_fo = din("w_four_o", [512, D]).ap()
    w_out = din("w_out", [D, D]).ap()
    w_fi = din("w_ffn_in", [D, 2 * DFF]).ap()
    w_f2 = din("w_ffn_out", [DFF, D]).ap()
    ident_d = din("ident", [128, 128], BF16).ap()
    identf_d = din("identf", [128, 128]).ap()
    cs128_d = din("cs128", [128, 256]).ap()
    rope_d, t1_d, t2_d = {}, {}, {}
    for n in sorted(set(seq_ns)):
        N1, N2 = FACT[n + 16]
        rope_d[n] = din("rope%d" % n, [(n // 128 + 1) * 128, 128]).ap()
        t1_d[n] = din("t1_%d" % n, [N1, N2, 3, N1], BF16).ap()
        t2_d[n] = din("t2_%d" % n, [N2, 2, N2], BF16).ap()

    nmax = max(seq_ns)
    win_s = nc.dram_tensor("win_s", [128, 8 * WIN_COLS], BF16).ap()
    zf_s = nc.dram_tensor("zf_s", [nmax + 16, 1024], BF16).ap()
    N1m, N2m = FACT[nmax + 16]
    bs_s = nc.dram_tensor("bs_s", [N1m * N2m * 1024], BF16).ap()
    qt_s = nc.dram_tensor("qt_s", [nmax // 128, 128, 512], BF16).ap()
    ot_s = nc.dram_tensor("ot_s", [nmax // 128, 128, 512], BF16).ap()
    g_s = nc.dram_tensor("g_s", [128, 16, nmax], BF16).ap()
    x1_s = [nc.dram_tensor("x1_s%d" % i, [n, D], F32).ap() for i, n in enumerate(seq_ns)]
    w1_s = nc.dram_tensor("w1_s", [D, 2 * DFF], BF16).ap()
    w2_s = nc.dram_tensor("w2_s", [DFF, D], BF16).ap()

    ARW = 53100
    with ExitStack() as st:
        arena_t = st.enter_context(nc.sbuf_tensor("arena", [128, ARW], F32))
        ps = st.enter_context(nc.psum_tensor("ps", [128, 8, 512], F32))
        block = st.enter_context(nc.Block())
        S = Sched(nc, st)
        AR = Arena(arena_t, ARW)
        psflat = ps[:, :, :].rearrange("p b f -> p (b f)")

        def psb(b):
            return ps[:, b, :].bitcast(BF16)

        sem_cache = {}
        WINK = ["win_q", "win_kv", "win_g", "win_f"]

        def dsem(name):
            if name not in sem_cache:
                sem_cache[name] = S.dsem(name)
            return sem_cache[name]

        def dma(eng, out, in_, reads=(), writes=(), sem="misc", **kw):
            S.op(eng, lambda e: e.dma_start(out=out, in_=in_, **kw), reads=reads, writes=writes, dsem=dsem(sem))

        ident = AR.alloc([128], BF16)
        ones64 = AR.alloc([64], BF16)
        eps_t = AR.alloc([1], F32)
        ebias = AR.alloc([1], F32)
        gqk = AR.alloc([640], F32)
        tmp2 = AR.alloc([2], F32)
        P_MARK = AR.top

        dma("sp", ident, ident_d[:, :], writes=["ident"], sem="c0")
        S.op("pool", lambda e: e.memset(ones64, 1.0), writes=["ones64"])
        S.op("pool", lambda e: e.memset(eps_t, EPS), writes=["eps"])
        dma("sp", gqk[:, 0:512].rearrange("p (h d) -> p h d", d=64), bass.AP(g_q_t, 0, [[0, 128], [0, 8], [1, 64]]),
            writes=["gqk"], sem="c1")
        dma("sp", gqk[:, 512:640].rearrange("p (h d) -> p h d", d=64), bass.AP(g_k_t, 0, [[0, 128], [0, 2], [1, 64]]),
            writes=["gqk2"], sem="c2")
        S.op("dve", lambda e: e.tensor_reduce(out=tmp2[:, 0:1], in_=gqk[:, 0:64], axis=AX.X, op=ALU.max,
                                              apply_absolute_value=True), reads=["gqk"], writes=["tmp2a"])
        S.op("dve", lambda e: e.tensor_reduce(out=tmp2[:, 1:2], in_=gqk[:, 512:576], axis=AX.X, op=ALU.max,
                                              apply_absolute_value=True), reads=["gqk2"], writes=["tmp2b"])
        S.op("dve", lambda e: e.tensor_tensor(out=ebias, in0=tmp2[:, 0:1], in1=tmp2[:, 1:2], op=ALU.mult),
             reads=["tmp2a", "tmp2b"], writes=["ebias"])
        S.op("dve", lambda e: e.tensor_scalar(out=ebias, in0=ebias, scalar1=-8.0, scalar2=None, op0=ALU.mult),
             reads=["ebias"], writes=["ebias"])

        def bcast_row(t, nel):
            return bass.AP(t, 0, [[0, 128], [1, nel]])

        def rms_rstd(x_ap, ntok, junk, ssq, col, key_x, key_junk, key_ssq):
            S.op("dve", lambda e: e.scalar_tensor_tensor(out=junk[:ntok], in0=x_ap, scalar=1.0, in1=x_ap,
                                                         op0=ALU.mult, op1=ALU.mult, accum_out=ssq[:ntok, col:col + 1]),
                 reads=[key_x], writes=[key_junk, key_ssq])

        def sqrt_recip(ssq_ap, scale, keys):
            S.op("act", lambda e: e.activation(out=ssq_ap, in_=ssq_ap, func=AF.Sqrt, bias=eps_t[:ssq_ap.shape[0], 0:1],
                                               scale=scale), reads=keys + ["eps"], writes=keys)
            S.op("dve", lambda e: e.reciprocal(out=ssq_ap, in_=ssq_ap), reads=keys, writes=keys)

        AR.top = P_MARK
        KT_W = (nmax // 128 + 1) * 128
        Win = AR.alloc([8, WIN_COLS], BF16)
        A_MARK = AR.top
        cs128 = AR.alloc([256], F32)
        identf = AR.alloc([128], F32)
        wfT = [AR.alloc([128], F32) for _ in range(2)]
        stg = [AR.alloc([3328], F32) for _ in range(3)]
        dma("sp", identf, identf_d[:, :], writes=["identf"], sem="c3")
        dma("sp", cs128, cs128_d[:, :], writes=["cs128"], sem="c4")

        def load_stg(c):
            dma("sp", stg[c % 3], w_in[c * 128:(c + 1) * 128, :], writes=["stg%d" % (c % 3)], sem="stg%d" % (c % 3))

        load_stg(0)
        load_stg(1)
        it = 0
        for c in range(8):
            if c + 2 < 8:
                load_stg(c + 2)
            st_ = stg[c % 3]
            sk = "stg%d" % (c % 3)
            S.op("dve", lambda e: e.tensor_copy(out=Win[:, c, 0:512].rearrange("p (g k d) -> p g k d", g=4, k=2),
                                                in_=st_[:, 0:512].rearrange("p (k g d) -> p g k d", k=2, g=4)),
                 reads=[sk], writes=["win_q"])
            S.op("pool", lambda e: e.tensor_copy(out=Win[:, c, 512:768], in_=st_[:, 512:768]), reads=[sk], writes=["win_kv"])
            S.op("act", lambda e: e.activation(out=Win[:, c, C_G:WIN_COLS], in_=st_[:, 1280:3328], func=AF.Copy),
                 reads=[sk], writes=["win_g"])
            for g in range(4):
                b = it % 2
                it += 1
                S.op("pe", lambda e: e.transpose(out=ps[:, b, 0:128], in_=st_[:, 768 + g * 128:768 + (g + 1) * 128], identity=identf),
                     reads=[sk, "identf"], writes=["ps%d" % b])
                S.op("dve", lambda e: e.tensor_copy(out=wfT[b], in_=ps[:, b, 0:128]), reads=["ps%d" % b], writes=["wfT%d" % b])
                S.op("pe", lambda e: e.matmul(ps[:, 2 + b, 0:256], lhsT=wfT[b], rhs=cs128, start=True, stop=True),
                     reads=["wfT%d" % b, "cs128"], writes=["ps%d" % (2 + b)])
                S.op("dve", lambda e: e.tensor_copy(
                    out=Win[:, c, C_F:C_G].rearrange("p (r g k) -> p r g k", r=2, g=4)[:, :, g, :],
                    in_=ps[:, 2 + b, 0:256].rearrange("p (r k) -> p r k", r=2)),
                    reads=["ps%d" % (2 + b)], writes=["win_f"])
        dma("sp", win_s[:, :], Win[:, :, :].rearrange("p c f -> p (c f)"), reads=WINK, sem="ws")
        S.fence()

        for si, n in enumerate(seq_ns):
            L = n + 16
            N1, N2 = FACT[L]
            nqb = n // 128
            nkb = nqb + 1
            nmt = n // 512
            x_d = xs[si]
            zf = zf_s[0:L, :]
            bs = bs_s[0:N1 * N2 * 1024].rearrange("(k n r c) -> k n r c", k=N1, n=N2, r=2)

            AR.top = A_MARK
            KT = AR.alloc([KT_W], BF16)
            Vt = AR.alloc([KT_W // 128, 128], BF16)
            B_MARK = AR.top
            gmix = AR.alloc([D], F32)
            xbuf = [AR.alloc([4, D], F32) for _ in range(2)]
            junk = AR.alloc([D], BF16)
            ssq = AR.alloc([8], F32)
            hb = AR.alloc([4, D], BF16)
            hT = [AR.alloc([8, 512], BF16) for _ in range(2)]
            zc = [AR.alloc([640], F32) for _ in range(2)]
            sq = AR.alloc([640], F32)
            ssh = AR.alloc([16], F32)
            qn = AR.alloc([640], F32)
            t1b = AR.alloc([640], F32)
            t2b = AR.alloc([640], F32)
            qko = [AR.alloc([640], BF16) for _ in range(2)]
            zfs = [AR.alloc([1024], BF16) for _ in range(2)]
            qTs = [AR.alloc([512], BF16) for _ in range(2)]
            gst = [AR.alloc([4, 512], BF16) for _ in range(2)]
            rt = [AR.alloc([128], F32) for _ in range(2)]

            if si > 0:
                dma("sp", Win[:, :, :].rearrange("p c f -> p (c f)"), win_s[:, :], writes=WINK, sem="wl")
            dma("sp", gmix, bcast_row(g_mix_t, D), writes=["gmix"], sem="c6")

            mts = [(m * 512, [128] * 4, False) for m in range(nmt)] + [(n, [16], True)]

            def load_x(mi):
                t0, subs, is_meta = mts[mi]
                sl = mi % 2
                if is_meta:
                    dma("sp", xbuf[sl][0:16, 0, :], meta[:, :], writes=["x%d_0" % sl], sem="x%d" % sl)
                else:
                    dma("sp", xbuf[sl][:, :, :], x_d[t0:t0 + 512, :].rearrange("(s p) d -> p s d", p=128),
                        writes=["x%d_%d" % (sl, s) for s in range(4)], sem="x%d" % sl)

            def front_a(mi, s):
                t0, subs, is_meta = mts[mi]
                sl = mi % 2
                ntok = subs[s]
                rms_rstd(xbuf[sl][:ntok, s, :], ntok, junk, ssq, s, "x%d_%d" % (sl, s), "junk", "ssq%d" % s)
                S.op("act", lambda e: e.activation(out=ssq[:ntok, s:s + 1], in_=ssq[:ntok, s:s + 1], func=AF.Sqrt,
                                                   bias=eps_t[:ntok, 0:1], scale=1.0 / D), reads=["ssq%d" % s, "eps"], writes=["ssq%d" % s])

            def front_b(mi, s):
                t0, subs, is_meta = mts[mi]
                sl = mi % 2
                ntok = subs[s]
                S.op("dve", lambda e: e.reciprocal(out=ssq[:ntok, s:s + 1], in_=ssq[:ntok, s:s + 1]), reads=["ssq%d" % s], writes=["ssq%d" % s])
                S.op("dve", lambda e: e.scalar_tensor_tensor(
                    out=hb[:ntok, s, :], in0=xbuf[sl][:ntok, s, :], scalar=ssq[:ntok, s:s + 1], in1=gmix[:ntok],
                    op0=ALU.mult, op1=ALU.mult), reads=["x%d_%d" % (sl, s), "ssq%d" % s, "gmix"], writes=["hb%d" % s])

            def front_dve(mi, s):
                front_a(mi, s)
                front_b(mi, s)

            def front_pe(mi, s):
                t0, subs, is_meta = mts[mi]
                sl = mi % 2
                ntok = subs[s]
                for c in range(8):
                    S.op("pe", lambda e: e.transpose(
                        out=psb(4)[:, c * 128:c * 128 + ntok], in_=hb[:ntok, s, c * 128:(c + 1) * 128],
                        identity=ident[:ntok, :ntok]), reads=["hb%d" % s, "ident"], writes=["ps4"])
                S.op("dve", lambda e: e.tensor_copy(
                    out=hT[sl][:, :, s * 128:s * 128 + ntok],
                    in_=psb(4).rearrange("p (c t) -> p c t", c=8)[:, :, 0:ntok]),
                    reads=["ps4"], writes=["hT%d_%d" % (sl, s)])

            def front(mi):
                for s in range(len(mts[mi][1])):
                    front_dve(mi, s)
                    front_pe(mi, s)

            def chain_a(ntok, so, kb, is_meta):
                z = zc[so]
                S.op("dve", lambda e: e.tensor_tensor(out=sq[:ntok], in0=z[:ntok], in1=z[:ntok], op=ALU.mult),
                     reads=["zc%d" % so], writes=["sq"])
                S.op("dve", lambda e: e.tensor_reduce(out=ssh[:ntok, 0:10], in_=sq[:ntok].rearrange("p (h d) -> p h d", d=64),
                                                      axis=AX.X, op=ALU.add), reads=["sq"], writes=["ssh"])
                S.op("act", lambda e: e.activation(out=ssh[:ntok, 0:10], in_=ssh[:ntok, 0:10], func=AF.Sqrt,
                                                   bias=eps_t[:ntok, 0:1], scale=1.0 / 64), reads=["ssh", "eps"], writes=["ssh"])

            def chain_b(ntok, so, kb, is_meta):
                z = zc[so]
                S.op("dve", lambda e: e.reciprocal(out=ssh[:ntok, 0:10], in_=ssh[:ntok, 0:10]), reads=["ssh"], writes=["ssh"])
                S.op("dve", lambda e: e.tensor_tensor(
                    out=qn[:ntok].rearrange("p (h d) -> p h d", d=64), in0=z[:ntok].rearrange("p (h d) -> p h d", d=64),
                    in1=ssh[:ntok, 0:10].unsqueeze(2).broadcast_to([ntok, 10, 64]), op=ALU.mult),
                    reads=["zc%d" % so, "ssh"], writes=["qn"])
                S.op("dve", lambda e: e.tensor_tensor(out=qn[:ntok], in0=qn[:ntok], in1=gqk[:ntok], op=ALU.mult),
                     reads=["qn", "gqk", "gqk2"], writes=["qn"])
                S.op("dve", lambda e: e.tensor_tensor(
                    out=t1b[:ntok].rearrange("p (h d) -> p h d", d=64), in0=qn[:ntok].rearrange("p (h d) -> p h d", d=64),
                    in1=rt[so][:ntok, 0:64].unsqueeze(1).broadcast_to([ntok, 10, 64]), op=ALU.mult),
                    reads=["qn", "rt%d" % so], writes=["t1b"])
                for hf in range(2):
                    S.op("dve", lambda e: e.tensor_tensor(
                        out=t2b[:ntok].rearrange("p (h a b) -> p h a b", a=2, b=32)[:, :, :, hf * 16:hf * 16 + 16],
                        in0=qn[:ntok].rearrange("p (h a b) -> p h a b", a=2, b=32)[:, :, :, (1 - hf) * 16:(1 - hf) * 16 + 16],
                        in1=rt[so][:ntok, 64:128].rearrange("p (a b) -> p a b", a=2)[:, :, hf * 16:hf * 16 + 16]
                        .unsqueeze(1).broadcast_to([ntok, 10, 2, 16]), op=ALU.mult),
                        reads=["qn", "rt%d" % so], writes=["t2b%d" % hf])
                S.op("dve", lambda e: e.tensor_tensor(out=qko[so][:ntok], in0=t1b[:ntok], in1=t2b[:ntok], op=ALU.add),
                     reads=["t1b", "t2b0", "t2b1"], writes=["qko%d" % so])

            def chain(ntok, so, kb, is_meta):
                chain_a(ntok, so, kb, is_meta)
                chain_b(ntok, so, kb, is_meta)

            def chain_pe(ntok, so, kb, is_meta):
                for j in range(5):
                    S.op("pe", lambda e: e.transpose(
                        out=psb(5)[:, j * 128:j * 128 + ntok], in_=qko[so][:ntok, j * 128:(j + 1) * 128],
                        identity=ident[:ntok, :ntok]), reads=["qko%d" % so, "ident"], writes=["ps5"])
                S.op("dve", lambda e: e.tensor_copy(out=KT[:, kb * 128:kb * 128 + ntok], in_=psb(5)[:, 512:512 + ntok]),
                     reads=["ps5"], writes=["KT%d" % kb])
                if not is_meta:
                    S.op("dve", lambda e: e.tensor_copy(out=qTs[so], in_=psb(5)[:, 0:512]), reads=["ps5"], writes=["qTs%d" % so])
                    dma("sp", qt_s[kb, :, :], qTs[so], reads=["qTs%d" % so], sem="qTs%d" % so)

            load_x(0)
            front(0)
            sub_ctr = 0
            gate_ctr = 0
            pending = None
            pending2 = None
            for mi, (t0, subs, is_meta) in enumerate(mts):
                sl = mi % 2
                if mi + 1 < len(mts):
                    load_x(mi + 1)
                ns = len(subs)
                for s, ntok in enumerate(subs):
                    kb = (t0 // 128) + s
                    so = sub_ctr % 2
                    sub_ctr += 1
                    dma("sp", rt[so][:, :], rope_d[n][kb * 128:(kb + 1) * 128, :], writes=["rt%d" % so], sem="rt%d" % so)
                    for j, (c0, c1) in enumerate([(0, 512), (512, 1024), (1024, 1536), (1536, 1792)]):
                        for c in range(8):
                            S.op("pe", lambda e: e.matmul(
                                ps[:ntok, j, 0:c1 - c0], lhsT=hT[sl][:, c, s * 128:s * 128 + ntok], rhs=Win[:, c, c0:c1],
                                start=(c == 0), stop=(c == 7)),
                                reads=["hT%d_%d" % (sl, s)] + WINK, writes=["ps%d" % j])
                    S.op("dve", lambda e: e.tensor_copy(out=zc[so][:ntok], in_=psflat[:ntok, 0:640]),
                         reads=["ps0", "ps1"], writes=["zc%d" % so])
                    S.op("dve", lambda e: e.tensor_copy(out=Vt[:ntok, kb, :], in_=psflat[:ntok, 640:768]),
                         reads=["ps1"], writes=["Vt%d" % kb])
                    S.op("act", lambda e: e.activation(out=zfs[so][:ntok], in_=psflat[:ntok, 768:1792], func=AF.Copy),
                         reads=["ps1", "ps2", "ps3"], writes=["zfs%d" % so])
                    p0 = 0 if is_meta else 16 + t0 + s * 128
                    dma("sp", zf[p0:p0 + ntok, :], zfs[so][:ntok], reads=["zfs%d" % so], sem="zfs%d" % so)
                    has_next = mi + 1 < len(mts) and s < len(mts[mi + 1][1])
                    if pending is not None:
                        chain_a(*pending)
                    if has_next:
                        front_a(mi + 1, s)
                    if not is_meta:
                        for jj in range(4):
                            j = s * 4 + jj
                            gb = 6 + (gate_ctr % 2)
                            gsl = (gate_ctr // 4) % 2
                            gate_ctr += 1
                            for c in range(8):
                                S.op("pe", lambda e: e.matmul(
                                    ps[:, gb, :], lhsT=Win[:, c, C_G + j * 128:C_G + (j + 1) * 128], rhs=hT[sl][:, c, :],
                                    start=(c == 0), stop=(c == 7)),
                                    reads=["hT%d_%d" % (sl, q) for q in range(4)] + WINK, writes=["ps%d" % gb])
                            S.op("act", lambda e: e.activation(out=gst[gsl][:, jj, :], in_=ps[:, gb, :], func=AF.Sigmoid),
                                 reads=["ps%d" % gb], writes=["gst%d_%d" % (gsl, jj)])
                            if jj == 3:
                                dma("sp", g_s[:, j - 3:j + 1, t0:t0 + 512], gst[gsl][:, :, :],
                                    reads=["gst%d_%d" % (gsl, q) for q in range(4)], sem="gst%d" % gsl)
                    if has_next:
                        front_b(mi + 1, s)
                    if pending is not None:
                        chain_b(*pending)
                    if pending2 is not None:
                        chain_pe(*pending2)
                    pending2 = pending
                    pending = (ntok, so, kb, is_meta)
                    if has_next:
                        front_pe(mi + 1, s)
            if pending2 is not None:
                chain_pe(*pending2)
            chain(*pending)
            chain_pe(*pending)
            S.fence()

            AR.top = B_MARK
            qT = [AR.alloc([512], BF16) for _ in range(2)]
            pT = [AR.alloc([2, 512], BF16) for _ in range(4)]
            psum2 = [AR.alloc([2, 512], BF16) for _ in range(2)]
            rec = AR.alloc([512], F32)
            osb = AR.alloc([512], F32)
            oT = [AR.alloc([512], BF16) for _ in range(2)]
            dma("sp", qT[0], qt_s[0, :, :], writes=["qT0"], sem="qT0")
            if si == nseq - 1:
                for c in range(8):
                    dma("pool", w1_s[c * 128:(c + 1) * 128, :], w_fi[c * 128:(c + 1) * 128, :], sem="wc1")
                for jq in range(NJ):
                    dma("pool", w2_s[jq * 128:(jq + 1) * 128, :], w_f2[jq * 128:(jq + 1) * 128, :], sem="wc2")
            its = [(qb, kb) for qb in range(nqb) for kb in range(nkb)]

            def emit_qk(i):
                qb, kb = its[i]
                nk = 128 if kb < nkb - 1 else 16
                sb = 2 * (i % 3)
                qs = qb % 2
                for h in range(2):
                    S.op("pe", lambda e, h=h, nk=nk, sb=sb, qs=qs, kb=kb: e.matmul(
                        ps[:nk, sb + h, :], lhsT=KT[h * 64:(h + 1) * 64, kb * 128:kb * 128 + nk],
                        rhs=qT[qs][h * 64:(h + 1) * 64, :], start=True, stop=True),
                        reads=["qT%d" % qs], writes=["ps%d" % (sb + h)])

            norm_q = []

            def run_norm(cnt):
                for _ in range(cnt):
                    o_, qb_, qq = norm_q.pop(0)
                    cs = slice(qq * 128, (qq + 1) * 128)
                    S.op("dve", lambda e: e.reciprocal(out=rec[:, cs], in_=rec[:, cs]), reads=["rec"], writes=["rec%d" % qq])
                    S.op("dve", lambda e: e.tensor_tensor(out=oT[o_][:, cs], in0=osb[:, cs], in1=rec[:, cs], op=ALU.mult),
                         reads=["osb", "rec%d" % qq], writes=["oT%d_%d" % (o_, qq)])
                    if qq == 3:
                        dma("sp", ot_s[qb_, :, :], oT[o_], reads=["oT%d_%d" % (o_, q) for q in range(4)], sem="oT%d" % o_)

            emit_qk(0)
            if len(its) > 1:
                emit_qk(1)
            pend_sum = None
            for i, (qb, kb) in enumerate(its):
                nk = 128 if kb < nkb - 1 else 16
                sb = 2 * (i % 3)
                pslot = i % 4
                ob = 6
                if kb == 0 and qb + 1 < nqb:
                    dma("sp", qT[(qb + 1) % 2], qt_s[qb + 1, :, :], writes=["qT%d" % ((qb + 1) % 2)], sem="qT%d" % ((qb + 1) % 2))
                if i + 2 < len(its):
                    emit_qk(i + 2)
                S.op("act", lambda e: e.activation(
                    out=pT[pslot][:nk], in_=ps[:nk, sb:sb + 2, :], func=AF.Exp, bias=EXP_SHIFT, scale=0.125),
                    reads=["ps%d" % sb, "ps%d" % (sb + 1)], writes=["pT%d" % pslot])
                first, last = (kb == 0), (kb == nkb - 1)
                for h in range(2):
                    S.op("pe", lambda e: e.matmul(
                        ps[h * 64:(h + 1) * 64, ob, :], lhsT=Vt[:nk, kb, h * 64:(h + 1) * 64], rhs=pT[pslot][:nk, h, :],
                        start=first, stop=last, tile_position=(0, h * 64)),
                        reads=["pT%d" % pslot], writes=["ps%d_%d" % (ob, h)])
                if pend_sum is not None:
                    pp2, pst = pend_sum
                    pend_sum = None
                    for h in range(2):
                        S.op("pe", lambda e: e.matmul(
                            ps[h * 64:(h + 1) * 64, ob + 1, :], lhsT=ones64[:, :], rhs=psum2[pp2][:, h, :],
                            start=pst, stop=False, tile_position=(0, h * 64)),
                            reads=["psum2_%d" % pp2], writes=["ps%d_%d" % (ob + 1, h)])
                if last:
                    for h in range(2):
                        S.op("pe", lambda e: e.matmul(
                            ps[h * 64:(h + 1) * 64, ob + 1, :], lhsT=ones64[:nk, :], rhs=pT[pslot][:nk, h, :],
                            start=False, stop=True, tile_position=(0, h * 64)),
                            reads=["pT%d" % pslot], writes=["ps%d_%d" % (ob + 1, h)])
                elif kb % 2 == 1:
                    p2 = (kb // 2) % 2
                    pprev = (i - 1) % 4
                    S.op("dve", lambda e: e.tensor_tensor(out=psum2[p2], in0=pT[pprev], in1=pT[pslot], op=ALU.add),
                         reads=["pT%d" % pprev, "pT%d" % pslot], writes=["psum2_%d" % p2])
                    pend_sum = (p2, kb == 1)
                if norm_q and (last or (kb >= 2 and kb % 2 == 0)):
                    run_norm(len(norm_q) if last else 1)
                if last:
                    os_ = qb % 2
                    S.op("dve", lambda e: e.tensor_copy(out=rec, in_=ps[:, ob + 1, :]),
                         reads=["ps%d_0" % (ob + 1), "ps%d_1" % (ob + 1)], writes=["rec"])
                    S.op("dve", lambda e: e.tensor_copy(out=osb, in_=ps[:, ob, :]),
                         reads=["ps%d_0" % ob, "ps%d_1" % ob], writes=["osb"])
                    for qq in range(4):
                        norm_q.append((os_, qb, qq))
            run_norm(len(norm_q))
            S.fence()

            AR.top = P_MARK
            YT = AR.alloc([4, L], BF16)
            Wao = AR.alloc([4, D], BF16)
            Wfo = AR.alloc([4, D], BF16)
            Wo = AR.alloc([8, D], BF16)
            C_MARK = AR.top
            for h in range(2):
                dma("pool", Wao[h * 64:(h + 1) * 64], w_ao[h * 256:(h + 1) * 256, :].rearrange("(g d) o -> d g o", d=64),
                    writes=["Wao"], sem="w3")
            dma("pool", Wfo, w_fo.rearrange("(g c) o -> c g o", c=128), writes=["Wfo"], sem="w4")
            dma("pool", Wo, w_out.rearrange("(c p) o -> p c o", p=128), writes=["Wo"], sem="w5")
            NB = 4
            T1 = AR.alloc([N2, 3, N1], BF16)
            T2 = AR.alloc([2, N2], BF16)
            NZ = 4
            zin = [AR.alloc([NB, 1024], BF16) for _ in range(NZ)]
            bst = [AR.alloc([NB, 2, 512], BF16) for _ in range(3)]
            dma("sp", T1[:N1], t1_d[n][:, :, :, :], writes=["T1"], sem="c7")
            dma("sp", T2[:N2], t2_d[n][:, :, :], writes=["T2"], sem="c8")
            zf3 = zf.rearrange("(a b) c -> a b c", b=N2)
            grp = [(a, min(NB, N2 - a)) for a in range(0, N2, NB)]

            def load_z(gi):
                a, nb = grp[gi]
                dma("sp", zin[gi % NZ][:N1, 0:nb, :], zf3[:, a:a + nb, :], writes=["zin%d" % (gi % NZ)], sem="zin%d" % (gi % NZ))

            for gi in range(min(NZ - 1, len(grp))):
                load_z(gi)
            ctr = 0
            for gi, (a, nb) in enumerate(grp):
                if gi + NZ - 1 < len(grp):
                    load_z(gi + NZ - 1)
                zs = gi % NZ
                bs_ = gi % 3
                for j in range(nb):
                    n2 = a + j
                    br = 2 * (ctr % 2)
                    ctr += 1
                    zr = zin[zs][:N1, j, 0:512]
                    zi = zin[zs][:N1, j, 512:1024]
                    for (bank, m0, r0, m1, r1) in [(br, 0, zr, 1, zi), (br + 1, 2, zr, 0, zi)]:
                        S.op("pe", lambda e, bank=bank, m0=m0, r0=r0, n2=n2: e.matmul(
                            ps[:N1, bank, :], lhsT=T1[:N1, n2, m0, :], rhs=r0, start=True, stop=False),
                            reads=["T1", "zin%d" % zs], writes=["ps%d" % bank])
                        S.op("pe", lambda e, bank=bank, m1=m1, r1=r1, n2=n2: e.matmul(
                            ps[:N1, bank, :], lhsT=T1[:N1, n2, m1, :], rhs=r1, start=False, stop=True),
                            reads=["T1", "zin%d" % zs], writes=["ps%d" % bank])
                    if ctr % 2 == 0:
                        S.op("dve", lambda e: e.tensor_copy(out=bst[bs_][:N1, j, :, :], in_=ps[:N1, br:br + 2, :]),
                             reads=["ps%d" % br, "ps%d" % (br + 1)], writes=["bst%d_%d" % (bs_, j)])
                    else:
                        S.op("act", lambda e: e.activation(out=bst[bs_][:N1, j, :, :], in_=ps[:N1, br:br + 2, :], func=AF.Copy),
                             reads=["ps%d" % br, "ps%d" % (br + 1)], writes=["bst%d_%d" % (bs_, j)])
                dma("sp", bs[:, a:a + nb, :, :], bst[bs_][:N1, 0:nb, :, :], reads=["bst%d_%d" % (bs_, j) for j in range(nb)],
                    sem="bst%d" % bs_)
            S.fence()
            bin_ = zin
            grp2 = [(a, min(NB, N1 - a)) for a in range(0, N1, NB)]

            def load_b(gi):
                a, nb = grp2[gi]
                dma("sp", bin_[gi % NZ][:N2, 0:nb, :], bs[a:a + nb].rearrange("k n r c -> n k (r c)"),
                    writes=["bin%d" % (gi % NZ)], sem="zin%d" % (gi % NZ))

            for gi in range(min(NZ - 1, len(grp2))):
                load_b(gi)
            ctr = 0
            for gi, (a, nb) in enumerate(grp2):
                if gi + NZ - 1 < len(grp2):
                    load_b(gi + NZ - 1)
                zs = gi % NZ
                for j in range(nb):
                    k1 = a + j
                    bank = ctr % 2
                    ctr += 1
                    for g in range(4):
                        for r in range(2):
                            S.op("pe", lambda e, g=g, r=r, j=j, zs=zs, bank=bank: e.matmul(
                                ps[:, bank, g * N2:(g + 1) * N2], lhsT=bin_[zs][:N2, j, r * 512 + g * 128:r * 512 + (g + 1) * 128],
                                rhs=T2[:N2, r, :], start=(r == 0), stop=(r == 1)),
                                reads=["T2", "bin%d" % zs], writes=["ps%d" % bank])
                    if ctr % 2 == 0:
                        S.op("dve", lambda e: e.tensor_copy(
                            out=YT[:, :, k1:k1 + N1 * (N2 - 1) + 1:N1],
                            in_=ps[:, bank, 0:4 * N2].rearrange("p (g k) -> p g k", g=4)),
                            reads=["ps%d" % bank], writes=["YT%d" % (ctr % 2)])
                    else:
                        S.op("act", lambda e: e.activation(
                            out=YT[:, :, k1:k1 + N1 * (N2 - 1) + 1:N1],
                            in_=ps[:, bank, 0:4 * N2].rearrange("p (g k) -> p g k", g=4), func=AF.Copy),
                            reads=["ps%d" % bank], writes=["YT%d" % (ctr % 2)])
            S.fence()

            AR.top = C_MARK
            oin = [AR.alloc([4, 512], BF16) for _ in range(2)]
            gins = [AR.alloc([16, 512], BF16) for _ in range(2)]
            ta = [AR.alloc([512], F32) for _ in range(2)]
            tb = [AR.alloc([512], F32) for _ in range(2)]
            mg = [AR.alloc([8, 512], BF16) for _ in range(2)]
            xin = [AR.alloc([D], F32) for _ in range(3)]
            x1o = [AR.alloc([D], F32) for _ in range(2)]
            def load_c1(m):
                dma("sp", oin[m % 2], ot_s[4 * m:4 * m + 4].rearrange("q p f -> p q f"), writes=["oin%d" % (m % 2)], sem="oin%d" % (m % 2))
                dma("sp", gins[m % 2], g_s[:, :, m * 512:(m + 1) * 512], writes=["gin%d" % (m % 2)], sem="gin%d" % (m % 2))

            load_c1(0)
            xctr = 0

            def c1_merge(m):
                ms = m % 2
                t0 = m * 512
                gin = gins[ms]
                gk = "gin%d" % ms
                if m + 1 < nmt:
                    load_c1(m + 1)
                for oc in range(8):
                    ba = (oc % 2)
                    bf = 2 + (oc % 2)
                    for g in range(4):
                        S.op("pe", lambda e: e.matmul(
                            ps[:, ba, :], lhsT=Wao[:, g, oc * 128:(oc + 1) * 128],
                            rhs=oin[ms].rearrange("p q (g t) -> p q g t", g=4)[:, :, g, :],
                            start=(g == 0), stop=(g == 3)), reads=["Wao", "oin%d" % ms], writes=["ps%d" % ba])
                    for g in range(4):
                        S.op("pe", lambda e: e.matmul(
                            ps[:, bf, :], lhsT=Wfo[:, g, oc * 128:(oc + 1) * 128], rhs=YT[:, g, 16 + t0:16 + t0 + 512],
                            start=(g == 0), stop=(g == 3)), reads=["Wfo", "YT"], writes=["ps%d" % bf])
                    o2 = oc % 2
                    S.op("dve", lambda e: e.tensor_tensor(out=ta[o2], in0=ps[:, ba, :], in1=gin[:, oc, :], op=ALU.mult),
                         reads=["ps%d" % ba, gk], writes=["ta%d" % o2])
                    S.op("dve", lambda e: e.tensor_tensor(out=tb[o2], in0=ps[:, bf, :], in1=gin[:, 8 + oc, :], op=ALU.mult),
                         reads=["ps%d" % bf, gk], writes=["tb%d" % o2])
                    S.op("pool", lambda e: e.tensor_tensor(out=mg[ms][:, oc, :], in0=ta[o2], in1=tb[o2], op=ALU.add),
                         reads=["ta%d" % o2, "tb%d" % o2], writes=["mg%d_%d" % (ms, oc)])

            c1_merge(0)
            for m in range(nmt):
                ms = m % 2
                t0 = m * 512
                if m + 1 < nmt:
                    c1_merge(m + 1)
                for s in range(4):
                    xsl = xctr % 3
                    osl = xctr % 2
                    xctr += 1
                    tt = t0 + s * 128
                    dma("sp", xin[xsl], x_d[tt:tt + 128, :], writes=["xin%d" % xsl], sem="xin%d" % xsl)
                    pb = 4 + 2 * (s % 2)
                    for hh in range(2):
                        for oc in range(8):
                            S.op("pe", lambda e: e.matmul(
                                ps[:, pb + hh, :], lhsT=mg[ms][:, oc, s * 128:(s + 1) * 128], rhs=Wo[:, oc, hh * 512:(hh + 1) * 512],
                                start=(oc == 0), stop=(oc == 7)), reads=["mg%d_%d" % (ms, oc), "Wo"], writes=["ps%d" % (pb + hh)])
                    S.op("dve", lambda e: e.tensor_tensor(
                        out=x1o[osl], in0=xin[xsl], in1=psflat[:, pb * 512:(pb + 2) * 512], op=ALU.add),
                        reads=["xin%d" % xsl, "ps%d" % pb, "ps%d" % (pb + 1)], writes=["x1o%d" % osl])
                    dma("sp", x1_s[si][tt:tt + 128, :], x1o[osl], reads=["x1o%d" % osl], sem="x1o%d" % osl)
            S.fence()

        AR.top = P_MARK
        W1 = AR.alloc([8, 2 * DFF], BF16)
        W2 = AR.alloc([NJ, D], BF16)
        gfin = AR.alloc([D], F32)
        gffn = AR.alloc([D], F32)
        xn = [AR.alloc([D], F32) for _ in range(4)]
        xr = [AR.alloc([D], F32) for _ in range(2)]
        ssq = AR.alloc([8], F32)
        hb2 = [AR.alloc([D], BF16) for _ in range(2)]
        hT2 = AR.alloc([8, 512], BF16)
        sg = [AR.alloc([512], BF16) for _ in range(2)]
        actT = AR.alloc([NJ, 512], BF16)
        yo = AR.alloc([D], F32)
        dma("sp", gffn, bcast_row(g_ffn_t, D), writes=["gffn"], sem="c6")
        dma("sp", gfin, bcast_row(g_fin_t, D), writes=["gfin"], sem="c7")

        tiles = [(si, m) for si, n in enumerate(seq_ns) for m in range(n // 512)]

        def load_xn(ti):
            si, m = tiles[ti]
            for s in range(4):
                tt = m * 512 + s * 128
                dma("sp", xn[s], x1_s[si][tt:tt + 128, :], writes=["xn%d" % s], sem="xn%d" % s)

        def front2_dve(s):
            hs = s % 2
            rms_rstd(xn[s][:, :], 128, hb2[hs], ssq, s, "xn%d" % s, "hb2_%d" % hs, "ssqn%d" % s)
            sqrt_recip(ssq[:, s:s + 1], 1.0 / D, ["ssqn%d" % s])
            S.op("dve", lambda e: e.scalar_tensor_tensor(out=hb2[hs], in0=xn[s], scalar=ssq[:, s:s + 1], in1=gffn, op0=ALU.mult, op1=ALU.mult),
                 reads=["xn%d" % s, "ssqn%d" % s, "gffn"], writes=["hb2_%d" % hs])

        def front2_pe(s):
            hs = s % 2
            for c in range(8):
                S.op("pe", lambda e: e.transpose(out=psb(7)[:, c * 128:(c + 1) * 128], in_=hb2[hs][:, c * 128:(c + 1) * 128],
                                                 identity=ident), reads=["hb2_%d" % hs, "ident"], writes=["ps7"])
            S.op("dve", lambda e: e.tensor_copy(out=hT2[:, :, s * 128:(s + 1) * 128], in_=psb(7).rearrange("p (c t) -> p c t", c=8)),
                 reads=["ps7"], writes=["hT2_%d" % s])

        load_xn(0)
        NBLK = (NJ + 3) // 4
        w1v = w1_s.rearrange("(c p) f -> p c f", p=128)
        for bk in range(NBLK):
            j0, j1 = bk * 4, min(NJ, bk * 4 + 4)
            for half in range(2):
                c0 = half * DFF + j0 * 128
                c1 = half * DFF + j1 * 128
                dma("sp", W1[:, :, c0:c1], w1v[:, :, c0:c1], writes=["W1b_%d" % bk], sem="w6_%d" % bk)
        for jq in range(0, NJ, 2):
            dma("sp", W2[:, jq:jq + 2, :], w2_s[jq * 128:(jq + 2) * 128, :].rearrange("(c p) o -> p c o", p=128),
                writes=["W2"], sem="w7")
        for s in range(4):
            front2_dve(s)
            front2_pe(s)
        fctr = 0
        octr = 0
        rctr = 0
        hkeys = ["hT2_%d" % s for s in range(4)]
        for ti, (si, m) in enumerate(tiles):
            t0 = m * 512
            x1d = x1_s[si]
            if ti + 1 < len(tiles):
                load_xn(ti + 1)
            for j in range(NJ):
                bg = 2 * (fctr % 2)
                ss_ = fctr % 2
                fctr += 1
                for (bank, col0) in [(bg, j * 128), (bg + 1, DFF + j * 128)]:
                    for c in range(8):
                        S.op("pe", lambda e: e.matmul(ps[:, bank, :], lhsT=W1[:, c, col0:col0 + 128], rhs=hT2[:, c, :],
                                                      start=(c == 0), stop=(c == 7)),
                             reads=hkeys + ["W1b_%d" % (j // 4)], writes=["ps%d" % bank])
                S.op("act", lambda e: e.activation(out=sg[ss_], in_=ps[:, bg, :], func=AF.Silu), reads=["ps%d" % bg], writes=["sg%d" % ss_])
                S.op("dve", lambda e: e.tensor_tensor(out=actT[:, j, :], in0=ps[:, bg + 1, :], in1=sg[ss_], op=ALU.mult),
                     reads=["ps%d" % (bg + 1), "sg%d" % ss_], writes=["actT%d" % j])
            for s in range(4):
                tt = t0 + s * 128
                rs = rctr % 2
                rctr += 1
                dma("sp", xr[rs], x1d[tt:tt + 128, :], writes=["xr%d" % rs], sem="xr%d" % rs)
                if ti + 1 < len(tiles):
                    front2_dve(s)
                banks = []
                for hh in range(2):
                    bank = 4 + (octr % 3)
                    octr += 1
                    banks.append(bank)
                    for j in range(NJ):
                        S.op("pe", lambda e: e.matmul(ps[:, bank, :], lhsT=actT[:, j, s * 128:(s + 1) * 128],
                                                      rhs=W2[:, j, hh * 512:(hh + 1) * 512], start=(j == 0), stop=(j == NJ - 1)),
                             reads=["actT%d" % j, "W2"], writes=["ps%d" % bank])
                if ti + 1 < len(tiles):
                    front2_pe(s)
                for hh in range(2):
                    S.op("dve", lambda e: e.tensor_tensor(out=xr[rs][:, hh * 512:(hh + 1) * 512], in0=xr[rs][:, hh * 512:(hh + 1) * 512],
                                                          in1=ps[:, banks[hh], :], op=ALU.add),
                         reads=["xr%d" % rs, "ps%d" % banks[hh]], writes=["xr%d" % rs])
                rms_rstd(xr[rs][:, :], 128, yo, ssq, 4, "xr%d" % rs, "yo", "ssq4")
                sqrt_recip(ssq[:, 4:5], 1.0 / D, ["ssq4"])
                S.op("dve", lambda e: e.scalar_tensor_tensor(out=yo, in0=xr[rs], scalar=ssq[:, 4:5], in1=gfin, op0=ALU.mult, op1=ALU.mult),
                     reads=["xr%d" % rs, "ssq4", "gfin"], writes=["yo"])
                dma("sp", ys[si][tt:tt + 128, :], yo, reads=["yo"], sem="yo")
        S.fence()
        S.emit(block)
    return nc


SEQ_NS = [2048, 2048, 8192]
_CACHE = {}


def _get_program(seq_ns):
    key = tuple(seq_ns)
    if key not in _CACHE:
        _CACHE[key] = (build(list(seq_ns)), host_consts(list(seq_ns)))
    return _CACHE[key]


def run_cores(seq_ns, per_core_x, weights):
    nc, consts = _get_program(seq_ns)
    in_maps = []
    for xl in per_core_x:
        m = dict(consts)
        m.update(weights)
        for i, x in enumerate(xl):
            m["x%d" % i] = np.ascontiguousarray(x, dtype=np.float32)
        in_maps.append(m)
    res = run_bass_kernel_spmd(nc, in_maps, core_ids=list(range(len(in_maps))))
    return [[r["y%d" % i] for i in range(len(seq_ns))] for r in res.results]


def kernel(x_prompt, x_sample, meta_tokens, norm_mix_g, w_in, q_norm_g, k_norm_g, w_attn_o, w_four_o,
           w_out, norm_ffn_g, w_ffn_in, w_ffn_out, final_norm_g):
    f = lambda a: np.ascontiguousarray(np.asarray(a), dtype=np.float32)
    weights = {
        "meta_tokens": f(meta_tokens), "norm_mix_g": f(norm_mix_g).reshape(-1), "q_norm_g": f(q_norm_g).reshape(-1),
        "k_norm_g": f(k_norm_g).reshape(-1), "norm_ffn_g": f(norm_ffn_g).reshape(-1), "final_norm_g": f(final_norm_g).reshape(-1),
        "w_in": f(w_in)[0], "w_attn_o": f(w_attn_o)[0], "w_four_o": f(w_four_o)[0], "w_out": f(w_out)[0],
        "w_ffn_in": f(w_ffn_in)[0], "w_ffn_out": f(w_ffn_out)[0],
    }
    xp = np.asarray(x_prompt)
    xsm = np.asarray(x_sample)
    per_core = [[xp[2 * c], xp[2 * c + 1], xsm[c]] for c in range(8)]
    outs = run_cores(SEQ_NS, per_core, weights)
    y_prompt = np.stack([outs[c][k] for c in range(8) for k in range(2)], 0).astype(np.float32)
    y_sample = np.stack([outs[c][2] for c in range(8)], 0).astype(np.float32)
    return (y_prompt, y_sample)
```

```python
import bisect
from contextlib import ExitStack
import numpy as np
import ml_dtypes
import concourse.bass as bass
import concourse.mybir as mybir
from concourse.bass_utils import run_bass_kernel_spmd

F32 = mybir.dt.float32
BF16 = mybir.dt.bfloat16
AF = mybir.ActivationFunctionType
ALU = mybir.AluOpType
AX = mybir.AxisListType

D = 1024
DFF = 2816
NJ = DFF // 128
WIN_COLS = 3840
C_K, C_V, C_F, C_G = 512, 640, 768, 1792
FACT = {528: (24, 22), 1040: (40, 26), 2064: (86, 24), 8208: (114, 72)}
EPS = 1e-6
EXP_SHIFT = -10.0

ENGS = ["pe", "act", "dve", "pool"]
EIDX = {e: i for i, e in enumerate(ENGS)}
EPOCH = 30000


class DSem:
    def __init__(self, h, name):
        self.h = h
        self.count = 0
        self.name = name


class _Rec:
    def __init__(self):
        self.call = None

    def __getattr__(self, name):
        def f(*a, **k):
            self.call = (name, a, k)
            return self
        return f


class Sched:
    def __init__(self, nc, stack):
        self.nc = nc
        self.stack = stack
        self.streams = {e: [] for e in ENGS + ["sp"]}
        self.nops = {e: 0 for e in ENGS}
        self.snap = {e: [] for e in ENGS}
        self.signaled = {e: [] for e in ENGS}
        self.clock = {e: [-1, -1, -1, -1] for e in ENGS + ["sp"]}
        self.known_sem = {e: {} for e in ENGS + ["sp"]}
        self.res = {}
        self.dsems = []

    def dsem(self, name):
        h = self.stack.enter_context(self.nc.semaphore(name))
        s = DSem(h, name)
        self.dsems.append(s)
        return s

    def _need(self, eng, tok):
        if tok is None:
            return
        if tok[0] == "eng":
            _, E, idx = tok
            if E == "pe" and eng == "pe":
                return
            if self.clock[eng][EIDX[E]] >= idx:
                return
            sl = self.signaled[E]
            j = bisect.bisect_left(sl, idx)
            if j < len(sl) and sl[j] - idx <= (3 if E == "pe" else 0):
                tgt = sl[j]
            else:
                bisect.insort(sl, idx)
                tgt = idx
            self.streams[eng].append(("we", E, tgt))
            sn = self.snap[E][tgt]
            c = self.clock[eng]
            for i in range(4):
                if sn[i] > c[i]:
                    c[i] = sn[i]
            if c[EIDX[E]] < tgt:
                c[EIDX[E]] = tgt
        else:
            _, s, cnt = tok
            if self.known_sem[eng].get(s, 0) >= cnt:
                return
            self.streams[eng].append(("ws", s, cnt))
            self.known_sem[eng][s] = cnt

    def op(self, eng, fn, reads=(), writes=(), dsem=None):
        rec = _Rec()
        fn(rec)
        fn = rec.call
        deps = []
        for r in reads:
            st = self.res.get(r)
            if st is not None:
                deps.append(st[0])
        for w in writes:
            st = self.res.get(w)
            if st is not None:
                deps.append(st[0])
                deps.extend(st[1])
        for d in deps:
            self._need(eng, d)
        if dsem is not None:
            dsem.count += 16
            tok = ("dma", dsem, dsem.count)
            self.streams[eng].append(("dma", fn, dsem))
        else:
            idx = self.nops[eng]
            self.nops[eng] = idx + 1
            tok = ("eng", eng, idx)
            self.snap[eng].append(tuple(self.clock[eng]))
            if eng == "pe":
                self.clock[eng][0] = idx
            self.streams[eng].append(("op", fn, idx))
        for w in writes:
            self.res[w] = [tok, []]
        for r in reads:
            st = self.res.get(r)
            if st is None:
                st = self.res[r] = [None, []]
            rl = st[1]
            for i, t in enumerate(rl):
                if t[0] == tok[0] and t[1] == tok[1]:
                    rl[i] = tok
                    break
            else:
                rl.append(tok)
        return tok

    def fence(self):
        toks = [("eng", E, self.nops[E] - 1) for E in ENGS if self.nops[E] > 0]
        toks += [("dma", s, s.count) for s in self.dsems if s.count]
        for eng in ENGS + ["sp"]:
            for t in toks:
                self._need(eng, t)
        self.res = {}

    def emit(self, block):
        nc = self.nc
        esem = {}
        for e in ENGS:
            self.signaled[e].sort()
            n = len(self.signaled[e])
            ne = max(1, (n + EPOCH - 1) // EPOCH)
            esem[e] = [self.stack.enter_context(nc.semaphore("es_%s_%d" % (e, k))) for k in range(ne)]
        rank = {e: {idx: i for i, idx in enumerate(self.signaled[e])} for e in ENGS}

        def run(eng, eo):
            for ent in self.streams[eng]:
                k = ent[0]
                if k == "we":
                    r = rank[ent[1]][ent[2]]
                    eo.wait_ge(esem[ent[1]][r // EPOCH], r % EPOCH + 1)
                elif k == "ws":
                    eo.wait_ge(ent[1].h, ent[2])
                elif k == "dma":
                    nm, a, kw = ent[1]
                    getattr(eo, nm)(*a, **kw).then_inc(ent[2].h, 16)
                else:
                    nm, a, kw = ent[1]
                    ins = getattr(eo, nm)(*a, **kw)
                    r = rank[eng].get(ent[2])
                    if r is not None:
                        ins.then_inc(esem[eng][r // EPOCH], 1)

        @block.tensor
        def _(t):
            run("pe", t)

        @block.scalar
        def _(t):
            run("act", t)

        @block.vector
        def _(t):
            run("dve", t)

        @block.gpsimd
        def _(t):
            run("pool", t)

        @block.sync
        def _(t):
            run("sp", t)


class Arena:
    def __init__(self, base, nwords):
        self.base = base
        self.n = nwords
        self.top = 0

    def alloc(self, shape, dtype):
        nel = int(np.prod(shape))
        words = nel if dtype == F32 else (nel + 1) // 2
        a = self.base[:, self.top:self.top + words]
        self.top += words
        assert self.top <= self.n, ("SBUF arena overflow", self.top, self.n)
        if dtype != F32:
            a = a.bitcast(dtype)[:, 0:nel]
        if len(shape) == 2:
            a = a.rearrange("p (a b) -> p a b", a=shape[0], b=shape[1])
        elif len(shape) == 3:
            a = a.rearrange("p (a b c) -> p a b c", a=shape[0], b=shape[1], c=shape[2])
        return a


def host_consts(seq_ns):
    c = {}
    c["ident"] = np.eye(128, dtype=np.float32).astype(ml_dtypes.bfloat16)
    c["identf"] = np.eye(128, dtype=np.float32)
    k = np.arange(128, dtype=np.float64)
    ang = 2 * np.pi * np.outer(k, k) / 128.0
    c["cs128"] = (np.concatenate([np.cos(ang), -np.sin(ang)], 1) / np.sqrt(128.0)).astype(np.float32)
    inv_freq = 1.0 / (10000.0 ** (np.arange(0, 32, 2, dtype=np.float64) / 32.0))
    for n in sorted(set(seq_ns)):
        L = n + 16
        N1, N2 = FACT[L]
        nkb = n // 128 + 1
        t = np.arange(n)
        row = np.concatenate([t // 64, np.full(16, -1)]).astype(np.float64)
        col = np.concatenate([t % 64, np.arange(16)]).astype(np.float64)
        ar = row[:, None] * inv_freq[None]
        ac = col[:, None] * inv_freq[None]
        tab = np.zeros((nkb * 128, 128), np.float32)
        tab[:L, 0:64] = np.concatenate([np.cos(ar), np.cos(ar), np.cos(ac), np.cos(ac)], 1)
        tab[:L, 64:128] = np.concatenate([-np.sin(ar), np.sin(ar), -np.sin(ac), np.sin(ac)], 1)
        c["rope%d" % n] = tab
        n1 = np.arange(N1, dtype=np.float64)
        n2 = np.arange(N2, dtype=np.float64)
        a1 = 2 * np.pi * (n1[:, None, None] * n1[None, None, :] / N1 + n2[None, :, None] * n1[None, None, :] / L)
        t1 = np.stack([np.cos(a1), np.sin(a1), -np.sin(a1)], 2)
        c["t1_%d" % n] = t1.astype(np.float32).astype(ml_dtypes.bfloat16)
        a2 = 2 * np.pi * np.outer(n2, n2) / N2
        t2 = np.stack([np.cos(a2), np.sin(a2)], 1) / np.sqrt(float(L))
        c["t2_%d" % n] = t2.astype(np.float32).astype(ml_dtypes.bfloat16)
    return c


def build(seq_ns):
    nc = bass.Bass("TRN2", target_bir_lowering=False)
    nseq = len(seq_ns)

    def din(name, shape, dt=F32):
        return nc.dram_tensor(name, list(shape), dt, kind="ExternalInput")

    xs = [din("x%d" % i, [n, D]).ap() for i, n in enumerate(seq_ns)]
    ys = [nc.dram_tensor("y%d" % i, [n, D], F32, kind="ExternalOutput").ap() for i, n in enumerate(seq_ns)]
    meta = din("meta_tokens", [16, D]).ap()
    g_mix_t = din("norm_mix_g", [D])
    g_q_t = din("q_norm_g", [64])
    g_k_t = din("k_norm_g", [64])
    g_ffn_t = din("norm_ffn_g", [D])
    g_fin_t = din("final_norm_g", [D])
    w_in = din("w_in", [D, 3328]).ap()
    w_ao = din("w_attn_o", [512, D]).ap()
    w_fo = din("w_four_o", [512, D]).ap()
    w_out = din("w_out", [D, D]).ap()
    w_fi = din("w_ffn_in", [D, 2 * DFF]).ap()
    w_f2 = din("w_ffn_out", [DFF, D]).ap()
    ident_d = din("ident", [128, 128], BF16).ap()
    identf_d = din("identf", [128, 128]).ap()
    cs128_d = din("cs128", [128, 256]).ap()
    rope_d, t1_d, t2_d = {}, {}, {}
    for n in sorted(set(seq_ns)):
        N1, N2 = FACT[n + 16]
        rope_d[n] = din("rope%d" % n, [(n // 128 + 1) * 128, 128]).ap()
        t1_d[n] = din("t1_%d" % n, [N1, N2, 3, N1], BF16).ap()
        t2_d[n] = din("t2_%d" % n, [N2, 2, N2], BF16).ap()

    nmax = max(seq_ns)
    win_s = nc.dram_tensor("win_s", [128, 8 * WIN_COLS], BF16).ap()
    zf_s = nc.dram_tensor("zf_s", [nmax + 16, 1024], BF16).ap()
    N1m, N2m = FACT[nmax + 16]
    bs_s = nc.dram_tensor("bs_s", [N1m * N2m * 1024], BF16).ap()
    qt_s = nc.dram_tensor("qt_s", [nmax // 128, 128, 512], BF16).ap()
    ot_s = nc.dram_tensor("ot_s", [nmax // 128, 128, 512], BF16).ap()
    g_s = nc.dram_tensor("g_s", [128, 16, nmax], BF16).ap()
    x1_s = [nc.dram_tensor("x1_s%d" % i, [n, D], F32).ap() for i, n in enumerate(seq_ns)]
    w1_s = nc.dram_tensor("w1_s", [D, 2 * DFF], BF16).ap()
    w2_s = nc.dram_tensor("w2_s", [DFF, D], BF16).ap()

    ARW = 53100
    with ExitStack() as st:
        arena_t = st.enter_context(nc.sbuf_tensor("arena", [128, ARW], F32))
        ps = st.enter_context(nc.psum_tensor("ps", [128, 8, 512], F32))
        block = st.enter_context(nc.Block())
        S = Sched(nc, st)
        AR = Arena(arena_t, ARW)
        psflat = ps[:, :, :].rearrange("p b f -> p (b f)")

        def psb(b):
            return ps[:, b, :].bitcast(BF16)

        sem_cache = {}
        WINK = ["win_q", "win_kv", "win_g", "win_f"]

        def dsem(name):
            if name not in sem_cache:
                sem_cache[name] = S.dsem(name)
            return sem_cache[name]

        def dma(eng, out, in_, reads=(), writes=(), sem="misc", **kw):
            S.op(eng, lambda e: e.dma_start(out=out, in_=in_, **kw), reads=reads, writes=writes, dsem=dsem(sem))

        ident = AR.alloc([128], BF16)
        ones64 = AR.alloc([64], BF16)
        eps_t = AR.alloc([1], F32)
        ebias = AR.alloc([1], F32)
        gqk = AR.alloc([640], F32)
        tmp2 = AR.alloc([2], F32)
        P_MARK = AR.top

        dma("sp", ident, ident_d[:, :], writes=["ident"], sem="c0")
        S.op("pool", lambda e: e.memset(ones64, 1.0), writes=["ones64"])
        S.op("pool", lambda e: e.memset(eps_t, EPS), writes=["eps"])
        dma("sp", gqk[:, 0:512].rearrange("p (h d) -> p h d", d=64), bass.AP(g_q_t, 0, [[0, 128], [0, 8], [1, 64]]),
            writes=["gqk"], sem="c1")
        dma("sp", gqk[:, 512:640].rearrange("p (h d) -> p h d", d=64), bass.AP(g_k_t, 0, [[0, 128], [0, 2], [1, 64]]),
            writes=["gqk2"], sem="c2")
        S.op("dve", lambda e: e.tensor_reduce(out=tmp2[:, 0:1], in_=gqk[:, 0:64], axis=AX.X, op=ALU.max,
                                              apply_absolute_value=True), reads=["gqk"], writes=["tmp2a"])
        S.op("dve", lambda e: e.tensor_reduce(out=tmp2[:, 1:2], in_=gqk[:, 512:576], axis=AX.X, op=ALU.max,
                                              apply_absolute_value=True), reads=["gqk2"], writes=["tmp2b"])
        S.op("dve", lambda e: e.tensor_tensor(out=ebias, in0=tmp2[:, 0:1], in1=tmp2[:, 1:2], op=ALU.mult),
             reads=["tmp2a", "tmp2b"], writes=["ebias"])
        S.op("dve", lambda e: e.tensor_scalar(out=ebias, in0=ebias, scalar1=-8.0, scalar2=None, op0=ALU.mult),
             reads=["ebias"], writes=["ebias"])

        def bcast_row(t, nel):
            return bass.AP(t, 0, [[0, 128], [1, nel]])

        def rms_rstd(x_ap, ntok, junk, ssq, col, key_x, key_junk, key_ssq):
            S.op("dve", lambda e: e.scalar_tensor_tensor(out=junk[:ntok], in0=x_ap, scalar=1.0, in1=x_ap,
                                                         op0=ALU.mult, op1=ALU.mult, accum_out=ssq[:ntok, col:col + 1]),
                 reads=[key_x], writes=[key_junk, key_ssq])

        def sqrt_recip(ssq_ap, scale, keys):
            S.op("act", lambda e: e.activation(out=ssq_ap, in_=ssq_ap, func=AF.Sqrt, bias=eps_t[:ssq_ap.shape[0], 0:1],
                                               scale=scale), reads=keys + ["eps"], writes=keys)
            S.op("dve", lambda e: e.reciprocal(out=ssq_ap, in_=ssq_ap), reads=keys, writes=keys)

        AR.top = P_MARK
        KT_W = (nmax // 128 + 1) * 128
        T1_WORDS = max(max(FACT[n_ + 16][1] * 3 * FACT[n_ + 16][0] for n_ in seq_ns) // 2 + 2, 10240)
        Win = AR.alloc([8, WIN_COLS], BF16)
        A_MARK = AR.top
        cs128 = AR.alloc([256], F32)
        identf = AR.alloc([128], F32)
        wfT = [AR.alloc([128], F32) for _ in range(2)]
        stg = [AR.alloc([3328], F32) for _ in range(3)]
        dma("sp", identf, identf_d[:, :], writes=["identf"], sem="c3")
        dma("sp", cs128, cs128_d[:, :], writes=["cs128"], sem="c4")

        def load_stg(c):
            dma("sp", stg[c % 3], w_in[c * 128:(c + 1) * 128, :], writes=["stg%d" % (c % 3)], sem="stg%d" % (c % 3))

        load_stg(0)
        load_stg(1)
        it = 0
        for c in range(8):
            if c + 2 < 8:
                load_stg(c + 2)
            st_ = stg[c % 3]
            sk = "stg%d" % (c % 3)
            S.op("dve", lambda e: e.tensor_copy(out=Win[:, c, 0:512].rearrange("p (g k d) -> p g k d", g=4, k=2),
                                                in_=st_[:, 0:512].rearrange("p (k g d) -> p g k d", k=2, g=4)),
                 reads=[sk], writes=["win_q"])
            S.op("pool", lambda e: e.tensor_copy(out=Win[:, c, 512:768], in_=st_[:, 512:768]), reads=[sk], writes=["win_kv"])
            S.op("act", lambda e: e.activation(out=Win[:, c, C_G:WIN_COLS], in_=st_[:, 1280:3328], func=AF.Copy),
                 reads=[sk], writes=["win_g"])
            for g in range(4):
                b = it % 2
                it += 1
                S.op("pe", lambda e: e.transpose(out=ps[:, b, 0:128], in_=st_[:, 768 + g * 128:768 + (g + 1) * 128], identity=identf),
                     reads=[sk, "identf"], writes=["ps%d" % b])
                S.op("dve", lambda e: e.tensor_copy(out=wfT[b], in_=ps[:, b, 0:128]), reads=["ps%d" % b], writes=["wfT%d" % b])
                S.op("pe", lambda e: e.matmul(ps[:, 2 + b, 0:256], lhsT=wfT[b], rhs=cs128, start=True, stop=True),
                     reads=["wfT%d" % b, "cs128"], writes=["ps%d" % (2 + b)])
                S.op("dve", lambda e: e.tensor_copy(
                    out=Win[:, c, C_F:C_G].rearrange("p (r g k) -> p r g k", r=2, g=4)[:, :, g, :],
                    in_=ps[:, 2 + b, 0:256].rearrange("p (r k) -> p r k", r=2)),
                    reads=["ps%d" % (2 + b)], writes=["win_f"])
        dma("sp", win_s[:, :], Win[:, :, :].rearrange("p c f -> p (c f)"), reads=WINK, sem="ws")
        S.fence()

        for si, n in enumerate(seq_ns):
            L = n + 16
            N1, N2 = FACT[L]
            nqb = n // 128
            nkb = nqb + 1
            nmt = n // 512
            x_d = xs[si]
            zf = zf_s[0:L, :]
            bs = bs_s[0:N1 * N2 * 1024].rearrange("(k n r c) -> k n r c", k=N1, n=N2, r=2)

            AR.top = A_MARK
            KT = AR.alloc([KT_W], BF16)
            Vt = AR.alloc([KT_W // 128, 128], BF16)
            B_MARK = AR.top
            gmix = AR.alloc([D], F32)
            xbuf = [AR.alloc([4, D], F32) for _ in range(2)]
            junk = AR.alloc([D], BF16)
            ssq = AR.alloc([8], F32)
            hb = AR.alloc([4, D], BF16)
            hT = [AR.alloc([8, 512], BF16) for _ in range(2)]
            zc = [AR.alloc([640], F32) for _ in range(2)]
            sq = AR.alloc([640], F32)
            ssh = AR.alloc([16], F32)
            qn = AR.alloc([640], F32)
            t1b = AR.alloc([640], F32)
            t2b = AR.alloc([640], F32)
            qko = [AR.alloc([640], BF16) for _ in range(2)]
            zfs = [AR.alloc([1024], BF16) for _ in range(2)]
            qTs = [AR.alloc([512], BF16) for _ in range(2)]
            gst = [AR.alloc([4, 512], BF16) for _ in range(2)]
            rt = [AR.alloc([128], F32) for _ in range(2)]

            if si > 0:
                dma("sp", Win[:, :, :].rearrange("p c f -> p (c f)"), win_s[:, :], writes=WINK, sem="wl")
            dma("sp", gmix, bcast_row(g_mix_t, D), writes=["gmix"], sem="c6")

            mts = [(m * 512, [128] * 4, False) for m in range(nmt)] + [(n, [16], True)]

            def load_x(mi):
                t0, subs, is_meta = mts[mi]
                sl = mi % 2
                if is_meta:
                    dma("sp", xbuf[sl][0:16, 0, :], meta[:, :], writes=["x%d_0" % sl], sem="x%d" % sl)
                else:
                    dma("sp", xbuf[sl][:, :, :], x_d[t0:t0 + 512, :].rearrange("(s p) d -> p s d", p=128),
                        writes=["x%d_%d" % (sl, s) for s in range(4)], sem="x%d" % sl)

            def front_a(mi, s):
                t0, subs, is_meta = mts[mi]
                sl = mi % 2
                ntok = subs[s]
                rms_rstd(xbuf[sl][:ntok, s, :], ntok, junk, ssq, s, "x%d_%d" % (sl, s), "junk", "ssq%d" % s)
                S.op("act", lambda e: e.activation(out=ssq[:ntok, s:s + 1], in_=ssq[:ntok, s:s + 1], func=AF.Sqrt,
                                                   bias=eps_t[:ntok, 0:1], scale=1.0 / D), reads=["ssq%d" % s, "eps"], writes=["ssq%d" % s])

            def front_b(mi, s):
                t0, subs, is_meta = mts[mi]
                sl = mi % 2
                ntok = subs[s]
                S.op("dve", lambda e: e.reciprocal(out=ssq[:ntok, s:s + 1], in_=ssq[:ntok, s:s + 1]), reads=["ssq%d" % s], writes=["ssq%d" % s])
                S.op("dve", lambda e: e.scalar_tensor_tensor(
                    out=hb[:ntok, s, :], in0=xbuf[sl][:ntok, s, :], scalar=ssq[:ntok, s:s + 1], in1=gmix[:ntok],
                    op0=ALU.mult, op1=ALU.mult), reads=["x%d_%d" % (sl, s), "ssq%d" % s, "gmix"], writes=["hb%d" % s])

            def front_dve(mi, s):
                front_a(mi, s)
                front_b(mi, s)

            def front_pe(mi, s):
                t0, subs, is_meta = mts[mi]
                sl = mi % 2
                ntok = subs[s]
                for c in range(8):
                    S.op("pe", lambda e: e.transpose(
                        out=psb(4)[:, c * 128:c * 128 + ntok], in_=hb[:ntok, s, c * 128:(c + 1) * 128],
                        identity=ident[:ntok, :ntok]), reads=["hb%d" % s, "ident"], writes=["ps4"])
                S.op("dve", lambda e: e.tensor_copy(
                    out=hT[sl][:, :, s * 128:s * 128 + ntok],
                    in_=psb(4).rearrange("p (c t) -> p c t", c=8)[:, :, 0:ntok]),
                    reads=["ps4"], writes=["hT%d_%d" % (sl, s)])

            def front(mi):
                for s in range(len(mts[mi][1])):
                    front_dve(mi, s)
                    front_pe(mi, s)

            def chain_a(ntok, so, kb, is_meta):
                z = zc[so]
                S.op("dve", lambda e: e.tensor_tensor(out=sq[:ntok], in0=z[:ntok], in1=z[:ntok], op=ALU.mult),
                     reads=["zc%d" % so], writes=["sq"])
                S.op("dve", lambda e: e.tensor_reduce(out=ssh[:ntok, 0:10], in_=sq[:ntok].rearrange("p (h d) -> p h d", d=64),
                                                      axis=AX.X, op=ALU.add), reads=["sq"], writes=["ssh"])
                S.op("act", lambda e: e.activation(out=ssh[:ntok, 0:10], in_=ssh[:ntok, 0:10], func=AF.Sqrt,
                                                   bias=eps_t[:ntok, 0:1], scale=1.0 / 64), reads=["ssh", "eps"], writes=["ssh"])

            def chain_b(ntok, so, kb, is_meta):
                z = zc[so]
                S.op("dve", lambda e: e.reciprocal(out=ssh[:ntok, 0:10], in_=ssh[:ntok, 0:10]), reads=["ssh"], writes=["ssh"])
                S.op("dve", lambda e: e.tensor_tensor(
                    out=qn[:ntok].rearrange("p (h d) -> p h d", d=64), in0=z[:ntok].rearrange("p (h d) -> p h d", d=64),
                    in1=ssh[:ntok, 0:10].unsqueeze(2).broadcast_to([ntok, 10, 64]), op=ALU.mult),
                    reads=["zc%d" % so, "ssh"], writes=["qn"])
                S.op("dve", lambda e: e.tensor_tensor(out=qn[:ntok], in0=qn[:ntok], in1=gqk[:ntok], op=ALU.mult),
                     reads=["qn", "gqk", "gqk2"], writes=["qn"])
                S.op("dve", lambda e: e.tensor_tensor(
                    out=t1b[:ntok].rearrange("p (h d) -> p h d", d=64), in0=qn[:ntok].rearrange("p (h d) -> p h d", d=64),
                    in1=rt[so][:ntok, 0:64].unsqueeze(1).broadcast_to([ntok, 10, 64]), op=ALU.mult),
                    reads=["qn", "rt%d" % so], writes=["t1b"])
                for hf in range(2):
                    S.op("dve", lambda e: e.tensor_tensor(
                        out=t2b[:ntok].rearrange("p (h a b) -> p h a b", a=2, b=32)[:, :, :, hf * 16:hf * 16 + 16],
                        in0=qn[:ntok].rearrange("p (h a b) -> p h a b", a=2, b=32)[:, :, :, (1 - hf) * 16:(1 - hf) * 16 + 16],
                        in1=rt[so][:ntok, 64:128].rearrange("p (a b) -> p a b", a=2)[:, :, hf * 16:hf * 16 + 16]
                        .unsqueeze(1).broadcast_to([ntok, 10, 2, 16]), op=ALU.mult),
                        reads=["qn", "rt%d" % so], writes=["t2b%d" % hf])
                S.op("dve", lambda e: e.tensor_tensor(out=qko[so][:ntok], in0=t1b[:ntok], in1=t2b[:ntok], op=ALU.add),
                     reads=["t1b", "t2b0", "t2b1"], writes=["qko%d" % so])

            def chain(ntok, so, kb, is_meta):
                chain_a(ntok, so, kb, is_meta)
                chain_b(ntok, so, kb, is_meta)

            def chain_pe(ntok, so, kb, is_meta):
                for j in range(5):
                    S.op("pe", lambda e: e.transpose(
                        out=psb(5)[:, j * 128:j * 128 + ntok], in_=qko[so][:ntok, j * 128:(j + 1) * 128],
                        identity=ident[:ntok, :ntok]), reads=["qko%d" % so, "ident"], writes=["ps5"])
                S.op("dve", lambda e: e.tensor_copy(out=KT[:, kb * 128:kb * 128 + ntok], in_=psb(5)[:, 512:512 + ntok]),
                     reads=["ps5"], writes=["KT%d" % kb])
                if not is_meta:
                    S.op("dve", lambda e: e.tensor_copy(out=qTs[so], in_=psb(5)[:, 0:512]), reads=["ps5"], writes=["qTs%d" % so])
                    dma("sp", qt_s[kb, :, :], qTs[so], reads=["qTs%d" % so], sem="qTs%d" % so)

            load_x(0)
            front(0)
            sub_ctr = 0
            gate_ctr = 0
            pending = None
            pending2 = None
            for mi, (t0, subs, is_meta) in enumerate(mts):
                sl = mi % 2
                if mi + 1 < len(mts):
                    load_x(mi + 1)
                ns = len(subs)
                for s, ntok in enumerate(subs):
                    kb = (t0 // 128) + s
                    so = sub_ctr % 2
                    sub_ctr += 1
                    dma("sp", rt[so][:, :], rope_d[n][kb * 128:(kb + 1) * 128, :], writes=["rt%d" % so], sem="rt%d" % so)
                    for j, (c0, c1) in enumerate([(0, 512), (512, 1024), (1024, 1536), (1536, 1792)]):
                        for c in range(8):
                            S.op("pe", lambda e: e.matmul(
                                ps[:ntok, j, 0:c1 - c0], lhsT=hT[sl][:, c, s * 128:s * 128 + ntok], rhs=Win[:, c, c0:c1],
                                start=(c == 0), stop=(c == 7)),
                                reads=["hT%d_%d" % (sl, s)] + WINK, writes=["ps%d" % j])
                    S.op("dve", lambda e: e.tensor_copy(out=zc[so][:ntok], in_=psflat[:ntok, 0:640]),
                         reads=["ps0", "ps1"], writes=["zc%d" % so])
                    S.op("dve", lambda e: e.tensor_copy(out=Vt[:ntok, kb, :], in_=psflat[:ntok, 640:768]),
                         reads=["ps1"], writes=["Vt%d" % kb])
                    S.op("act", lambda e: e.activation(out=zfs[so][:ntok], in_=psflat[:ntok, 768:1792], func=AF.Copy),
                         reads=["ps1", "ps2", "ps3"], writes=["zfs%d" % so])
                    p0 = 0 if is_meta else 16 + t0 + s * 128
                    dma("sp", zf[p0:p0 + ntok, :], zfs[so][:ntok], reads=["zfs%d" % so], sem="zfs%d" % so)
                    has_next = mi + 1 < len(mts) and s < len(mts[mi + 1][1])
                    if pending is not None:
                        chain_a(*pending)
                    if has_next:
                        front_a(mi + 1, s)
                    if not is_meta:
                        for jj in range(4):
                            j = s * 4 + jj
                            gb = 6 + (gate_ctr % 2)
                            gsl = (gate_ctr // 4) % 2
                            gate_ctr += 1
                            for c in range(8):
                                S.op("pe", lambda e: e.matmul(
                                    ps[:, gb, :], lhsT=Win[:, c, C_G + j * 128:C_G + (j + 1) * 128], rhs=hT[sl][:, c, :],
                                    start=(c == 0), stop=(c == 7)),
                                    reads=["hT%d_%d" % (sl, q) for q in range(4)] + WINK, writes=["ps%d" % gb])
                            S.op("act", lambda e: e.activation(out=gst[gsl][:, jj, :], in_=ps[:, gb, :], func=AF.Sigmoid),
                                 reads=["ps%d" % gb], writes=["gst%d_%d" % (gsl, jj)])
                            if jj == 3:
                                dma("sp", g_s[:, j - 3:j + 1, t0:t0 + 512], gst[gsl][:, :, :],
                                    reads=["gst%d_%d" % (gsl, q) for q in range(4)], sem="gst%d" % gsl)
                    if has_next:
                        front_b(mi + 1, s)
                    if pending is not None:
                        chain_b(*pending)
                    if pending2 is not None:
                        chain_pe(*pending2)
                    pending2 = pending
                    pending = (ntok, so, kb, is_meta)
                    if has_next:
                        front_pe(mi + 1, s)
            if pending2 is not None:
                chain_pe(*pending2)
            chain(*pending)
            chain_pe(*pending)
            S.fence()

            AR.top = B_MARK
            qT = [AR.alloc([512], BF16) for _ in range(2)]
            pT = [AR.alloc([2, 512], BF16) for _ in range(4)]
            psum2 = [AR.alloc([2, 512], BF16) for _ in range(2)]
            rec = AR.alloc([512], F32)
            osb = AR.alloc([512], F32)
            oT = [AR.alloc([512], BF16) for _ in range(2)]
            dma("sp", qT[0], qt_s[0, :, :], writes=["qT0"], sem="qT0")
            b_top = AR.top
            AR.top = P_MARK
            T1 = AR.alloc([N2, 3, N1], BF16)
            assert AR.top <= P_MARK + T1_WORDS
            AR.top = b_top
            dma("sp", T1[:N1], t1_d[n][:, :, :, :], sem="c7")
            if si == nseq - 1:
                for c in range(8):
                    dma("pool", w1_s[c * 128:(c + 1) * 128, :], w_fi[c * 128:(c + 1) * 128, :], sem="wc1")
                for jq in range(NJ):
                    dma("pool", w2_s[jq * 128:(jq + 1) * 128, :], w_f2[jq * 128:(jq + 1) * 128, :], sem="wc2")
            its = [(qb, kb) for qb in range(nqb) for kb in range(nkb)]

            def emit_qk(i):
                qb, kb = its[i]
                nk = 128 if kb < nkb - 1 else 16
                sb = 2 * (i % 3)
                qs = qb % 2
                for h in range(2):
                    S.op("pe", lambda e, h=h, nk=nk, sb=sb, qs=qs, kb=kb: e.matmul(
                        ps[:nk, sb + h, :], lhsT=KT[h * 64:(h + 1) * 64, kb * 128:kb * 128 + nk],
                        rhs=qT[qs][h * 64:(h + 1) * 64, :], start=True, stop=True),
                        reads=["qT%d" % qs], writes=["ps%d" % (sb + h)])

            norm_q = []

            def run_norm(cnt):
                for _ in range(cnt):
                    o_, qb_, qq = norm_q.pop(0)
                    cs = slice(qq * 128, (qq + 1) * 128)
                    S.op("dve", lambda e: e.reciprocal(out=rec[:, cs], in_=rec[:, cs]), reads=["rec"], writes=["rec%d" % qq])
                    S.op("dve", lambda e: e.tensor_tensor(out=oT[o_][:, cs], in0=osb[:, cs], in1=rec[:, cs], op=ALU.mult),
                         reads=["osb", "rec%d" % qq], writes=["oT%d_%d" % (o_, qq)])
                    if qq == 3:
                        dma("sp", ot_s[qb_, :, :], oT[o_], reads=["oT%d_%d" % (o_, q) for q in range(4)], sem="oT%d" % o_)

            emit_qk(0)
            if len(its) > 1:
                emit_qk(1)
            pend_sum = None
            for i, (qb, kb) in enumerate(its):
                nk = 128 if kb < nkb - 1 else 16
                sb = 2 * (i % 3)
                pslot = i % 4
                ob = 6
                if kb == 0 and qb + 1 < nqb:
                    dma("sp", qT[(qb + 1) % 2], qt_s[qb + 1, :, :], writes=["qT%d" % ((qb + 1) % 2)], sem="qT%d" % ((qb + 1) % 2))
                if i + 2 < len(its):
                    emit_qk(i + 2)
                S.op("act", lambda e: e.activation(
                    out=pT[pslot][:nk], in_=ps[:nk, sb:sb + 2, :], func=AF.Exp, bias=EXP_SHIFT, scale=0.125),
                    reads=["ps%d" % sb, "ps%d" % (sb + 1)], writes=["pT%d" % pslot])
                first, last = (kb == 0), (kb == nkb - 1)
                for h in range(2):
                    S.op("pe", lambda e: e.matmul(
                        ps[h * 64:(h + 1) * 64, ob, :], lhsT=Vt[:nk, kb, h * 64:(h + 1) * 64], rhs=pT[pslot][:nk, h, :],
                        start=first, stop=last, tile_position=(0, h * 64)),
                        reads=["pT%d" % pslot], writes=["ps%d_%d" % (ob, h)])
                if pend_sum is not None:
                    pp2, pst = pend_sum
                    pend_sum = None
                    for h in range(2):
                        S.op("pe", lambda e: e.matmul(
                            ps[h * 64:(h + 1) * 64, ob + 1, :], lhsT=ones64[:, :], rhs=psum2[pp2][:, h, :],
                            start=pst, stop=False, tile_position=(0, h * 64)),
                            reads=["psum2_%d" % pp2], writes=["ps%d_%d" % (ob + 1, h)])
                if last:
                    for h in range(2):
                        S.op("pe", lambda e: e.matmul(
                            ps[h * 64:(h + 1) * 64, ob + 1, :], lhsT=ones64[:nk, :], rhs=pT[pslot][:nk, h, :],
                            start=False, stop=True, tile_position=(0, h * 64)),
                            reads=["pT%d" % pslot], writes=["ps%d_%d" % (ob + 1, h)])
                elif kb % 2 == 1:
                    p2 = (kb // 2) % 2
                    pprev = (i - 1) % 4
                    S.op("dve", lambda e: e.tensor_tensor(out=psum2[p2], in0=pT[pprev], in1=pT[pslot], op=ALU.add),
                         reads=["pT%d" % pprev, "pT%d" % pslot], writes=["psum2_%d" % p2])
                    pend_sum = (p2, kb == 1)
                if norm_q and (last or (kb >= 2 and kb % 2 == 0)):
                    run_norm(len(norm_q) if last else 1)
                if last:
                    os_ = qb % 2
                    S.op("dve", lambda e: e.tensor_copy(out=rec, in_=ps[:, ob + 1, :]),
                         reads=["ps%d_0" % (ob + 1), "ps%d_1" % (ob + 1)], writes=["rec"])
                    S.op("dve", lambda e: e.tensor_copy(out=osb, in_=ps[:, ob, :]),
                         reads=["ps%d_0" % ob, "ps%d_1" % ob], writes=["osb"])
                    for qq in range(4):
                        norm_q.append((os_, qb, qq))
            run_norm(len(norm_q))
            S.fence()

            AR.top = P_MARK + T1_WORDS
            YT = AR.alloc([4, L], BF16)
            Wao = AR.alloc([4, D], BF16)
            Wfo = AR.alloc([4, D], BF16)
            Wo = AR.alloc([8, D], BF16)
            C_MARK = AR.top
            for h in range(2):
                dma("pool", Wao[h * 64:(h + 1) * 64], w_ao[h * 256:(h + 1) * 256, :].rearrange("(g d) o -> d g o", d=64),
                    writes=["Wao"], sem="w3")
            dma("pool", Wfo, w_fo.rearrange("(g c) o -> c g o", c=128), writes=["Wfo"], sem="w4")
            dma("pool", Wo, w_out.rearrange("(c p) o -> p c o", p=128), writes=["Wo"], sem="w5")
            NB = 4
            T2 = AR.alloc([2, N2], BF16)
            NZ = 4
            zin = [AR.alloc([NB, 1024], BF16) for _ in range(NZ)]
            bst = [AR.alloc([NB, 2, 512], BF16) for _ in range(3)]
            dma("sp", T2[:N2], t2_d[n][:, :, :], writes=["T2"], sem="c8")
            zf3 = zf.rearrange("(a b) c -> a b c", b=N2)
            grp = [(a, min(NB, N2 - a)) for a in range(0, N2, NB)]

            def load_z(gi):
                a, nb = grp[gi]
                dma("sp", zin[gi % NZ][:N1, 0:nb, :], zf3[:, a:a + nb, :], writes=["zin%d" % (gi % NZ)], sem="zin%d" % (gi % NZ))

            for gi in range(min(NZ - 1, len(grp))):
                load_z(gi)
            ctr = 0
            for gi, (a, nb) in enumerate(grp):
                if gi + NZ - 1 < len(grp):
                    load_z(gi + NZ - 1)
                zs = gi % NZ
                bs_ = gi % 3
                for j in range(nb):
                    n2 = a + j
                    br = 2 * (ctr % 2)
                    ctr += 1
                    zr = zin[zs][:N1, j, 0:512]
                    zi = zin[zs][:N1, j, 512:1024]
                    for (bank, m0, r0, m1, r1) in [(br, 0, zr, 1, zi), (br + 1, 2, zr, 0, zi)]:
                        S.op("pe", lambda e, bank=bank, m0=m0, r0=r0, n2=n2: e.matmul(
                            ps[:N1, bank, :], lhsT=T1[:N1, n2, m0, :], rhs=r0, start=True, stop=False),
                            reads=["T1", "zin%d" % zs], writes=["ps%d" % bank])
                        S.op("pe", lambda e, bank=bank, m1=m1, r1=r1, n2=n2: e.matmul(
                            ps[:N1, bank, :], lhsT=T1[:N1, n2, m1, :], rhs=r1, start=False, stop=True),
                            reads=["T1", "zin%d" % zs], writes=["ps%d" % bank])
                    if ctr % 2 == 0:
                        S.op("dve", lambda e: e.tensor_copy(out=bst[bs_][:N1, j, :, :], in_=ps[:N1, br:br + 2, :]),
                             reads=["ps%d" % br, "ps%d" % (br + 1)], writes=["bst%d_%d" % (bs_, j)])
                    else:
                        S.op("act", lambda e: e.activation(out=bst[bs_][:N1, j, :, :], in_=ps[:N1, br:br + 2, :], func=AF.Copy),
                             reads=["ps%d" % br, "ps%d" % (br + 1)], writes=["bst%d_%d" % (bs_, j)])
                dma("sp", bs[:, a:a + nb, :, :], bst[bs_][:N1, 0:nb, :, :], reads=["bst%d_%d" % (bs_, j) for j in range(nb)],
                    sem="bst%d" % bs_)
            S.fence()
            f_top = AR.top
            AR.top = P_MARK
            oin = [AR.alloc([4, 512], BF16) for _ in range(2)]
            gins = [AR.alloc([16, 512], BF16) for _ in range(2)]
            assert AR.top <= P_MARK + T1_WORDS
            AR.top = f_top

            def load_c1(m):
                dma("sp", oin[m % 2], ot_s[4 * m:4 * m + 4].rearrange("q p f -> p q f"), writes=["oin%d" % (m % 2)], sem="oin%d" % (m % 2))
                dma("sp", gins[m % 2], g_s[:, :, m * 512:(m + 1) * 512], writes=["gin%d" % (m % 2)], sem="gin%d" % (m % 2))

            load_c1(0)
            bin_ = zin
            grp2 = [(a, min(NB, N1 - a)) for a in range(0, N1, NB)]

            def load_b(gi):
                a, nb = grp2[gi]
                dma("sp", bin_[gi % NZ][:N2, 0:nb, :], bs[a:a + nb].rearrange("k n r c -> n k (r c)"),
                    writes=["bin%d" % (gi % NZ)], sem="zin%d" % (gi % NZ))

            for gi in range(min(NZ - 1, len(grp2))):
                load_b(gi)
            ctr = 0
            for gi, (a, nb) in enumerate(grp2):
                if gi + NZ - 1 < len(grp2):
                    load_b(gi + NZ - 1)
                zs = gi % NZ
                for j in range(nb):
                    k1 = a + j
                    bank = ctr % 2
                    ctr += 1
                    for g in range(4):
                        for r in range(2):
                            S.op("pe", lambda e, g=g, r=r, j=j, zs=zs, bank=bank: e.matmul(
                                ps[:, bank, g * N2:(g + 1) * N2], lhsT=bin_[zs][:N2, j, r * 512 + g * 128:r * 512 + (g + 1) * 128],
                                rhs=T2[:N2, r, :], start=(r == 0), stop=(r == 1)),
                                reads=["T2", "bin%d" % zs], writes=["ps%d" % bank])
                    if ctr % 2 == 0:
                        S.op("dve", lambda e: e.tensor_copy(
                            out=YT[:, :, k1:k1 + N1 * (N2 - 1) + 1:N1],
                            in_=ps[:, bank, 0:4 * N2].rearrange("p (g k) -> p g k", g=4)),
                            reads=["ps%d" % bank], writes=["YT%d" % (ctr % 2)])
                    else:
                        S.op("act", lambda e: e.activation(
                            out=YT[:, :, k1:k1 + N1 * (N2 - 1) + 1:N1],
                            in_=ps[:, bank, 0:4 * N2].rearrange("p (g k) -> p g k", g=4), func=AF.Copy),
                            reads=["ps%d" % bank], writes=["YT%d" % (ctr % 2)])
            S.fence()

            AR.top = C_MARK
            ta = [AR.alloc([512], F32) for _ in range(2)]
            tb = [AR.alloc([512], F32) for _ in range(2)]
            mg = [AR.alloc([8, 512], BF16) for _ in range(2)]
            xin = [AR.alloc([D], F32) for _ in range(3)]
            x1o = [AR.alloc([D], F32) for _ in range(2)]
            xctr = 0

            def c1_merge(m):
                ms = m % 2
                t0 = m * 512
                gin = gins[ms]
                gk = "gin%d" % ms
                if m + 1 < nmt:
                    load_c1(m + 1)
                for oc in range(8):
                    ba = (oc % 2)
                    bf = 2 + (oc % 2)
                    for g in range(4):
                        S.op("pe", lambda e: e.matmul(
                            ps[:, ba, :], lhsT=Wao[:, g, oc * 128:(oc + 1) * 128],
                            rhs=oin[ms].rearrange("p q (g t) -> p q g t", g=4)[:, :, g, :],
                            start=(g == 0), stop=(g == 3)), reads=["Wao", "oin%d" % ms], writes=["ps%d" % ba])
                    for g in range(4):
                        S.op("pe", lambda e: e.matmul(
                            ps[:, bf, :], lhsT=Wfo[:, g, oc * 128:(oc + 1) * 128], rhs=YT[:, g, 16 + t0:16 + t0 + 512],
                            start=(g == 0), stop=(g == 3)), reads=["Wfo", "YT"], writes=["ps%d" % bf])
                    o2 = oc % 2
                    S.op("dve", lambda e: e.tensor_tensor(out=ta[o2], in0=ps[:, ba, :], in1=gin[:, oc, :], op=ALU.mult),
                         reads=["ps%d" % ba, gk], writes=["ta%d" % o2])
                    S.op("dve", lambda e: e.tensor_tensor(out=tb[o2], in0=ps[:, bf, :], in1=gin[:, 8 + oc, :], op=ALU.mult),
                         reads=["ps%d" % bf, gk], writes=["tb%d" % o2])
                    S.op("pool", lambda e: e.tensor_tensor(out=mg[ms][:, oc, :], in0=ta[o2], in1=tb[o2], op=ALU.add),
                         reads=["ta%d" % o2, "tb%d" % o2], writes=["mg%d_%d" % (ms, oc)])

            c1_merge(0)
            for m in range(nmt):
                ms = m % 2
                t0 = m * 512
                if m + 1 < nmt:
                    c1_merge(m + 1)
                for s in range(4):
                    xsl = xctr % 3
                    osl = xctr % 2
                    xctr += 1
                    tt = t0 + s * 128
                    dma("sp", xin[xsl], x_d[tt:tt + 128, :], writes=["xin%d" % xsl], sem="xin%d" % xsl)
                    pb = 4 + 2 * (s % 2)
                    for hh in range(2):
                        for oc in range(8):
                            S.op("pe", lambda e: e.matmul(
                                ps[:, pb + hh, :], lhsT=mg[ms][:, oc, s * 128:(s + 1) * 128], rhs=Wo[:, oc, hh * 512:(hh + 1) * 512],
                                start=(oc == 0), stop=(oc == 7)), reads=["mg%d_%d" % (ms, oc), "Wo"], writes=["ps%d" % (pb + hh)])
                    S.op("dve", lambda e: e.tensor_tensor(
                        out=x1o[osl], in0=xin[xsl], in1=psflat[:, pb * 512:(pb + 2) * 512], op=ALU.add),
                        reads=["xin%d" % xsl, "ps%d" % pb, "ps%d" % (pb + 1)], writes=["x1o%d" % osl])
                    dma("sp", x1_s[si][tt:tt + 128, :], x1o[osl], reads=["x1o%d" % osl], sem="x1o%d" % osl)
            S.fence()

        AR.top = P_MARK
        W1 = AR.alloc([8, 2 * DFF], BF16)
        W2 = AR.alloc([NJ, D], BF16)
        gfin = AR.alloc([D], F32)
        gffn = AR.alloc([D], F32)
        xn = [AR.alloc([D], F32) for _ in range(4)]
        xr = [AR.alloc([D], F32) for _ in range(2)]
        ssq = AR.alloc([8], F32)
        hb2 = [AR.alloc([D], BF16) for _ in range(2)]
        hT2 = AR.alloc([8, 512], BF16)
        sg = [AR.alloc([512], BF16) for _ in range(2)]
        actT = AR.alloc([NJ, 512], BF16)
        yo = AR.alloc([D], F32)
        dma("sp", gffn, bcast_row(g_ffn_t, D), writes=["gffn"], sem="c6")
        dma("sp", gfin, bcast_row(g_fin_t, D), writes=["gfin"], sem="c7")

        tiles = [(si, m) for si, n in enumerate(seq_ns) for m in range(n // 512)]

        def load_xn(ti):
            si, m = tiles[ti]
            for s in range(4):
                tt = m * 512 + s * 128
                dma("sp", xn[s], x1_s[si][tt:tt + 128, :], writes=["xn%d" % s], sem="xn%d" % s)

        def front2_dve(s):
            hs = s % 2
            rms_rstd(xn[s][:, :], 128, hb2[hs], ssq, s, "xn%d" % s, "hb2_%d" % hs, "ssqn%d" % s)
            sqrt_recip(ssq[:, s:s + 1], 1.0 / D, ["ssqn%d" % s])
            S.op("dve", lambda e: e.scalar_tensor_tensor(out=hb2[hs], in0=xn[s], scalar=ssq[:, s:s + 1], in1=gffn, op0=ALU.mult, op1=ALU.mult),
                 reads=["xn%d" % s, "ssqn%d" % s, "gffn"], writes=["hb2_%d" % hs])

        def front2_pe(s):
            hs = s % 2
            for c in range(8):
                S.op("pe", lambda e: e.transpose(out=psb(7)[:, c * 128:(c + 1) * 128], in_=hb2[hs][:, c * 128:(c + 1) * 128],
                                                 identity=ident), reads=["hb2_%d" % hs, "ident"], writes=["ps7"])
            S.op("dve", lambda e: e.tensor_copy(out=hT2[:, :, s * 128:(s + 1) * 128], in_=psb(7).rearrange("p (c t) -> p c t", c=8)),
                 reads=["ps7"], writes=["hT2_%d" % s])

        load_xn(0)
        NBLK = (NJ + 3) // 4
        w1v = w1_s.rearrange("(c p) f -> p c f", p=128)
        for bk in range(NBLK):
            j0, j1 = bk * 4, min(NJ, bk * 4 + 4)
            for half in range(2):
                c0 = half * DFF + j0 * 128
                c1 = half * DFF + j1 * 128
                dma("sp", W1[:, :, c0:c1], w1v[:, :, c0:c1], writes=["W1b_%d" % bk], sem="w6_%d" % bk)
        for jq in range(0, NJ, 2):
            dma("sp", W2[:, jq:jq + 2, :], w2_s[jq * 128:(jq + 2) * 128, :].rearrange("(c p) o -> p c o", p=128),
                writes=["W2"], sem="w7")
        for s in range(4):
            front2_dve(s)
            front2_pe(s)
        fctr = 0
        octr = 0
        rctr = 0
        hkeys = ["hT2_%d" % s for s in range(4)]
        for ti, (si, m) in enumerate(tiles):
            t0 = m * 512
            x1d = x1_s[si]
            if ti + 1 < len(tiles):
                load_xn(ti + 1)
            for j in range(NJ):
                bg = 2 * (fctr % 2)
                ss_ = fctr % 2
                fctr += 1
                for (bank, col0) in [(bg, j * 128), (bg + 1, DFF + j * 128)]:
                    for c in range(8):
                        S.op("pe", lambda e: e.matmul(ps[:, bank, :], lhsT=W1[:, c, col0:col0 + 128], rhs=hT2[:, c, :],
                                                      start=(c == 0), stop=(c == 7)),
                             reads=hkeys + ["W1b_%d" % (j // 4)], writes=["ps%d" % bank])
                S.op("act", lambda e: e.activation(out=sg[ss_], in_=ps[:, bg, :], func=AF.Silu), reads=["ps%d" % bg], writes=["sg%d" % ss_])
                S.op("dve", lambda e: e.tensor_tensor(out=actT[:, j, :], in0=ps[:, bg + 1, :], in1=sg[ss_], op=ALU.mult),
                     reads=["ps%d" % (bg + 1), "sg%d" % ss_], writes=["actT%d" % j])
            for s in range(4):
                tt = t0 + s * 128
                rs = rctr % 2
                rctr += 1
                dma("sp", xr[rs], x1d[tt:tt + 128, :], writes=["xr%d" % rs], sem="xr%d" % rs)
                if ti + 1 < len(tiles):
                    front2_dve(s)
                banks = []
                for hh in range(2):
                    bank = 4 + (octr % 3)
                    octr += 1
                    banks.append(bank)
                    for j in range(NJ):
                        S.op("pe", lambda e: e.matmul(ps[:, bank, :], lhsT=actT[:, j, s * 128:(s + 1) * 128],
                                                      rhs=W2[:, j, hh * 512:(hh + 1) * 512], start=(j == 0), stop=(j == NJ - 1)),
                             reads=["actT%d" % j, "W2"], writes=["ps%d" % bank])
                if ti + 1 < len(tiles):
                    front2_pe(s)
                for hh in range(2):
                    S.op("dve", lambda e: e.tensor_tensor(out=xr[rs][:, hh * 512:(hh + 1) * 512], in0=xr[rs][:, hh * 512:(hh + 1) * 512],
                                                          in1=ps[:, banks[hh], :], op=ALU.add),
                         reads=["xr%d" % rs, "ps%d" % banks[hh]], writes=["xr%d" % rs])
                rms_rstd(xr[rs][:, :], 128, yo, ssq, 4, "xr%d" % rs, "yo", "ssq4")
                sqrt_recip(ssq[:, 4:5], 1.0 / D, ["ssq4"])
                S.op("dve", lambda e: e.scalar_tensor_tensor(out=yo, in0=xr[rs], scalar=ssq[:, 4:5], in1=gfin, op0=ALU.mult, op1=ALU.mult),
                     reads=["xr%d" % rs, "ssq4", "gfin"], writes=["yo"])
                dma("sp", ys[si][tt:tt + 128, :], yo, reads=["yo"], sem="yo")
        S.fence()
        S.emit(block)
    return nc


SEQ_NS = [2048, 2048, 8192]
_CACHE = {}


def _get_program(seq_ns):
    key = tuple(seq_ns)
    if key not in _CACHE:
        _CACHE[key] = (build(list(seq_ns)), host_consts(list(seq_ns)))
    return _CACHE[key]


def run_cores(seq_ns, per_core_x, weights):
    nc, consts = _get_program(seq_ns)
    in_maps = []
    for xl in per_core_x:
        m = dict(consts)
        m.update(weights)
        for i, x in enumerate(xl):
            m["x%d" % i] = np.ascontiguousarray(x, dtype=np.float32)
        in_maps.append(m)
    res = run_bass_kernel_spmd(nc, in_maps, core_ids=list(range(len(in_maps))))
    return [[r["y%d" % i] for i in range(len(seq_ns))] for r in res.results]


def kernel(x_prompt, x_sample, meta_tokens, norm_mix_g, w_in, q_norm_g, k_norm_g, w_attn_o, w_four_o,
           w_out, norm_ffn_g, w_ffn_in, w_ffn_out, final_norm_g):
    f = lambda a: np.ascontiguousarray(np.asarray(a), dtype=np.float32)
    weights = {
        "meta_tokens": f(meta_tokens), "norm_mix_g": f(norm_mix_g).reshape(-1), "q_norm_g": f(q_norm_g).reshape(-1),
        "k_norm_g": f(k_norm_g).reshape(-1), "norm_ffn_g": f(norm_ffn_g).reshape(-1), "final_norm_g": f(final_norm_g).reshape(-1),
        "w_in": f(w_in)[0], "w_attn_o": f(w_attn_o)[0], "w_four_o": f(w_four_o)[0], "w_out": f(w_out)[0],
        "w_ffn_in": f(w_ffn_in)[0], "w_ffn_out": f(w_ffn_out)[0],
    }
    xp = np.asarray(x_prompt)
    xsm = np.asarray(x_sample)
    per_core = [[xp[2 * c], xp[2 * c + 1], xsm[c]] for c in range(8)]
    outs = run_cores(SEQ_NS, per_core, weights)
    y_prompt = np.stack([outs[c][k] for c in range(8) for k in range(2)], 0).astype(np.float32)
    y_sample = np.stack([outs[c][2] for c in range(8)], 0).astype(np.float32)
    return (y_prompt, y_sample)
```

```python
import bisect
from contextlib import ExitStack
import numpy as np
import ml_dtypes
import concourse.bass as bass
import concourse.mybir as mybir
from concourse.bass_utils import run_bass_kernel_spmd

F32 = mybir.dt.float32
BF16 = mybir.dt.bfloat16
AF = mybir.ActivationFunctionType
ALU = mybir.AluOpType
AX = mybir.AxisListType

D = 1024
DFF = 2816
NJ = DFF // 128
WIN_COLS = 3840
C_K, C_V, C_F, C_G = 512, 640, 768, 1792
FACT = {528: (24, 22), 1040: (40, 26), 2064: (86, 24), 8208: (114, 72)}
EPS = 1e-6
EXP_SHIFT = -10.0

ENGS = ["pe", "act", "dve", "pool"]
EIDX = {e: i for i, e in enumerate(ENGS)}
EPOCH = 30000


class DSem:
    def __init__(self, h, name):
        self.h = h
        self.count = 0
        self.name = name


class _Rec:
    def __init__(self):
        self.call = None

    def __getattr__(self, name):
        def f(*a, **k):
            self.call = (name, a, k)
            return self
        return f


class Sched:
    def __init__(self, nc, stack):
        self.nc = nc
        self.stack = stack
        self.streams = {e: [] for e in ENGS + ["sp"]}
        self.nops = {e: 0 for e in ENGS}
        self.snap = {e: [] for e in ENGS}
        self.signaled = {e: [] for e in ENGS}
        self.clock = {e: [-1, -1, -1, -1] for e in ENGS + ["sp"]}
        self.known_sem = {e: {} for e in ENGS + ["sp"]}
        self.res = {}
        self.dsems = []

    def dsem(self, name):
        h = self.stack.enter_context(self.nc.semaphore(name))
        s = DSem(h, name)
        self.dsems.append(s)
        return s

    def _need(self, eng, tok):
        if tok is None:
            return
        if tok[0] == "eng":
            _, E, idx = tok
            if E == "pe" and eng == "pe":
                return
            if self.clock[eng][EIDX[E]] >= idx:
                return
            sl = self.signaled[E]
            j = bisect.bisect_left(sl, idx)
            if j < len(sl) and sl[j] - idx <= (3 if E == "pe" else 0):
                tgt = sl[j]
            else:
                bisect.insort(sl, idx)
                tgt = idx
            self.streams[eng].append(("we", E, tgt))
            sn = self.snap[E][tgt]
            c = self.clock[eng]
            for i in range(4):
                if sn[i] > c[i]:
                    c[i] = sn[i]
            if c[EIDX[E]] < tgt:
                c[EIDX[E]] = tgt
        else:
            _, s, cnt = tok
            if self.known_sem[eng].get(s, 0) >= cnt:
                return
            self.streams[eng].append(("ws", s, cnt))
            self.known_sem[eng][s] = cnt

    def op(self, eng, fn, reads=(), writes=(), dsem=None):
        rec = _Rec()
        fn(rec)
        fn = rec.call
        deps = []
        for r in reads:
            st = self.res.get(r)
            if st is not None:
                deps.append(st[0])
        for w in writes:
            st = self.res.get(w)
            if st is not None:
                deps.append(st[0])
                deps.extend(st[1])
        for d in deps:
            self._need(eng, d)
        if dsem is not None:
            dsem.count += 16
            tok = ("dma", dsem, dsem.count)
            self.streams[eng].append(("dma", fn, dsem))
        else:
            idx = self.nops[eng]
            self.nops[eng] = idx + 1
            tok = ("eng", eng, idx)
            self.snap[eng].append(tuple(self.clock[eng]))
            if eng == "pe":
                self.clock[eng][0] = idx
            self.streams[eng].append(("op", fn, idx))
        for w in writes:
            self.res[w] = [tok, []]
        for r in reads:
            st = self.res.get(r)
            if st is None:
                st = self.res[r] = [None, []]
            rl = st[1]
            for i, t in enumerate(rl):
                if t[0] == tok[0] and t[1] == tok[1]:
                    rl[i] = tok
                    break
            else:
                rl.append(tok)
        return tok

    def wait_dma(self, eng, sems):
        for s_ in sems:
            if s_.count:
                self._need(eng, ("dma", s_, s_.count))

    def fence(self):
        toks = [("eng", E, self.nops[E] - 1) for E in ENGS if self.nops[E] > 0]
        toks += [("dma", s, s.count) for s in self.dsems if s.count]
        for eng in ENGS + ["sp"]:
            for t in toks:
                self._need(eng, t)
        self.res = {}

    def emit(self, block):
        nc = self.nc
        esem = {}
        for e in ENGS:
            self.signaled[e].sort()
            n = len(self.signaled[e])
            ne = max(1, (n + EPOCH - 1) // EPOCH)
            esem[e] = [self.stack.enter_context(nc.semaphore("es_%s_%d" % (e, k))) for k in range(ne)]
        rank = {e: {idx: i for i, idx in enumerate(self.signaled[e])} for e in ENGS}

        def run(eng, eo):
            for ent in self.streams[eng]:
                k = ent[0]
                if k == "we":
                    r = rank[ent[1]][ent[2]]
                    eo.wait_ge(esem[ent[1]][r // EPOCH], r % EPOCH + 1)
                elif k == "ws":
                    eo.wait_ge(ent[1].h, ent[2])
                elif k == "dma":
                    nm, a, kw = ent[1]
                    getattr(eo, nm)(*a, **kw).then_inc(ent[2].h, 16)
                else:
                    nm, a, kw = ent[1]
                    ins = getattr(eo, nm)(*a, **kw)
                    r = rank[eng].get(ent[2])
                    if r is not None:
                        ins.then_inc(esem[eng][r // EPOCH], 1)

        @block.tensor
        def _(t):
            run("pe", t)

        @block.scalar
        def _(t):
            run("act", t)

        @block.vector
        def _(t):
            run("dve", t)

        @block.gpsimd
        def _(t):
            run("pool", t)

        @block.sync
        def _(t):
            run("sp", t)


class Arena:
    def __init__(self, base, nwords):
        self.base = base
        self.n = nwords
        self.top = 0

    def alloc(self, shape, dtype):
        nel = int(np.prod(shape))
        words = nel if dtype == F32 else (nel + 1) // 2
        a = self.base[:, self.top:self.top + words]
        self.top += words
        assert self.top <= self.n, ("SBUF arena overflow", self.top, self.n)
        if dtype != F32:
            a = a.bitcast(dtype)[:, 0:nel]
        if len(shape) == 2:
            a = a.rearrange("p (a b) -> p a b", a=shape[0], b=shape[1])
        elif len(shape) == 3:
            a = a.rearrange("p (a b c) -> p a b c", a=shape[0], b=shape[1], c=shape[2])
        return a


def host_consts(seq_ns):
    c = {}
    c["ident"] = np.eye(128, dtype=np.float32).astype(ml_dtypes.bfloat16)
    c["identf"] = np.eye(128, dtype=np.float32)
    k = np.arange(128, dtype=np.float64)
    ang = 2 * np.pi * np.outer(k, k) / 128.0
    c["cs128"] = (np.concatenate([np.cos(ang), -np.sin(ang)], 1) / np.sqrt(128.0)).astype(np.float32)
    inv_freq = 1.0 / (10000.0 ** (np.arange(0, 32, 2, dtype=np.float64) / 32.0))
    for n in sorted(set(seq_ns)):
        L = n + 16
        N1, N2 = FACT[L]
        nkb = n // 128 + 1
        t = np.arange(n)
        row = np.concatenate([t // 64, np.full(16, -1)]).astype(np.float64)
        col = np.concatenate([t % 64, np.arange(16)]).astype(np.float64)
        ar = row[:, None] * inv_freq[None]
        ac = col[:, None] * inv_freq[None]
        tab = np.zeros((nkb * 128, 128), np.float32)
        tab[:L, 0:64] = np.concatenate([np.cos(ar), np.cos(ar), np.cos(ac), np.cos(ac)], 1)
        tab[:L, 64:128] = np.concatenate([-np.sin(ar), np.sin(ar), -np.sin(ac), np.sin(ac)], 1)
        c["rope%d" % n] = tab
        n1 = np.arange(N1, dtype=np.float64)
        n2 = np.arange(N2, dtype=np.float64)
        a1 = 2 * np.pi * (n1[:, None, None] * n1[None, None, :] / N1 + n2[None, :, None] * n1[None, None, :] / L)
        t1 = np.stack([np.cos(a1), np.sin(a1), -np.sin(a1)], 2)
        c["t1_%d" % n] = t1.astype(np.float32).astype(ml_dtypes.bfloat16)
        a2 = 2 * np.pi * np.outer(n2, n2) / N2
        t2 = np.stack([np.cos(a2), np.sin(a2)], 1) / np.sqrt(float(L))
        c["t2_%d" % n] = t2.astype(np.float32).astype(ml_dtypes.bfloat16)
    return c


def build(seq_ns):
    nc = bass.Bass("TRN2", target_bir_lowering=False)
    nseq = len(seq_ns)

    def din(name, shape, dt=F32):
        return nc.dram_tensor(name, list(shape), dt, kind="ExternalInput")

    xs = [din("x%d" % i, [n, D]).ap() for i, n in enumerate(seq_ns)]
    ys = [nc.dram_tensor("y%d" % i, [n, D], F32, kind="ExternalOutput").ap() for i, n in enumerate(seq_ns)]
    meta = din("meta_tokens", [16, D]).ap()
    g_mix_t = din("norm_mix_g", [D])
    g_q_t = din("q_norm_g", [64])
    g_k_t = din("k_norm_g", [64])
    g_ffn_t = din("norm_ffn_g", [D])
    g_fin_t = din("final_norm_g", [D])
    w_in = din("w_in", [D, 3328]).ap()
    w_ao = din("w_attn_o", [512, D]).ap()
    w_fo = din("w_four_o", [512, D]).ap()
    w_out = din("w_out", [D, D]).ap()
    w_fi = din("w_ffn_in", [D, 2 * DFF]).ap()
    w_f2 = din("w_ffn_out", [DFF, D]).ap()
    ident_d = din("ident", [128, 128], BF16).ap()
    identf_d = din("identf", [128, 128]).ap()
    cs128_d = din("cs128", [128, 256]).ap()
    rope_d, t1_d, t2_d = {}, {}, {}
    for n in sorted(set(seq_ns)):
        N1, N2 = FACT[n + 16]
        rope_d[n] = din("rope%d" % n, [(n // 128 + 1) * 128, 128]).ap()
        t1_d[n] = din("t1_%d" % n, [N1, N2, 3, N1], BF16).ap()
        t2_d[n] = din("t2_%d" % n, [N2, 2, N2], BF16).ap()

    nmax = max(seq_ns)
    win_s = nc.dram_tensor("win_s", [128, 8 * WIN_COLS], BF16).ap()
    zf_s = nc.dram_tensor("zf_s", [nmax + 16, 1024], BF16).ap()
    N1m, N2m = FACT[nmax + 16]
    bs_s = nc.dram_tensor("bs_s", [N1m * N2m * 1024], BF16).ap()
    qt_s = nc.dram_tensor("qt_s", [nmax // 128, 128, 512], BF16).ap()
    ot_s = nc.dram_tensor("ot_s", [nmax // 128, 128, 512], BF16).ap()
    g_s = nc.dram_tensor("g_s", [128, 16, nmax], BF16).ap()
    x1_s = [nc.dram_tensor("x1_s%d" % i, [n, D], F32).ap() for i, n in enumerate(seq_ns)]
    w1_s = nc.dram_tensor("w1_s", [D, 2 * DFF], BF16).ap()
    w2_s = nc.dram_tensor("w2_s", [DFF, D], BF16).ap()

    ARW = 53100
    with ExitStack() as st:
        arena_t = st.enter_context(nc.sbuf_tensor("arena", [128, ARW], F32))
        ps = st.enter_context(nc.psum_tensor("ps", [128, 8, 512], F32))
        block = st.enter_context(nc.Block())
        S = Sched(nc, st)
        AR = Arena(arena_t, ARW)
        psflat = ps[:, :, :].rearrange("p b f -> p (b f)")

        def psb(b):
            return ps[:, b, :].bitcast(BF16)

        sem_cache = {}
        WINK = ["win_q", "win_kv", "win_g", "win_f"]

        def dsem(name):
            if name not in sem_cache:
                sem_cache[name] = S.dsem(name)
            return sem_cache[name]

        def dma(eng, out, in_, reads=(), writes=(), sem="misc", **kw):
            S.op(eng, lambda e: e.dma_start(out=out, in_=in_, **kw), reads=reads, writes=writes, dsem=dsem(sem))

        ident = AR.alloc([128], BF16)
        ones64 = AR.alloc([64], BF16)
        eps_t = AR.alloc([1], F32)
        ebias = AR.alloc([1], F32)
        gqk = AR.alloc([640], F32)
        tmp2 = AR.alloc([2], F32)
        P_MARK = AR.top

        dma("sp", ident, ident_d[:, :], writes=["ident"], sem="c0")
        S.op("pool", lambda e: e.memset(ones64, 1.0), writes=["ones64"])
        S.op("pool", lambda e: e.memset(eps_t, EPS), writes=["eps"])
        dma("sp", gqk[:, 0:512].rearrange("p (h d) -> p h d", d=64), bass.AP(g_q_t, 0, [[0, 128], [0, 8], [1, 64]]),
            writes=["gqk"], sem="c1")
        dma("sp", gqk[:, 512:640].rearrange("p (h d) -> p h d", d=64), bass.AP(g_k_t, 0, [[0, 128], [0, 2], [1, 64]]),
            writes=["gqk2"], sem="c2")
        S.op("dve", lambda e: e.tensor_reduce(out=tmp2[:, 0:1], in_=gqk[:, 0:64], axis=AX.X, op=ALU.max,
                                              apply_absolute_value=True), reads=["gqk"], writes=["tmp2a"])
        S.op("dve", lambda e: e.tensor_reduce(out=tmp2[:, 1:2], in_=gqk[:, 512:576], axis=AX.X, op=ALU.max,
                                              apply_absolute_value=True), reads=["gqk2"], writes=["tmp2b"])
        S.op("dve", lambda e: e.tensor_tensor(out=ebias, in0=tmp2[:, 0:1], in1=tmp2[:, 1:2], op=ALU.mult),
             reads=["tmp2a", "tmp2b"], writes=["ebias"])
        S.op("dve", lambda e: e.tensor_scalar(out=ebias, in0=ebias, scalar1=-8.0, scalar2=None, op0=ALU.mult),
             reads=["ebias"], writes=["ebias"])

        def bcast_row(t, nel):
            return bass.AP(t, 0, [[0, 128], [1, nel]])

        def rms_rstd(x_ap, ntok, junk, ssq, col, key_x, key_junk, key_ssq):
            S.op("dve", lambda e: e.scalar_tensor_tensor(out=junk[:ntok], in0=x_ap, scalar=1.0, in1=x_ap,
                                                         op0=ALU.mult, op1=ALU.mult, accum_out=ssq[:ntok, col:col + 1]),
                 reads=[key_x], writes=[key_junk, key_ssq])

        def sqrt_recip(ssq_ap, scale, keys):
            S.op("act", lambda e: e.activation(out=ssq_ap, in_=ssq_ap, func=AF.Sqrt, bias=eps_t[:ssq_ap.shape[0], 0:1],
                                               scale=scale), reads=keys + ["eps"], writes=keys)
            S.op("dve", lambda e: e.reciprocal(out=ssq_ap, in_=ssq_ap), reads=keys, writes=keys)

        AR.top = P_MARK
        KT_W = (nmax // 128 + 1) * 128
        T1_WORDS = max(max(FACT[n_ + 16][1] * 3 * FACT[n_ + 16][0] for n_ in seq_ns) // 2 + 2, 10240)
        Win = AR.alloc([8, WIN_COLS], BF16)
        A_MARK = AR.top
        cs128 = AR.alloc([256], F32)
        identf = AR.alloc([128], F32)
        wfT = [AR.alloc([128], F32) for _ in range(2)]
        stg = [AR.alloc([3328], F32) for _ in range(3)]
        dma("sp", identf, identf_d[:, :], writes=["identf"], sem="c3")
        dma("sp", cs128, cs128_d[:, :], writes=["cs128"], sem="c4")

        def load_stg(c):
            dma("sp", stg[c % 3], w_in[c * 128:(c + 1) * 128, :], writes=["stg%d" % (c % 3)], sem="stg%d" % (c % 3))

        load_stg(0)
        load_stg(1)
        it = 0
        for c in range(8):
            if c + 2 < 8:
                load_stg(c + 2)
            st_ = stg[c % 3]
            sk = "stg%d" % (c % 3)
            S.op("dve", lambda e: e.tensor_copy(out=Win[:, c, 0:512].rearrange("p (g k d) -> p g k d", g=4, k=2),
                                                in_=st_[:, 0:512].rearrange("p (k g d) -> p g k d", k=2, g=4)),
                 reads=[sk], writes=["win_q"])
            S.op("pool", lambda e: e.tensor_copy(out=Win[:, c, 512:768], in_=st_[:, 512:768]), reads=[sk], writes=["win_kv"])
            S.op("act", lambda e: e.activation(out=Win[:, c, C_G:WIN_COLS], in_=st_[:, 1280:3328], func=AF.Copy),
                 reads=[sk], writes=["win_g"])
            for g in range(4):
                b = it % 2
                it += 1
                S.op("pe", lambda e: e.transpose(out=ps[:, b, 0:128], in_=st_[:, 768 + g * 128:768 + (g + 1) * 128], identity=identf),
                     reads=[sk, "identf"], writes=["ps%d" % b])
                S.op("dve", lambda e: e.tensor_copy(out=wfT[b], in_=ps[:, b, 0:128]), reads=["ps%d" % b], writes=["wfT%d" % b])
                S.op("pe", lambda e: e.matmul(ps[:, 2 + b, 0:256], lhsT=wfT[b], rhs=cs128, start=True, stop=True),
                     reads=["wfT%d" % b, "cs128"], writes=["ps%d" % (2 + b)])
                S.op("dve", lambda e: e.tensor_copy(
                    out=Win[:, c, C_F:C_G].rearrange("p (r g k) -> p r g k", r=2, g=4)[:, :, g, :],
                    in_=ps[:, 2 + b, 0:256].rearrange("p (r k) -> p r k", r=2)),
                    reads=["ps%d" % (2 + b)], writes=["win_f"])
        dma("sp", win_s[:, :], Win[:, :, :].rearrange("p c f -> p (c f)"), reads=WINK, sem="ws")
        S.fence()

        for si, n in enumerate(seq_ns):
            L = n + 16
            N1, N2 = FACT[L]
            nqb = n // 128
            nkb = nqb + 1
            nmt = n // 512
            x_d = xs[si]
            zf = zf_s[0:L, :]
            bs = bs_s[0:N1 * N2 * 1024].rearrange("(k n r c) -> k n r c", k=N1, n=N2, r=2)

            AR.top = A_MARK
            KT = AR.alloc([KT_W], BF16)
            Vt = AR.alloc([KT_W // 128, 128], BF16)
            B_MARK = AR.top
            gmix = AR.alloc([D], F32)
            xbuf = [AR.alloc([4, D], F32) for _ in range(2)]
            junk = AR.alloc([D], BF16)
            ssq = AR.alloc([8], F32)
            hb = AR.alloc([4, D], BF16)
            hT = [AR.alloc([8, 512], BF16) for _ in range(2)]
            zc = [AR.alloc([640], F32) for _ in range(2)]
            sq = AR.alloc([640], F32)
            ssh = AR.alloc([16], F32)
            qn = AR.alloc([640], F32)
            t1b = AR.alloc([640], F32)
            t2b = AR.alloc([640], F32)
            qko = [AR.alloc([640], BF16) for _ in range(2)]
            zfs = [AR.alloc([1024], BF16) for _ in range(2)]
            qTs = [AR.alloc([512], BF16) for _ in range(2)]
            gst = [AR.alloc([4, 512], BF16) for _ in range(2)]
            rt = [AR.alloc([128], F32) for _ in range(2)]

            if si > 0:
                dma("sp", Win[:, :, :].rearrange("p c f -> p (c f)"), win_s[:, :], writes=WINK, sem="wl")
            dma("sp", gmix, bcast_row(g_mix_t, D), writes=["gmix"], sem="c6")

            mts = [(m * 512, [128] * 4, False) for m in range(nmt)] + [(n, [16], True)]

            def load_x(mi):
                t0, subs, is_meta = mts[mi]
                sl = mi % 2
                if is_meta:
                    dma("sp", xbuf[sl][0:16, 0, :], meta[:, :], writes=["x%d_0" % sl], sem="x%d" % sl)
                else:
                    dma("sp", xbuf[sl][:, :, :], x_d[t0:t0 + 512, :].rearrange("(s p) d -> p s d", p=128),
                        writes=["x%d_%d" % (sl, s) for s in range(4)], sem="x%d" % sl)

            def front_a(mi, s):
                t0, subs, is_meta = mts[mi]
                sl = mi % 2
                ntok = subs[s]
                rms_rstd(xbuf[sl][:ntok, s, :], ntok, junk, ssq, s, "x%d_%d" % (sl, s), "junk", "ssq%d" % s)
                S.op("act", lambda e: e.activation(out=ssq[:ntok, s:s + 1], in_=ssq[:ntok, s:s + 1], func=AF.Sqrt,
                                                   bias=eps_t[:ntok, 0:1], scale=1.0 / D), reads=["ssq%d" % s, "eps"], writes=["ssq%d" % s])

            def front_b(mi, s):
                t0, subs, is_meta = mts[mi]
                sl = mi % 2
                ntok = subs[s]
                S.op("dve", lambda e: e.reciprocal(out=ssq[:ntok, s:s + 1], in_=ssq[:ntok, s:s + 1]), reads=["ssq%d" % s], writes=["ssq%d" % s])
                S.op("dve", lambda e: e.scalar_tensor_tensor(
                    out=hb[:ntok, s, :], in0=xbuf[sl][:ntok, s, :], scalar=ssq[:ntok, s:s + 1], in1=gmix[:ntok],
                    op0=ALU.mult, op1=ALU.mult), reads=["x%d_%d" % (sl, s), "ssq%d" % s, "gmix"], writes=["hb%d" % s])

            def front_dve(mi, s):
                front_a(mi, s)
                front_b(mi, s)

            def front_pe(mi, s):
                t0, subs, is_meta = mts[mi]
                sl = mi % 2
                ntok = subs[s]
                for c in range(8):
                    S.op("pe", lambda e: e.transpose(
                        out=psb(4)[:, c * 128:c * 128 + ntok], in_=hb[:ntok, s, c * 128:(c + 1) * 128],
                        identity=ident[:ntok, :ntok]), reads=["hb%d" % s, "ident"], writes=["ps4"])
                S.op("dve", lambda e: e.tensor_copy(
                    out=hT[sl][:, :, s * 128:s * 128 + ntok],
                    in_=psb(4).rearrange("p (c t) -> p c t", c=8)[:, :, 0:ntok]),
                    reads=["ps4"], writes=["hT%d_%d" % (sl, s)])

            def front(mi):
                for s in range(len(mts[mi][1])):
                    front_dve(mi, s)
                    front_pe(mi, s)

            def chain_a(ntok, so, kb, is_meta):
                z = zc[so]
                S.op("dve", lambda e: e.tensor_tensor(out=sq[:ntok], in0=z[:ntok], in1=z[:ntok], op=ALU.mult),
                     reads=["zc%d" % so], writes=["sq"])
                S.op("dve", lambda e: e.tensor_reduce(out=ssh[:ntok, 0:10], in_=sq[:ntok].rearrange("p (h d) -> p h d", d=64),
                                                      axis=AX.X, op=ALU.add), reads=["sq"], writes=["ssh"])
                S.op("act", lambda e: e.activation(out=ssh[:ntok, 0:10], in_=ssh[:ntok, 0:10], func=AF.Sqrt,
                                                   bias=eps_t[:ntok, 0:1], scale=1.0 / 64), reads=["ssh", "eps"], writes=["ssh"])

            def chain_b(ntok, so, kb, is_meta):
                z = zc[so]
                S.op("dve", lambda e: e.reciprocal(out=ssh[:ntok, 0:10], in_=ssh[:ntok, 0:10]), reads=["ssh"], writes=["ssh"])
                S.op("dve", lambda e: e.tensor_tensor(
                    out=qn[:ntok].rearrange("p (h d) -> p h d", d=64), in0=z[:ntok].rearrange("p (h d) -> p h d", d=64),
                    in1=ssh[:ntok, 0:10].unsqueeze(2).broadcast_to([ntok, 10, 64]), op=ALU.mult),
                    reads=["zc%d" % so, "ssh"], writes=["qn"])
                S.op("dve", lambda e: e.tensor_tensor(out=qn[:ntok], in0=qn[:ntok], in1=gqk[:ntok], op=ALU.mult),
                     reads=["qn", "gqk", "gqk2"], writes=["qn"])
                S.op("dve", lambda e: e.tensor_tensor(
                    out=t1b[:ntok].rearrange("p (h d) -> p h d", d=64), in0=qn[:ntok].rearrange("p (h d) -> p h d", d=64),
                    in1=rt[so][:ntok, 0:64].unsqueeze(1).broadcast_to([ntok, 10, 64]), op=ALU.mult),
                    reads=["qn", "rt%d" % so], writes=["t1b"])
                for hf in range(2):
                    S.op("dve", lambda e: e.tensor_tensor(
                        out=t2b[:ntok].rearrange("p (h a b) -> p h a b", a=2, b=32)[:, :, :, hf * 16:hf * 16 + 16],
                        in0=qn[:ntok].rearrange("p (h a b) -> p h a b", a=2, b=32)[:, :, :, (1 - hf) * 16:(1 - hf) * 16 + 16],
                        in1=rt[so][:ntok, 64:128].rearrange("p (a b) -> p a b", a=2)[:, :, hf * 16:hf * 16 + 16]
                        .unsqueeze(1).broadcast_to([ntok, 10, 2, 16]), op=ALU.mult),
                        reads=["qn", "rt%d" % so], writes=["t2b%d" % hf])
                S.op("dve", lambda e: e.tensor_tensor(out=qko[so][:ntok], in0=t1b[:ntok], in1=t2b[:ntok], op=ALU.add),
                     reads=["t1b", "t2b0", "t2b1"], writes=["qko%d" % so])

            def chain(ntok, so, kb, is_meta):
                chain_a(ntok, so, kb, is_meta)
                chain_b(ntok, so, kb, is_meta)

            def chain_pe(ntok, so, kb, is_meta):
                for j in range(5):
                    S.op("pe", lambda e: e.transpose(
                        out=psb(5)[:, j * 128:j * 128 + ntok], in_=qko[so][:ntok, j * 128:(j + 1) * 128],
                        identity=ident[:ntok, :ntok]), reads=["qko%d" % so, "ident"], writes=["ps5"])
                S.op("dve", lambda e: e.tensor_copy(out=KT[:, kb * 128:kb * 128 + ntok], in_=psb(5)[:, 512:512 + ntok]),
                     reads=["ps5"], writes=["KT%d" % kb])
                if not is_meta:
                    S.op("dve", lambda e: e.tensor_copy(out=qTs[so], in_=psb(5)[:, 0:512]), reads=["ps5"], writes=["qTs%d" % so])
                    dma("sp", qt_s[kb, :, :], qTs[so], reads=["qTs%d" % so], sem="qTs%d" % so)

            load_x(0)
            front(0)
            sub_ctr = 0
            gate_ctr = 0
            pending = None
            pending2 = None
            for mi, (t0, subs, is_meta) in enumerate(mts):
                sl = mi % 2
                if mi + 1 < len(mts):
                    load_x(mi + 1)
                ns = len(subs)
                for s, ntok in enumerate(subs):
                    kb = (t0 // 128) + s
                    so = sub_ctr % 2
                    sub_ctr += 1
                    dma("sp", rt[so][:, :], rope_d[n][kb * 128:(kb + 1) * 128, :], writes=["rt%d" % so], sem="rt%d" % so)
                    for j, (c0, c1) in enumerate([(0, 512), (512, 1024), (1024, 1536), (1536, 1792)]):
                        for c in range(8):
                            S.op("pe", lambda e: e.matmul(
                                ps[:ntok, j, 0:c1 - c0], lhsT=hT[sl][:, c, s * 128:s * 128 + ntok], rhs=Win[:, c, c0:c1],
                                start=(c == 0), stop=(c == 7)),
                                reads=["hT%d_%d" % (sl, s)] + WINK, writes=["ps%d" % j])
                    S.op("dve", lambda e: e.tensor_copy(out=zc[so][:ntok], in_=psflat[:ntok, 0:640]),
                         reads=["ps0", "ps1"], writes=["zc%d" % so])
                    S.op("dve", lambda e: e.tensor_copy(out=Vt[:ntok, kb, :], in_=psflat[:ntok, 640:768]),
                         reads=["ps1"], writes=["Vt%d" % kb])
                    S.op("act", lambda e: e.activation(out=zfs[so][:ntok], in_=psflat[:ntok, 768:1792], func=AF.Copy),
                         reads=["ps1", "ps2", "ps3"], writes=["zfs%d" % so])
                    p0 = 0 if is_meta else 16 + t0 + s * 128
                    dma("sp", zf[p0:p0 + ntok, :], zfs[so][:ntok], reads=["zfs%d" % so], sem="zfs%d" % so)
                    has_next = mi + 1 < len(mts) and s < len(mts[mi + 1][1])
                    if pending is not None:
                        chain_a(*pending)
                    if has_next:
                        front_a(mi + 1, s)
                    if not is_meta:
                        for jj in range(4):
                            j = s * 4 + jj
                            gb = 6 + (gate_ctr % 2)
                            gsl = (gate_ctr // 4) % 2
                            gate_ctr += 1
                            for c in range(8):
                                S.op("pe", lambda e: e.matmul(
                                    ps[:, gb, :], lhsT=Win[:, c, C_G + j * 128:C_G + (j + 1) * 128], rhs=hT[sl][:, c, :],
                                    start=(c == 0), stop=(c == 7)),
                                    reads=["hT%d_%d" % (sl, q) for q in range(4)] + WINK, writes=["ps%d" % gb])
                            S.op("act", lambda e: e.activation(out=gst[gsl][:, jj, :], in_=ps[:, gb, :], func=AF.Sigmoid),
                                 reads=["ps%d" % gb], writes=["gst%d_%d" % (gsl, jj)])
                            if jj == 3:
                                dma("sp", g_s[:, j - 3:j + 1, t0:t0 + 512], gst[gsl][:, :, :],
                                    reads=["gst%d_%d" % (gsl, q) for q in range(4)], sem="gst%d" % gsl)
                    if has_next:
                        front_b(mi + 1, s)
                    if pending is not None:
                        chain_b(*pending)
                    if pending2 is not None:
                        chain_pe(*pending2)
                    pending2 = pending
                    pending = (ntok, so, kb, is_meta)
                    if has_next:
                        front_pe(mi + 1, s)
            if pending2 is not None:
                chain_pe(*pending2)
            chain(*pending)
            chain_pe(*pending)
            S.fence()

            AR.top = B_MARK
            qT = [AR.alloc([512], BF16) for _ in range(2)]
            pT = [AR.alloc([2, 512], BF16) for _ in range(4)]
            psum2 = [AR.alloc([2, 512], BF16) for _ in range(2)]
            rec = AR.alloc([512], F32)
            osb = AR.alloc([512], F32)
            oT = [AR.alloc([512], BF16) for _ in range(2)]
            dma("sp", qT[0], qt_s[0, :, :], writes=["qT0"], sem="qT0")
            b_top = AR.top
            AR.top = P_MARK
            T1 = AR.alloc([N2, 3, N1], BF16)
            assert AR.top <= P_MARK + T1_WORDS
            AR.top = b_top
            dma("sp", T1[:N1], t1_d[n][:, :, :, :], sem="c7")
            if si == nseq - 1:
                for c in range(8):
                    dma("pool", w1_s[c * 128:(c + 1) * 128, :], w_fi[c * 128:(c + 1) * 128, :], sem="wc1")
                for jq in range(NJ):
                    dma("pool", w2_s[jq * 128:(jq + 1) * 128, :], w_f2[jq * 128:(jq + 1) * 128, :], sem="wc2")
            its = [(qb, kb) for qb in range(nqb) for kb in range(nkb)]

            def emit_qk(i):
                qb, kb = its[i]
                nk = 128 if kb < nkb - 1 else 16
                sb = 2 * (i % 3)
                qs = qb % 2
                for h in range(2):
                    S.op("pe", lambda e, h=h, nk=nk, sb=sb, qs=qs, kb=kb: e.matmul(
                        ps[:nk, sb + h, :], lhsT=KT[h * 64:(h + 1) * 64, kb * 128:kb * 128 + nk],
                        rhs=qT[qs][h * 64:(h + 1) * 64, :], start=True, stop=True),
                        reads=["qT%d" % qs], writes=["ps%d" % (sb + h)])

            norm_q = []

            def run_norm(cnt):
                for _ in range(cnt):
                    o_, qb_, qq = norm_q.pop(0)
                    cs = slice(qq * 128, (qq + 1) * 128)
                    S.op("dve", lambda e: e.reciprocal(out=rec[:, cs], in_=rec[:, cs]), reads=["rec"], writes=["rec%d" % qq])
                    S.op("dve", lambda e: e.tensor_tensor(out=oT[o_][:, cs], in0=osb[:, cs], in1=rec[:, cs], op=ALU.mult),
                         reads=["osb", "rec%d" % qq], writes=["oT%d_%d" % (o_, qq)])
                    if qq == 3:
                        dma("sp", ot_s[qb_, :, :], oT[o_], reads=["oT%d_%d" % (o_, q) for q in range(4)], sem="oT%d" % o_)

            emit_qk(0)
            if len(its) > 1:
                emit_qk(1)
            pend_sum = None
            for i, (qb, kb) in enumerate(its):
                nk = 128 if kb < nkb - 1 else 16
                sb = 2 * (i % 3)
                pslot = i % 4
                ob = 6
                if kb == 0 and qb + 1 < nqb:
                    dma("sp", qT[(qb + 1) % 2], qt_s[qb + 1, :, :], writes=["qT%d" % ((qb + 1) % 2)], sem="qT%d" % ((qb + 1) % 2))
                if i + 2 < len(its):
                    emit_qk(i + 2)
                S.op("act", lambda e: e.activation(
                    out=pT[pslot][:nk], in_=ps[:nk, sb:sb + 2, :], func=AF.Exp, bias=EXP_SHIFT, scale=0.125),
                    reads=["ps%d" % sb, "ps%d" % (sb + 1)], writes=["pT%d" % pslot])
                first, last = (kb == 0), (kb == nkb - 1)
                for h in range(2):
                    S.op("pe", lambda e: e.matmul(
                        ps[h * 64:(h + 1) * 64, ob, :], lhsT=Vt[:nk, kb, h * 64:(h + 1) * 64], rhs=pT[pslot][:nk, h, :],
                        start=first, stop=last, tile_position=(0, h * 64)),
                        reads=["pT%d" % pslot], writes=["ps%d_%d" % (ob, h)])
                if pend_sum is not None:
                    pp2, pst = pend_sum
                    pend_sum = None
                    for h in range(2):
                        S.op("pe", lambda e: e.matmul(
                            ps[h * 64:(h + 1) * 64, ob + 1, :], lhsT=ones64[:, :], rhs=psum2[pp2][:, h, :],
                            start=pst, stop=False, tile_position=(0, h * 64)),
                            reads=["psum2_%d" % pp2], writes=["ps%d_%d" % (ob + 1, h)])
                if last:
                    for h in range(2):
                        S.op("pe", lambda e: e.matmul(
                            ps[h * 64:(h + 1) * 64, ob + 1, :], lhsT=ones64[:nk, :], rhs=pT[pslot][:nk, h, :],
                            start=False, stop=True, tile_position=(0, h * 64)),
                            reads=["pT%d" % pslot], writes=["ps%d_%d" % (ob + 1, h)])
                elif kb % 2 == 1:
                    p2 = (kb // 2) % 2
                    pprev = (i - 1) % 4
                    S.op("dve", lambda e: e.tensor_tensor(out=psum2[p2], in0=pT[pprev], in1=pT[pslot], op=ALU.add),
                         reads=["pT%d" % pprev, "pT%d" % pslot], writes=["psum2_%d" % p2])
                    pend_sum = (p2, kb == 1)
                if norm_q and (last or (kb >= 2 and kb % 2 == 0)):
                    run_norm(len(norm_q) if last else 1)
                if last:
                    os_ = qb % 2
                    S.op("dve", lambda e: e.tensor_copy(out=rec, in_=ps[:, ob + 1, :]),
                         reads=["ps%d_0" % (ob + 1), "ps%d_1" % (ob + 1)], writes=["rec"])
                    S.op("dve", lambda e: e.tensor_copy(out=osb, in_=ps[:, ob, :]),
                         reads=["ps%d_0" % ob, "ps%d_1" % ob], writes=["osb"])
                    for qq in range(4):
                        norm_q.append((os_, qb, qq))
            run_norm(len(norm_q))
            S.fence()

            AR.top = P_MARK + T1_WORDS
            YT = AR.alloc([4, L], BF16)
            Wao = AR.alloc([4, D], BF16)
            Wfo = AR.alloc([4, D], BF16)
            Wo = AR.alloc([8, D], BF16)
            C_MARK = AR.top
            for h in range(2):
                dma("pool", Wao[h * 64:(h + 1) * 64], w_ao[h * 256:(h + 1) * 256, :].rearrange("(g d) o -> d g o", d=64),
                    writes=["Wao"], sem="w3")
            dma("pool", Wfo, w_fo.rearrange("(g c) o -> c g o", c=128), writes=["Wfo"], sem="w4")
            dma("pool", Wo, w_out.rearrange("(c p) o -> p c o", p=128), writes=["Wo"], sem="w5")
            NB = 4
            T2 = AR.alloc([2, N2], BF16)
            NZ = 4
            zin = [AR.alloc([NB, 1024], BF16) for _ in range(NZ)]
            bst = [AR.alloc([NB, 2, 512], BF16) for _ in range(3)]
            dma("sp", T2[:N2], t2_d[n][:, :, :], writes=["T2"], sem="c8")
            zf3 = zf.rearrange("(a b) c -> a b c", b=N2)
            grp = [(a, min(NB, N2 - a)) for a in range(0, N2, NB)]

            def load_z(gi):
                a, nb = grp[gi]
                dma("sp", zin[gi % NZ][:N1, 0:nb, :], zf3[:, a:a + nb, :], writes=["zin%d" % (gi % NZ)], sem="zin%d" % (gi % NZ))

            for gi in range(min(NZ - 1, len(grp))):
                load_z(gi)
            ctr = 0
            for gi, (a, nb) in enumerate(grp):
                if gi + NZ - 1 < len(grp):
                    load_z(gi + NZ - 1)
                zs = gi % NZ
                bs_ = gi % 3
                for j in range(nb):
                    n2 = a + j
                    br = 2 * (ctr % 2)
                    ctr += 1
                    zr = zin[zs][:N1, j, 0:512]
                    zi = zin[zs][:N1, j, 512:1024]
                    for (bank, m0, r0, m1, r1) in [(br, 0, zr, 1, zi), (br + 1, 2, zr, 0, zi)]:
                        S.op("pe", lambda e, bank=bank, m0=m0, r0=r0, n2=n2: e.matmul(
                            ps[:N1, bank, :], lhsT=T1[:N1, n2, m0, :], rhs=r0, start=True, stop=False),
                            reads=["T1", "zin%d" % zs], writes=["ps%d" % bank])
                        S.op("pe", lambda e, bank=bank, m1=m1, r1=r1, n2=n2: e.matmul(
                            ps[:N1, bank, :], lhsT=T1[:N1, n2, m1, :], rhs=r1, start=False, stop=True),
                            reads=["T1", "zin%d" % zs], writes=["ps%d" % bank])
                    if ctr % 2 == 0:
                        S.op("dve", lambda e: e.tensor_copy(out=bst[bs_][:N1, j, :, :], in_=ps[:N1, br:br + 2, :]),
                             reads=["ps%d" % br, "ps%d" % (br + 1)], writes=["bst%d_%d" % (bs_, j)])
                    else:
                        S.op("act", lambda e: e.activation(out=bst[bs_][:N1, j, :, :], in_=ps[:N1, br:br + 2, :], func=AF.Copy),
                             reads=["ps%d" % br, "ps%d" % (br + 1)], writes=["bst%d_%d" % (bs_, j)])
                dma("sp", bs[:, a:a + nb, :, :], bst[bs_][:N1, 0:nb, :, :], reads=["bst%d_%d" % (bs_, j) for j in range(nb)],
                    sem="bst%d" % bs_)
            S.wait_dma("sp", [sem_cache["bst%d" % q] for q in range(3) if ("bst%d" % q) in sem_cache])
            f_top = AR.top
            AR.top = P_MARK
            oin = [AR.alloc([4, 512], BF16) for _ in range(2)]
            gins = [AR.alloc([16, 512], BF16) for _ in range(2)]
            assert AR.top <= P_MARK + T1_WORDS
            AR.top = f_top

            def load_c1(m, extra=()):
                dma("sp", oin[m % 2], ot_s[4 * m:4 * m + 4].rearrange("q p f -> p q f"), writes=["oin%d" % (m % 2)] + list(extra),
                    sem="oin%d" % (m % 2))
                dma("sp", gins[m % 2], g_s[:, :, m * 512:(m + 1) * 512], writes=["gin%d" % (m % 2)] + list(extra), sem="gin%d" % (m % 2))

            load_c1(0, extra=["T1"])
            bin_ = zin
            grp2 = [(a, min(NB, N1 - a)) for a in range(0, N1, NB)]

            def load_b(gi):
                a, nb = grp2[gi]
                dma("sp", bin_[gi % NZ][:N2, 0:nb, :], bs[a:a + nb].rearrange("k n r c -> n k (r c)"),
                    writes=["zin%d" % (gi % NZ)], sem="zin%d" % (gi % NZ))

            for gi in range(min(NZ - 1, len(grp2))):
                load_b(gi)
            ctr = 0
            for gi, (a, nb) in enumerate(grp2):
                if gi + NZ - 1 < len(grp2):
                    load_b(gi + NZ - 1)
                zs = gi % NZ
                for j in range(nb):
                    k1 = a + j
                    bank = ctr % 2
                    ctr += 1
                    for g in range(4):
                        for r in range(2):
                            S.op("pe", lambda e, g=g, r=r, j=j, zs=zs, bank=bank: e.matmul(
                                ps[:, bank, g * N2:(g + 1) * N2], lhsT=bin_[zs][:N2, j, r * 512 + g * 128:r * 512 + (g + 1) * 128],
                                rhs=T2[:N2, r, :], start=(r == 0), stop=(r == 1)),
                                reads=["T2", "zin%d" % zs], writes=["ps%d" % bank])
                    if ctr % 2 == 0:
                        S.op("dve", lambda e: e.tensor_copy(
                            out=YT[:, :, k1:k1 + N1 * (N2 - 1) + 1:N1],
                            in_=ps[:, bank, 0:4 * N2].rearrange("p (g k) -> p g k", g=4)),
                            reads=["ps%d" % bank], writes=["YT%d" % (ctr % 2)])
                    else:
                        S.op("act", lambda e: e.activation(
                            out=YT[:, :, k1:k1 + N1 * (N2 - 1) + 1:N1],
                            in_=ps[:, bank, 0:4 * N2].rearrange("p (g k) -> p g k", g=4), func=AF.Copy),
                            reads=["ps%d" % bank], writes=["YT%d" % (ctr % 2)])
            S.fence()

            AR.top = C_MARK
            ta = [AR.alloc([512], F32) for _ in range(2)]
            tb = [AR.alloc([512], F32) for _ in range(2)]
            mg = [AR.alloc([8, 512], BF16) for _ in range(2)]
            xin = [AR.alloc([D], F32) for _ in range(3)]
            x1o = [AR.alloc([D], F32) for _ in range(2)]
            xctr = 0

            def c1_merge(m):
                ms = m % 2
                t0 = m * 512
                gin = gins[ms]
                gk = "gin%d" % ms
                if m + 1 < nmt:
                    load_c1(m + 1)
                for oc in range(8):
                    ba = (oc % 2)
                    bf = 2 + (oc % 2)
                    for g in range(4):
                        S.op("pe", lambda e: e.matmul(
                            ps[:, ba, :], lhsT=Wao[:, g, oc * 128:(oc + 1) * 128],
                            rhs=oin[ms].rearrange("p q (g t) -> p q g t", g=4)[:, :, g, :],
                            start=(g == 0), stop=(g == 3)), reads=["Wao", "oin%d" % ms], writes=["ps%d" % ba])
                    for g in range(4):
                        S.op("pe", lambda e: e.matmul(
                            ps[:, bf, :], lhsT=Wfo[:, g, oc * 128:(oc + 1) * 128], rhs=YT[:, g, 16 + t0:16 + t0 + 512],
                            start=(g == 0), stop=(g == 3)), reads=["Wfo", "YT"], writes=["ps%d" % bf])
                    o2 = oc % 2
                    S.op("dve", lambda e: e.tensor_tensor(out=ta[o2], in0=ps[:, ba, :], in1=gin[:, oc, :], op=ALU.mult),
                         reads=["ps%d" % ba, gk], writes=["ta%d" % o2])
                    S.op("dve", lambda e: e.tensor_tensor(out=tb[o2], in0=ps[:, bf, :], in1=gin[:, 8 + oc, :], op=ALU.mult),
                         reads=["ps%d" % bf, gk], writes=["tb%d" % o2])
                    S.op("pool", lambda e: e.tensor_tensor(out=mg[ms][:, oc, :], in0=ta[o2], in1=tb[o2], op=ALU.add),
                         reads=["ta%d" % o2, "tb%d" % o2], writes=["mg%d_%d" % (ms, oc)])

            c1_merge(0)
            for m in range(nmt):
                ms = m % 2
                t0 = m * 512
                if m + 1 < nmt:
                    c1_merge(m + 1)
                for s in range(4):
                    xsl = xctr % 3
                    osl = xctr % 2
                    xctr += 1
                    tt = t0 + s * 128
                    dma("sp", xin[xsl], x_d[tt:tt + 128, :], writes=["xin%d" % xsl], sem="xin%d" % xsl)
                    pb = 4 + 2 * (s % 2)
                    for hh in range(2):
                        for oc in range(8):
                            S.op("pe", lambda e: e.matmul(
                                ps[:, pb + hh, :], lhsT=mg[ms][:, oc, s * 128:(s + 1) * 128], rhs=Wo[:, oc, hh * 512:(hh + 1) * 512],
                                start=(oc == 0), stop=(oc == 7)), reads=["mg%d_%d" % (ms, oc), "Wo"], writes=["ps%d" % (pb + hh)])
                    S.op("dve", lambda e: e.tensor_tensor(
                        out=x1o[osl], in0=xin[xsl], in1=psflat[:, pb * 512:(pb + 2) * 512], op=ALU.add),
                        reads=["xin%d" % xsl, "ps%d" % pb, "ps%d" % (pb + 1)], writes=["x1o%d" % osl])
                    dma("sp", x1_s[si][tt:tt + 128, :], x1o[osl], reads=["x1o%d" % osl], sem="x1o%d" % osl)
            S.fence()

        AR.top = P_MARK
        W1 = AR.alloc([8, 2 * DFF], BF16)
        W2 = AR.alloc([NJ, D], BF16)
        gfin = AR.alloc([D], F32)
        gffn = AR.alloc([D], F32)
        xn = [AR.alloc([D], F32) for _ in range(4)]
        xr = [AR.alloc([D], F32) for _ in range(2)]
        ssq = AR.alloc([8], F32)
        hb2 = [AR.alloc([D], BF16) for _ in range(2)]
        hT2 = AR.alloc([8, 512], BF16)
        sg = [AR.alloc([512], BF16) for _ in range(2)]
        actT = AR.alloc([NJ, 512], BF16)
        yo = AR.alloc([D], F32)
        dma("sp", gffn, bcast_row(g_ffn_t, D), writes=["gffn"], sem="c6")
        dma("sp", gfin, bcast_row(g_fin_t, D), writes=["gfin"], sem="c7")

        tiles = [(si, m) for si, n in enumerate(seq_ns) for m in range(n // 512)]

        def load_xn(ti):
            si, m = tiles[ti]
            for s in range(4):
                tt = m * 512 + s * 128
                dma("sp", xn[s], x1_s[si][tt:tt + 128, :], writes=["xn%d" % s], sem="xn%d" % s)

        def front2_dve(s):
            hs = s % 2
            rms_rstd(xn[s][:, :], 128, hb2[hs], ssq, s, "xn%d" % s, "hb2_%d" % hs, "ssqn%d" % s)
            sqrt_recip(ssq[:, s:s + 1], 1.0 / D, ["ssqn%d" % s])
            S.op("dve", lambda e: e.scalar_tensor_tensor(out=hb2[hs], in0=xn[s], scalar=ssq[:, s:s + 1], in1=gffn, op0=ALU.mult, op1=ALU.mult),
                 reads=["xn%d" % s, "ssqn%d" % s, "gffn"], writes=["hb2_%d" % hs])

        def front2_pe(s):
            hs = s % 2
            for c in range(8):
                S.op("pe", lambda e: e.transpose(out=psb(7)[:, c * 128:(c + 1) * 128], in_=hb2[hs][:, c * 128:(c + 1) * 128],
                                                 identity=ident), reads=["hb2_%d" % hs, "ident"], writes=["ps7"])
            S.op("dve", lambda e: e.tensor_copy(out=hT2[:, :, s * 128:(s + 1) * 128], in_=psb(7).rearrange("p (c t) -> p c t", c=8)),
                 reads=["ps7"], writes=["hT2_%d" % s])

        load_xn(0)
        NBLK = (NJ + 3) // 4
        w1v = w1_s.rearrange("(c p) f -> p c f", p=128)
        for bk in range(NBLK):
            j0, j1 = bk * 4, min(NJ, bk * 4 + 4)
            for half in range(2):
                c0 = half * DFF + j0 * 128
                c1 = half * DFF + j1 * 128
                dma("sp", W1[:, :, c0:c1], w1v[:, :, c0:c1], writes=["W1b_%d" % bk], sem="w6_%d" % bk)
        for jq in range(0, NJ, 2):
            dma("sp", W2[:, jq:jq + 2, :], w2_s[jq * 128:(jq + 2) * 128, :].rearrange("(c p) o -> p c o", p=128),
                writes=["W2"], sem="w7")
        for s in range(4):
            front2_dve(s)
            front2_pe(s)
        fctr = 0
        octr = 0
        rctr = 0
        hkeys = ["hT2_%d" % s for s in range(4)]
        for ti, (si, m) in enumerate(tiles):
            t0 = m * 512
            x1d = x1_s[si]
            if ti + 1 < len(tiles):
                load_xn(ti + 1)
            for j in range(NJ):
                bg = 2 * (fctr % 2)
                ss_ = fctr % 2
                fctr += 1
                for (bank, col0) in [(bg, j * 128), (bg + 1, DFF + j * 128)]:
                    for c in range(8):
                        S.op("pe", lambda e: e.matmul(ps[:, bank, :], lhsT=W1[:, c, col0:col0 + 128], rhs=hT2[:, c, :],
                                                      start=(c == 0), stop=(c == 7)),
                             reads=hkeys + ["W1b_%d" % (j // 4)], writes=["ps%d" % bank])
                S.op("act", lambda e: e.activation(out=sg[ss_], in_=ps[:, bg, :], func=AF.Silu), reads=["ps%d" % bg], writes=["sg%d" % ss_])
                S.op("dve", lambda e: e.tensor_tensor(out=actT[:, j, :], in0=ps[:, bg + 1, :], in1=sg[ss_], op=ALU.mult),
                     reads=["ps%d" % (bg + 1), "sg%d" % ss_], writes=["actT%d" % j])
            for s in range(4):
                tt = t0 + s * 128
                rs = rctr % 2
                rctr += 1
                dma("sp", xr[rs], x1d[tt:tt + 128, :], writes=["xr%d" % rs], sem="xr%d" % rs)
                if ti + 1 < len(tiles):
                    front2_dve(s)
                banks = []
                for hh in range(2):
                    bank = 4 + (octr % 3)
                    octr += 1
                    banks.append(bank)
                    for j in range(NJ):
                        S.op("pe", lambda e: e.matmul(ps[:, bank, :], lhsT=actT[:, j, s * 128:(s + 1) * 128],
                                                      rhs=W2[:, j, hh * 512:(hh + 1) * 512], start=(j == 0), stop=(j == NJ - 1)),
                             reads=["actT%d" % j, "W2"], writes=["ps%d" % bank])
                if ti + 1 < len(tiles):
                    front2_pe(s)
                for hh in range(2):
                    S.op("dve", lambda e: e.tensor_tensor(out=xr[rs][:, hh * 512:(hh + 1) * 512], in0=xr[rs][:, hh * 512:(hh + 1) * 512],
                                                          in1=ps[:, banks[hh], :], op=ALU.add),
                         reads=["xr%d" % rs, "ps%d" % banks[hh]], writes=["xr%d" % rs])
                rms_rstd(xr[rs][:, :], 128, yo, ssq, 4, "xr%d" % rs, "yo", "ssq4")
                sqrt_recip(ssq[:, 4:5], 1.0 / D, ["ssq4"])
                S.op("dve", lambda e: e.scalar_tensor_tensor(out=yo, in0=xr[rs], scalar=ssq[:, 4:5], in1=gfin, op0=ALU.mult, op1=ALU.mult),
                     reads=["xr%d" % rs, "ssq4", "gfin"], writes=["yo"])
                dma("sp", ys[si][tt:tt + 128, :], yo, reads=["yo"], sem="yo")
        S.fence()
        S.emit(block)
    return nc


SEQ_NS = [2048, 2048, 8192]
_CACHE = {}


def _get_program(seq_ns):
    key = tuple(seq_ns)
    if key not in _CACHE:
        _CACHE[key] = (build(list(seq_ns)), host_consts(list(seq_ns)))
    return _CACHE[key]


def run_cores(seq_ns, per_core_x, weights):
    nc, consts = _get_program(seq_ns)
    in_maps = []
    for xl in per_core_x:
        m = dict(consts)
        m.update(weights)
        for i, x in enumerate(xl):
            m["x%d" % i] = np.ascontiguousarray(x, dtype=np.float32)
        in_maps.append(m)
    res = run_bass_kernel_spmd(nc, in_maps, core_ids=list(range(len(in_maps))))
    return [[r["y%d" % i] for i in range(len(seq_ns))] for r in res.results]


def kernel(x_prompt, x_sample, meta_tokens, norm_mix_g, w_in, q_norm_g, k_norm_g, w_attn_o, w_four_o,
           w_out, norm_ffn_g, w_ffn_in, w_ffn_out, final_norm_g):
    f = lambda a: np.ascontiguousarray(np.asarray(a), dtype=np.float32)
    weights = {
        "meta_tokens": f(meta_tokens), "norm_mix_g": f(norm_mix_g).reshape(-1), "q_norm_g": f(q_norm_g).reshape(-1),
        "k_norm_g": f(k_norm_g).reshape(-1), "norm_ffn_g": f(norm_ffn_g).reshape(-1), "final_norm_g": f(final_norm_g).reshape(-1),
        "w_in": f(w_in)[0], "w_attn_o": f(w_attn_o)[0], "w_four_o": f(w_four_o)[0], "w_out": f(w_out)[0],
        "w_ffn_in": f(w_ffn_in)[0], "w_ffn_out": f(w_ffn_out)[0],
    }
    xp = np.asarray(x_prompt)
    xsm = np.asarray(x_sample)
    per_core = [[xp[2 * c], xp[2 * c + 1], xsm[c]] for c in range(8)]
    outs = run_cores(SEQ_NS, per_core, weights)
    y_prompt = np.stack([outs[c][k] for c in range(8) for k in range(2)], 0).astype(np.float32)
    y_sample = np.stack([outs[c][2] for c in range(8)], 0).astype(np.float32)
    return (y_prompt, y_sample)
```

```python
import bisect
from contextlib import ExitStack
import numpy as np
import ml_dtypes
import concourse.bass as bass
import concourse.mybir as mybir
from concourse.bass_utils import run_bass_kernel_spmd

F32 = mybir.dt.float32
BF16 = mybir.dt.bfloat16
AF = mybir.ActivationFunctionType
ALU = mybir.AluOpType
AX = mybir.AxisListType

D = 1024
DFF = 2816
NJ = DFF // 128
WIN_COLS = 3840
C_K, C_V, C_F, C_G = 512, 640, 768, 1792
FACT = {528: (24, 22), 1040: (40, 26), 2064: (86, 24), 8208: (114, 72)}
EPS = 1e-6
EXP_SHIFT = -10.0

ENGS = ["pe", "act", "dve", "pool"]
EIDX = {e: i for i, e in enumerate(ENGS)}
EPOCH = 30000


class DSem:
    def __init__(self, h, name):
        self.h = h
        self.count = 0
        self.name = name


class _Rec:
    def __init__(self):
        self.call = None

    def __getattr__(self, name):
        def f(*a, **k):
            self.call = (name, a, k)
            return self
        return f


class Sched:
    def __init__(self, nc, stack):
        self.nc = nc
        self.stack = stack
        self.streams = {e: [] for e in ENGS + ["sp"]}
        self.nops = {e: 0 for e in ENGS}
        self.snap = {e: [] for e in ENGS}
        self.signaled = {e: [] for e in ENGS}
        self.clock = {e: [-1, -1, -1, -1] for e in ENGS + ["sp"]}
        self.known_sem = {e: {} for e in ENGS + ["sp"]}
        self.res = {}
        self.dsems = []

    def dsem(self, name):
        h = self.stack.enter_context(self.nc.semaphore(name))
        s = DSem(h, name)
        self.dsems.append(s)
        return s

    def _need(self, eng, tok):
        if tok is None:
            return
        if tok[0] == "eng":
            _, E, idx = tok
            if E == "pe" and eng == "pe":
                return
            if self.clock[eng][EIDX[E]] >= idx:
                return
            sl = self.signaled[E]
            j = bisect.bisect_left(sl, idx)
            if j < len(sl) and sl[j] - idx <= (3 if E == "pe" else 0):
                tgt = sl[j]
            else:
                bisect.insort(sl, idx)
                tgt = idx
            self.streams[eng].append(("we", E, tgt))
            sn = self.snap[E][tgt]
            c = self.clock[eng]
            for i in range(4):
                if sn[i] > c[i]:
                    c[i] = sn[i]
            if c[EIDX[E]] < tgt:
                c[EIDX[E]] = tgt
        else:
            _, s, cnt = tok
            if self.known_sem[eng].get(s, 0) >= cnt:
                return
            self.streams[eng].append(("ws", s, cnt))
            self.known_sem[eng][s] = cnt

    def op(self, eng, fn, reads=(), writes=(), dsem=None):
        rec = _Rec()
        fn(rec)
        fn = rec.call
        deps = []
        for r in reads:
            st = self.res.get(r)
            if st is not None:
                deps.append(st[0])
        for w in writes:
            st = self.res.get(w)
            if st is not None:
                deps.append(st[0])
                deps.extend(st[1])
        for d in deps:
            self._need(eng, d)
        if dsem is not None:
            dsem.count += 16
            tok = ("dma", dsem, dsem.count)
            self.streams[eng].append(("dma", fn, dsem))
        else:
            idx = self.nops[eng]
            self.nops[eng] = idx + 1
            tok = ("eng", eng, idx)
            self.snap[eng].append(tuple(self.clock[eng]))
            if eng == "pe":
                self.clock[eng][0] = idx
            self.streams[eng].append(("op", fn, idx))
        for w in writes:
            self.res[w] = [tok, []]
        for r in reads:
            st = self.res.get(r)
            if st is None:
                st = self.res[r] = [None, []]
            rl = st[1]
            for i, t in enumerate(rl):
                if t[0] == tok[0] and t[1] == tok[1]:
                    rl[i] = tok
                    break
            else:
                rl.append(tok)
        return tok

    def wait_dma(self, eng, sems):
        for s_ in sems:
            if s_.count:
                self._need(eng, ("dma", s_, s_.count))

    def fence(self):
        toks = [("eng", E, self.nops[E] - 1) for E in ENGS if self.nops[E] > 0]
        toks += [("dma", s, s.count) for s in self.dsems if s.count]
        for eng in ENGS + ["sp"]:
            for t in toks:
                self._need(eng, t)
        self.res = {}

    def emit(self, block):
        nc = self.nc
        esem = {}
        for e in ENGS:
            self.signaled[e].sort()
            n = len(self.signaled[e])
            ne = max(1, (n + EPOCH - 1) // EPOCH)
            esem[e] = [self.stack.enter_context(nc.semaphore("es_%s_%d" % (e, k))) for k in range(ne)]
        rank = {e: {idx: i for i, idx in enumerate(self.signaled[e])} for e in ENGS}

        def run(eng, eo):
            for ent in self.streams[eng]:
                k = ent[0]
                if k == "we":
                    r = rank[ent[1]][ent[2]]
                    eo.wait_ge(esem[ent[1]][r // EPOCH], r % EPOCH + 1)
                elif k == "ws":
                    eo.wait_ge(ent[1].h, ent[2])
                elif k == "dma":
                    nm, a, kw = ent[1]
                    getattr(eo, nm)(*a, **kw).then_inc(ent[2].h, 16)
                else:
                    nm, a, kw = ent[1]
                    ins = getattr(eo, nm)(*a, **kw)
                    r = rank[eng].get(ent[2])
                    if r is not None:
                        ins.then_inc(esem[eng][r // EPOCH], 1)

        @block.tensor
        def _(t):
            run("pe", t)

        @block.scalar
        def _(t):
            run("act", t)

        @block.vector
        def _(t):
            run("dve", t)

        @block.gpsimd
        def _(t):
            run("pool", t)

        @block.sync
        def _(t):
            run("sp", t)


class Arena:
    def __init__(self, base, nwords):
        self.base = base
        self.n = nwords
        self.top = 0

    def alloc(self, shape, dtype):
        nel = int(np.prod(shape))
        words = nel if dtype == F32 else (nel + 1) // 2
        a = self.base[:, self.top:self.top + words]
        self.top += words
        assert self.top <= self.n, ("SBUF arena overflow", self.top, self.n)
        if dtype != F32:
            a = a.bitcast(dtype)[:, 0:nel]
        if len(shape) == 2:
            a = a.rearrange("p (a b) -> p a b", a=shape[0], b=shape[1])
        elif len(shape) == 3:
            a = a.rearrange("p (a b c) -> p a b c", a=shape[0], b=shape[1], c=shape[2])
        return a


def host_consts(seq_ns):
    c = {}
    c["ident"] = np.eye(128, dtype=np.float32).astype(ml_dtypes.bfloat16)
    c["identf"] = np.eye(128, dtype=np.float32)
    k = np.arange(128, dtype=np.float64)
    ang = 2 * np.pi * np.outer(k, k) / 128.0
    c["cs128"] = (np.concatenate([np.cos(ang), -np.sin(ang)], 1) / np.sqrt(128.0)).astype(np.float32)
    inv_freq = 1.0 / (10000.0 ** (np.arange(0, 32, 2, dtype=np.float64) / 32.0))
    for n in sorted(set(seq_ns)):
        L = n + 16
        N1, N2 = FACT[L]
        nkb = n // 128 + 1
        t = np.arange(n)
        row = np.concatenate([t // 64, np.full(16, -1)]).astype(np.float64)
        col = np.concatenate([t % 64, np.arange(16)]).astype(np.float64)
        ar = row[:, None] * inv_freq[None]
        ac = col[:, None] * inv_freq[None]
        tab = np.zeros((nkb * 128, 128), np.float32)
        tab[:L, 0:64] = np.concatenate([np.cos(ar), np.cos(ar), np.cos(ac), np.cos(ac)], 1)
        tab[:L, 64:128] = np.concatenate([-np.sin(ar), np.sin(ar), -np.sin(ac), np.sin(ac)], 1)
        c["rope%d" % n] = tab
        n1 = np.arange(N1, dtype=np.float64)
        n2 = np.arange(N2, dtype=np.float64)
        a1 = 2 * np.pi * (n1[:, None, None] * n1[None, None, :] / N1 + n2[None, :, None] * n1[None, None, :] / L)
        t1 = np.stack([np.cos(a1), np.sin(a1), -np.sin(a1)], 2)
        c["t1_%d" % n] = t1.astype(np.float32).astype(ml_dtypes.bfloat16)
        a2 = 2 * np.pi * np.outer(n2, n2) / N2
        t2 = np.stack([np.cos(a2), np.sin(a2)], 1) / np.sqrt(float(L))
        c["t2_%d" % n] = t2.astype(np.float32).astype(ml_dtypes.bfloat16)
    return c


def build(seq_ns):
    nc = bass.Bass("TRN2", target_bir_lowering=False)
    nseq = len(seq_ns)

    def din(name, shape, dt=F32):
        return nc.dram_tensor(name, list(shape), dt, kind="ExternalInput")

    xs = [din("x%d" % i, [n, D]).ap() for i, n in enumerate(seq_ns)]
    ys = [nc.dram_tensor("y%d" % i, [n, D], F32, kind="ExternalOutput").ap() for i, n in enumerate(seq_ns)]
    meta = din("meta_tokens", [16, D]).ap()
    g_mix_t = din("norm_mix_g", [D])
    g_q_t = din("q_norm_g", [64])
    g_k_t = din("k_norm_g", [64])
    g_ffn_t = din("norm_ffn_g", [D])
    g_fin_t = din("final_norm_g", [D])
    w_in = din("w_in", [D, 3328]).ap()
    w_ao = din("w_attn_o", [512, D]).ap()
    w_fo = din("w_four_o", [512, D]).ap()
    w_out = din("w_out", [D, D]).ap()
    w_fi = din("w_ffn_in", [D, 2 * DFF]).ap()
    w_f2 = din("w_ffn_out", [DFF, D]).ap()
    ident_d = din("ident", [128, 128], BF16).ap()
    identf_d = din("identf", [128, 128]).ap()
    cs128_d = din("cs128", [128, 256]).ap()
    rope_d, t1_d, t2_d = {}, {}, {}
    for n in sorted(set(seq_ns)):
        N1, N2 = FACT[n + 16]
        rope_d[n] = din("rope%d" % n, [(n // 128 + 1) * 128, 128]).ap()
        t1_d[n] = din("t1_%d" % n, [N1, N2, 3, N1], BF16).ap()
        t2_d[n] = din("t2_%d" % n, [N2, 2, N2], BF16).ap()

    nmax = max(seq_ns)
    win_s = nc.dram_tensor("win_s", [128, 8 * WIN_COLS], BF16).ap()
    zf_s = nc.dram_tensor("zf_s", [nmax + 16, 1024], BF16).ap()
    N1m, N2m = FACT[nmax + 16]
    bs_s = nc.dram_tensor("bs_s", [N1m * N2m * 1024], BF16).ap()
    qt_s = nc.dram_tensor("qt_s", [nmax // 128, 128, 512], BF16).ap()
    ot_s = nc.dram_tensor("ot_s", [nmax // 128, 128, 512], BF16).ap()
    g_s = nc.dram_tensor("g_s", [128, 16, nmax], BF16).ap()
    x1_s = [nc.dram_tensor("x1_s%d" % i, [n, D], F32).ap() for i, n in enumerate(seq_ns)]
    w1_s = nc.dram_tensor("w1_s", [D, 2 * DFF], BF16).ap()
    wao_s = nc.dram_tensor("wao_s", [512, D], BF16).ap()
    wfo_s = nc.dram_tensor("wfo_s", [512, D], BF16).ap()
    wo_s = nc.dram_tensor("wo_s", [D, D], BF16).ap()
    w2_s = nc.dram_tensor("w2_s", [DFF, D], BF16).ap()

    ARW = 53100
    with ExitStack() as st:
        arena_t = st.enter_context(nc.sbuf_tensor("arena", [128, ARW], F32))
        ps = st.enter_context(nc.psum_tensor("ps", [128, 8, 512], F32))
        block = st.enter_context(nc.Block())
        S = Sched(nc, st)
        AR = Arena(arena_t, ARW)
        psflat = ps[:, :, :].rearrange("p b f -> p (b f)")

        def psb(b):
            return ps[:, b, :].bitcast(BF16)

        sem_cache = {}
        WINK = ["win_q", "win_kv", "win_g", "win_f"]

        def dsem(name):
            if name not in sem_cache:
                sem_cache[name] = S.dsem(name)
            return sem_cache[name]

        def dma(eng, out, in_, reads=(), writes=(), sem="misc", **kw):
            S.op(eng, lambda e: e.dma_start(out=out, in_=in_, **kw), reads=reads, writes=writes, dsem=dsem(sem))

        ident = AR.alloc([128], BF16)
        ones64 = AR.alloc([64], BF16)
        eps_t = AR.alloc([1], F32)
        ebias = AR.alloc([1], F32)
        gqk = AR.alloc([640], F32)
        tmp2 = AR.alloc([2], F32)
        P_MARK = AR.top

        dma("sp", ident, ident_d[:, :], writes=["ident"], sem="c0")
        S.op("pool", lambda e: e.memset(ones64, 1.0), writes=["ones64"])
        S.op("pool", lambda e: e.memset(eps_t, EPS), writes=["eps"])
        dma("sp", gqk[:, 0:512].rearrange("p (h d) -> p h d", d=64), bass.AP(g_q_t, 0, [[0, 128], [0, 8], [1, 64]]),
            writes=["gqk"], sem="c1")
        dma("sp", gqk[:, 512:640].rearrange("p (h d) -> p h d", d=64), bass.AP(g_k_t, 0, [[0, 128], [0, 2], [1, 64]]),
            writes=["gqk2"], sem="c2")
        S.op("dve", lambda e: e.tensor_reduce(out=tmp2[:, 0:1], in_=gqk[:, 0:64], axis=AX.X, op=ALU.max,
                                              apply_absolute_value=True), reads=["gqk"], writes=["tmp2a"])
        S.op("dve", lambda e: e.tensor_reduce(out=tmp2[:, 1:2], in_=gqk[:, 512:576], axis=AX.X, op=ALU.max,
                                              apply_absolute_value=True), reads=["gqk2"], writes=["tmp2b"])
        S.op("dve", lambda e: e.tensor_tensor(out=ebias, in0=tmp2[:, 0:1], in1=tmp2[:, 1:2], op=ALU.mult),
             reads=["tmp2a", "tmp2b"], writes=["ebias"])
        S.op("dve", lambda e: e.tensor_scalar(out=ebias, in0=ebias, scalar1=-8.0, scalar2=None, op0=ALU.mult),
             reads=["ebias"], writes=["ebias"])

        def bcast_row(t, nel):
            return bass.AP(t, 0, [[0, 128], [1, nel]])

        def rms_rstd(x_ap, ntok, junk, ssq, col, key_x, key_junk, key_ssq):
            S.op("dve", lambda e: e.scalar_tensor_tensor(out=junk[:ntok], in0=x_ap, scalar=1.0, in1=x_ap,
                                                         op0=ALU.mult, op1=ALU.mult, accum_out=ssq[:ntok, col:col + 1]),
                 reads=[key_x], writes=[key_junk, key_ssq])

        def sqrt_recip(ssq_ap, scale, keys):
            S.op("act", lambda e: e.activation(out=ssq_ap, in_=ssq_ap, func=AF.Sqrt, bias=eps_t[:ssq_ap.shape[0], 0:1],
                                               scale=scale), reads=keys + ["eps"], writes=keys)
            S.op("dve", lambda e: e.reciprocal(out=ssq_ap, in_=ssq_ap), reads=keys, writes=keys)

        AR.top = P_MARK
        KT_W = (nmax // 128 + 1) * 128
        T1_WORDS = max(max(FACT[n_ + 16][1] * 3 * FACT[n_ + 16][0] for n_ in seq_ns) // 2 + 2, 10240)
        Win = AR.alloc([8, WIN_COLS], BF16)
        A_MARK = AR.top
        cs128 = AR.alloc([256], F32)
        identf = AR.alloc([128], F32)
        wfT = [AR.alloc([128], F32) for _ in range(2)]
        stg = [AR.alloc([3328], F32) for _ in range(3)]
        dma("sp", identf, identf_d[:, :], writes=["identf"], sem="c3")
        dma("sp", cs128, cs128_d[:, :], writes=["cs128"], sem="c4")

        def load_stg(c):
            dma("sp", stg[c % 3], w_in[c * 128:(c + 1) * 128, :], writes=["stg%d" % (c % 3)], sem="stg%d" % (c % 3))

        load_stg(0)
        load_stg(1)
        it = 0
        for c in range(8):
            if c + 2 < 8:
                load_stg(c + 2)
            st_ = stg[c % 3]
            sk = "stg%d" % (c % 3)
            S.op("dve", lambda e: e.tensor_copy(out=Win[:, c, 0:512].rearrange("p (g k d) -> p g k d", g=4, k=2),
                                                in_=st_[:, 0:512].rearrange("p (k g d) -> p g k d", k=2, g=4)),
                 reads=[sk], writes=["win_q"])
            S.op("pool", lambda e: e.tensor_copy(out=Win[:, c, 512:768], in_=st_[:, 512:768]), reads=[sk], writes=["win_kv"])
            S.op("act", lambda e: e.activation(out=Win[:, c, C_G:WIN_COLS], in_=st_[:, 1280:3328], func=AF.Copy),
                 reads=[sk], writes=["win_g"])
            for g in range(4):
                b = it % 2
                it += 1
                S.op("pe", lambda e: e.transpose(out=ps[:, b, 0:128], in_=st_[:, 768 + g * 128:768 + (g + 1) * 128], identity=identf),
                     reads=[sk, "identf"], writes=["ps%d" % b])
                S.op("dve", lambda e: e.tensor_copy(out=wfT[b], in_=ps[:, b, 0:128]), reads=["ps%d" % b], writes=["wfT%d" % b])
                S.op("pe", lambda e: e.matmul(ps[:, 2 + b, 0:256], lhsT=wfT[b], rhs=cs128, start=True, stop=True),
                     reads=["wfT%d" % b, "cs128"], writes=["ps%d" % (2 + b)])
                S.op("dve", lambda e: e.tensor_copy(
                    out=Win[:, c, C_F:C_G].rearrange("p (r g k) -> p r g k", r=2, g=4)[:, :, g, :],
                    in_=ps[:, 2 + b, 0:256].rearrange("p (r k) -> p r k", r=2)),
                    reads=["ps%d" % (2 + b)], writes=["win_f"])
        dma("sp", win_s[:, :], Win[:, :, :].rearrange("p c f -> p (c f)"), reads=WINK, sem="ws")
        S.fence()
        dma("pool", wao_s[:, :], w_ao[:, :], sem="wc3")
        dma("pool", wfo_s[:, :], w_fo[:, :], sem="wc3")
        for q4 in range(4):
            dma("pool", wo_s[q4 * 256:(q4 + 1) * 256, :], w_out[q4 * 256:(q4 + 1) * 256, :], sem="wc3")

        for si, n in enumerate(seq_ns):
            L = n + 16
            N1, N2 = FACT[L]
            nqb = n // 128
            nkb = nqb + 1
            nmt = n // 512
            x_d = xs[si]
            zf = zf_s[0:L, :]
            bs = bs_s[0:N1 * N2 * 1024].rearrange("(k n r c) -> k n r c", k=N1, n=N2, r=2)

            AR.top = A_MARK
            KT = AR.alloc([KT_W], BF16)
            Vt = AR.alloc([KT_W // 128, 128], BF16)
            B_MARK = AR.top
            gmix = AR.alloc([D], F32)
            xbuf = [AR.alloc([4, D], F32) for _ in range(2)]
            junk = AR.alloc([D], BF16)
            ssq = AR.alloc([8], F32)
            hb = AR.alloc([4, D], BF16)
            hT = [AR.alloc([8, 512], BF16) for _ in range(2)]
            zc = [AR.alloc([640], F32) for _ in range(2)]
            sq = AR.alloc([640], F32)
            ssh = AR.alloc([16], F32)
            qn = AR.alloc([640], F32)
            t1b = AR.alloc([640], F32)
            t2b = AR.alloc([640], F32)
            qko = [AR.alloc([640], BF16) for _ in range(2)]
            zfs = [AR.alloc([1024], BF16) for _ in range(2)]
            qTs = [AR.alloc([512], BF16) for _ in range(2)]
            gst = [AR.alloc([4, 512], BF16) for _ in range(2)]
            rt = [AR.alloc([128], F32) for _ in range(2)]

            if si > 0:
                dma("sp", Win[:, :, :].rearrange("p c f -> p (c f)"), win_s[:, :], writes=WINK, sem="wl")
            dma("sp", gmix, bcast_row(g_mix_t, D), writes=["gmix"], sem="c6")

            mts = [(m * 512, [128] * 4, False) for m in range(nmt)] + [(n, [16], True)]

            def load_x(mi):
                t0, subs, is_meta = mts[mi]
                sl = mi % 2
                if is_meta:
                    dma("sp", xbuf[sl][0:16, 0, :], meta[:, :], writes=["x%d_0" % sl], sem="x%d" % sl)
                else:
                    dma("sp", xbuf[sl][:, :, :], x_d[t0:t0 + 512, :].rearrange("(s p) d -> p s d", p=128),
                        writes=["x%d_%d" % (sl, s) for s in range(4)], sem="x%d" % sl)

            def front_a(mi, s):
                t0, subs, is_meta = mts[mi]
                sl = mi % 2
                ntok = subs[s]
                rms_rstd(xbuf[sl][:ntok, s, :], ntok, junk, ssq, s, "x%d_%d" % (sl, s), "junk", "ssq%d" % s)
                S.op("act", lambda e: e.activation(out=ssq[:ntok, s:s + 1], in_=ssq[:ntok, s:s + 1], func=AF.Sqrt,
                                                   bias=eps_t[:ntok, 0:1], scale=1.0 / D), reads=["ssq%d" % s, "eps"], writes=["ssq%d" % s])

            def front_b(mi, s):
                t0, subs, is_meta = mts[mi]
                sl = mi % 2
                ntok = subs[s]
                S.op("dve", lambda e: e.reciprocal(out=ssq[:ntok, s:s + 1], in_=ssq[:ntok, s:s + 1]), reads=["ssq%d" % s], writes=["ssq%d" % s])
                S.op("dve", lambda e: e.scalar_tensor_tensor(
                    out=hb[:ntok, s, :], in0=xbuf[sl][:ntok, s, :], scalar=ssq[:ntok, s:s + 1], in1=gmix[:ntok],
                    op0=ALU.mult, op1=ALU.mult), reads=["x%d_%d" % (sl, s), "ssq%d" % s, "gmix"], writes=["hb%d" % s])

            def front_dve(mi, s):
                front_a(mi, s)
                front_b(mi, s)

            def front_pe(mi, s):
                t0, subs, is_meta = mts[mi]
                sl = mi % 2
                ntok = subs[s]
                for c in range(8):
                    S.op("pe", lambda e: e.transpose(
                        out=psb(4)[:, c * 128:c * 128 + ntok], in_=hb[:ntok, s, c * 128:(c + 1) * 128],
                        identity=ident[:ntok, :ntok]), reads=["hb%d" % s, "ident"], writes=["ps4"])
                S.op("dve", lambda e: e.tensor_copy(
                    out=hT[sl][:, :, s * 128:s * 128 + ntok],
                    in_=psb(4).rearrange("p (c t) -> p c t", c=8)[:, :, 0:ntok]),
                    reads=["ps4"], writes=["hT%d_%d" % (sl, s)])

            def front(mi):
                for s in range(len(mts[mi][1])):
                    front_dve(mi, s)
                    front_pe(mi, s)

            def chain_a(ntok, so, kb, is_meta):
                z = zc[so]
                S.op("dve", lambda e: e.tensor_tensor(out=sq[:ntok], in0=z[:ntok], in1=z[:ntok], op=ALU.mult),
                     reads=["zc%d" % so], writes=["sq"])
                S.op("dve", lambda e: e.tensor_reduce(out=ssh[:ntok, 0:10], in_=sq[:ntok].rearrange("p (h d) -> p h d", d=64),
                                                      axis=AX.X, op=ALU.add), reads=["sq"], writes=["ssh"])
                S.op("act", lambda e: e.activation(out=ssh[:ntok, 0:10], in_=ssh[:ntok, 0:10], func=AF.Sqrt,
                                                   bias=eps_t[:ntok, 0:1], scale=1.0 / 64), reads=["ssh", "eps"], writes=["ssh"])

            def chain_b(ntok, so, kb, is_meta):
                z = zc[so]
                S.op("dve", lambda e: e.reciprocal(out=ssh[:ntok, 0:10], in_=ssh[:ntok, 0:10]), reads=["ssh"], writes=["ssh"])
                S.op("dve", lambda e: e.tensor_tensor(
                    out=qn[:ntok].rearrange("p (h d) -> p h d", d=64), in0=z[:ntok].rearrange("p (h d) -> p h d", d=64),
                    in1=ssh[:ntok, 0:10].unsqueeze(2).broadcast_to([ntok, 10, 64]), op=ALU.mult),
                    reads=["zc%d" % so, "ssh"], writes=["qn"])
                S.op("dve", lambda e: e.tensor_tensor(out=qn[:ntok], in0=qn[:ntok], in1=gqk[:ntok], op=ALU.mult),
                     reads=["qn", "gqk", "gqk2"], writes=["qn"])
                S.op("dve", lambda e: e.tensor_tensor(
                    out=t1b[:ntok].rearrange("p (h d) -> p h d", d=64), in0=qn[:ntok].rearrange("p (h d) -> p h d", d=64),
                    in1=rt[so][:ntok, 0:64].unsqueeze(1).broadcast_to([ntok, 10, 64]), op=ALU.mult),
                    reads=["qn", "rt%d" % so], writes=["t1b"])
                for hf in range(2):
                    S.op("dve", lambda e: e.tensor_tensor(
                        out=t2b[:ntok].rearrange("p (h a b) -> p h a b", a=2, b=32)[:, :, :, hf * 16:hf * 16 + 16],
                        in0=qn[:ntok].rearrange("p (h a b) -> p h a b", a=2, b=32)[:, :, :, (1 - hf) * 16:(1 - hf) * 16 + 16],
                        in1=rt[so][:ntok, 64:128].rearrange("p (a b) -> p a b", a=2)[:, :, hf * 16:hf * 16 + 16]
                        .unsqueeze(1).broadcast_to([ntok, 10, 2, 16]), op=ALU.mult),
                        reads=["qn", "rt%d" % so], writes=["t2b%d" % hf])
                S.op("dve", lambda e: e.tensor_tensor(out=qko[so][:ntok], in0=t1b[:ntok], in1=t2b[:ntok], op=ALU.add),
                     reads=["t1b", "t2b0", "t2b1"], writes=["qko%d" % so])

            def chain(ntok, so, kb, is_meta):
                chain_a(ntok, so, kb, is_meta)
                chain_b(ntok, so, kb, is_meta)

            def chain_pe(ntok, so, kb, is_meta):
                for j in range(5):
                    S.op("pe", lambda e: e.transpose(
                        out=psb(5)[:, j * 128:j * 128 + ntok], in_=qko[so][:ntok, j * 128:(j + 1) * 128],
                        identity=ident[:ntok, :ntok]), reads=["qko%d" % so, "ident"], writes=["ps5"])
                S.op("dve", lambda e: e.tensor_copy(out=KT[:, kb * 128:kb * 128 + ntok], in_=psb(5)[:, 512:512 + ntok]),
                     reads=["ps5"], writes=["KT%d" % kb])
                if not is_meta:
                    S.op("dve", lambda e: e.tensor_copy(out=qTs[so], in_=psb(5)[:, 0:512]), reads=["ps5"], writes=["qTs%d" % so])
                    dma("sp", qt_s[kb, :, :], qTs[so], reads=["qTs%d" % so], sem="qTs%d" % so)

            load_x(0)
            front(0)
            sub_ctr = 0
            gate_ctr = 0
            pending = None
            pending2 = None
            for mi, (t0, subs, is_meta) in enumerate(mts):
                sl = mi % 2
                if mi + 1 < len(mts):
                    load_x(mi + 1)
                ns = len(subs)
                for s, ntok in enumerate(subs):
                    kb = (t0 // 128) + s
                    so = sub_ctr % 2
                    sub_ctr += 1
                    dma("sp", rt[so][:, :], rope_d[n][kb * 128:(kb + 1) * 128, :], writes=["rt%d" % so], sem="rt%d" % so)
                    for j, (c0, c1) in enumerate([(0, 512), (512, 1024), (1024, 1536), (1536, 1792)]):
                        for c in range(8):
                            S.op("pe", lambda e: e.matmul(
                                ps[:ntok, j, 0:c1 - c0], lhsT=hT[sl][:, c, s * 128:s * 128 + ntok], rhs=Win[:, c, c0:c1],
                                start=(c == 0), stop=(c == 7)),
                                reads=["hT%d_%d" % (sl, s)] + WINK, writes=["ps%d" % j])
                    S.op("dve", lambda e: e.tensor_copy(out=zc[so][:ntok], in_=psflat[:ntok, 0:640]),
                         reads=["ps0", "ps1"], writes=["zc%d" % so])
                    S.op("dve", lambda e: e.tensor_copy(out=Vt[:ntok, kb, :], in_=psflat[:ntok, 640:768]),
                         reads=["ps1"], writes=["Vt%d" % kb])
                    S.op("act", lambda e: e.activation(out=zfs[so][:ntok], in_=psflat[:ntok, 768:1792], func=AF.Copy),
                         reads=["ps1", "ps2", "ps3"], writes=["zfs%d" % so])
                    p0 = 0 if is_meta else 16 + t0 + s * 128
                    dma("sp", zf[p0:p0 + ntok, :], zfs[so][:ntok], reads=["zfs%d" % so], sem="zfs%d" % so)
                    has_next = mi + 1 < len(mts) and s < len(mts[mi + 1][1])
                    if pending is not None:
                        chain_a(*pending)
                    if has_next:
                        front_a(mi + 1, s)
                    if not is_meta:
                        for jj in range(4):
                            j = s * 4 + jj
                            gb = 6 + (gate_ctr % 2)
                            gsl = (gate_ctr // 4) % 2
                            gate_ctr += 1
                            for c in range(8):
                                S.op("pe", lambda e: e.matmul(
                                    ps[:, gb, :], lhsT=Win[:, c, C_G + j * 128:C_G + (j + 1) * 128], rhs=hT[sl][:, c, :],
                                    start=(c == 0), stop=(c == 7)),
                                    reads=["hT%d_%d" % (sl, q) for q in range(4)] + WINK, writes=["ps%d" % gb])
                            S.op("act", lambda e: e.activation(out=gst[gsl][:, jj, :], in_=ps[:, gb, :], func=AF.Sigmoid),
                                 reads=["ps%d" % gb], writes=["gst%d_%d" % (gsl, jj)])
                            if jj == 3:
                                dma("sp", g_s[:, j - 3:j + 1, t0:t0 + 512], gst[gsl][:, :, :],
                                    reads=["gst%d_%d" % (gsl, q) for q in range(4)], sem="gst%d" % gsl)
                    if has_next:
                        front_b(mi + 1, s)
                    if pending is not None:
                        chain_b(*pending)
                    if pending2 is not None:
                        chain_pe(*pending2)
                    pending2 = pending
                    pending = (ntok, so, kb, is_meta)
                    if has_next:
                        front_pe(mi + 1, s)
            if pending2 is not None:
                chain_pe(*pending2)
            chain(*pending)
            chain_pe(*pending)
            S.fence()

            AR.top = B_MARK
            qT = [AR.alloc([512], BF16) for _ in range(2)]
            pT = [AR.alloc([2, 512], BF16) for _ in range(4)]
            psum2 = [AR.alloc([2, 512], BF16) for _ in range(2)]
            rec = AR.alloc([512], F32)
            osb = AR.alloc([512], F32)
            oT = [AR.alloc([512], BF16) for _ in range(2)]
            dma("sp", qT[0], qt_s[0, :, :], writes=["qT0"], sem="qT0")
            b_top = AR.top
            AR.top = P_MARK
            T1 = AR.alloc([N2, 3, N1], BF16)
            assert AR.top <= P_MARK + T1_WORDS
            AR.top = b_top
            dma("sp", T1[:N1], t1_d[n][:, :, :, :], sem="c7")
            if si == nseq - 1:
                for c in range(8):
                    dma("pool", w1_s[c * 128:(c + 1) * 128, :], w_fi[c * 128:(c + 1) * 128, :], sem="wc1")
                for jq in range(NJ):
                    dma("pool", w2_s[jq * 128:(jq + 1) * 128, :], w_f2[jq * 128:(jq + 1) * 128, :], sem="wc2")
            its = [(qb, kb) for qb in range(nqb) for kb in range(nkb)]

            def emit_qk(i):
                qb, kb = its[i]
                nk = 128 if kb < nkb - 1 else 16
                sb = 2 * (i % 3)
                qs = qb % 2
                for h in range(2):
                    S.op("pe", lambda e, h=h, nk=nk, sb=sb, qs=qs, kb=kb: e.matmul(
                        ps[:nk, sb + h, :], lhsT=KT[h * 64:(h + 1) * 64, kb * 128:kb * 128 + nk],
                        rhs=qT[qs][h * 64:(h + 1) * 64, :], start=True, stop=True),
                        reads=["qT%d" % qs], writes=["ps%d" % (sb + h)])

            norm_q = []

            def run_norm(cnt):
                for _ in range(cnt):
                    o_, qb_, qq = norm_q.pop(0)
                    cs = slice(qq * 128, (qq + 1) * 128)
                    S.op("dve", lambda e: e.reciprocal(out=rec[:, cs], in_=rec[:, cs]), reads=["rec"], writes=["rec%d" % qq])
                    S.op("dve", lambda e: e.tensor_tensor(out=oT[o_][:, cs], in0=osb[:, cs], in1=rec[:, cs], op=ALU.mult),
                         reads=["osb", "rec%d" % qq], writes=["oT%d_%d" % (o_, qq)])
                    if qq == 3:
                        dma("sp", ot_s[qb_, :, :], oT[o_], reads=["oT%d_%d" % (o_, q) for q in range(4)], sem="oT%d" % o_)

            emit_qk(0)
            if len(its) > 1:
                emit_qk(1)
            pend_sum = None
            for i, (qb, kb) in enumerate(its):
                nk = 128 if kb < nkb - 1 else 16
                sb = 2 * (i % 3)
                pslot = i % 4
                ob = 6
                if kb == 0 and qb + 1 < nqb:
                    dma("sp", qT[(qb + 1) % 2], qt_s[qb + 1, :, :], writes=["qT%d" % ((qb + 1) % 2)], sem="qT%d" % ((qb + 1) % 2))
                if i + 2 < len(its):
                    emit_qk(i + 2)
                S.op("act", lambda e: e.activation(
                    out=pT[pslot][:nk], in_=ps[:nk, sb:sb + 2, :], func=AF.Exp, bias=EXP_SHIFT, scale=0.125),
                    reads=["ps%d" % sb, "ps%d" % (sb + 1)], writes=["pT%d" % pslot])
                first, last = (kb == 0), (kb == nkb - 1)
                for h in range(2):
                    S.op("pe", lambda e: e.matmul(
                        ps[h * 64:(h + 1) * 64, ob, :], lhsT=Vt[:nk, kb, h * 64:(h + 1) * 64], rhs=pT[pslot][:nk, h, :],
                        start=first, stop=last, tile_position=(0, h * 64)),
                        reads=["pT%d" % pslot], writes=["ps%d_%d" % (ob, h)])
                if pend_sum is not None:
                    pp2, pst = pend_sum
                    pend_sum = None
                    for h in range(2):
                        S.op("pe", lambda e: e.matmul(
                            ps[h * 64:(h + 1) * 64, ob + 1, :], lhsT=ones64[:, :], rhs=psum2[pp2][:, h, :],
                            start=pst, stop=False, tile_position=(0, h * 64)),
                            reads=["psum2_%d" % pp2], writes=["ps%d_%d" % (ob + 1, h)])
                if last:
                    for h in range(2):
                        S.op("pe", lambda e: e.matmul(
                            ps[h * 64:(h + 1) * 64, ob + 1, :], lhsT=ones64[:nk, :], rhs=pT[pslot][:nk, h, :],
                            start=False, stop=True, tile_position=(0, h * 64)),
                            reads=["pT%d" % pslot], writes=["ps%d_%d" % (ob + 1, h)])
                elif kb % 2 == 1:
                    p2 = (kb // 2) % 2
                    pprev = (i - 1) % 4
                    S.op("dve", lambda e: e.tensor_tensor(out=psum2[p2], in0=pT[pprev], in1=pT[pslot], op=ALU.add),
                         reads=["pT%d" % pprev, "pT%d" % pslot], writes=["psum2_%d" % p2])
                    pend_sum = (p2, kb == 1)
                if norm_q and (last or (kb >= 2 and kb % 2 == 0)):
                    run_norm(len(norm_q) if last else 1)
                if last:
                    os_ = qb % 2
                    S.op("dve", lambda e: e.tensor_copy(out=rec, in_=ps[:, ob + 1, :]),
                         reads=["ps%d_0" % (ob + 1), "ps%d_1" % (ob + 1)], writes=["rec"])
                    S.op("dve", lambda e: e.tensor_copy(out=osb, in_=ps[:, ob, :]),
                         reads=["ps%d_0" % ob, "ps%d_1" % ob], writes=["osb"])
                    for qq in range(4):
                        norm_q.append((os_, qb, qq))
            run_norm(len(norm_q))
            S.fence()

            AR.top = P_MARK + T1_WORDS
            YT = AR.alloc([4, L], BF16)
            Wao = AR.alloc([4, D], BF16)
            Wfo = AR.alloc([4, D], BF16)
            Wo = AR.alloc([8, D], BF16)
            C_MARK = AR.top
            for h in range(2):
                dma("sp", Wao[h * 64:(h + 1) * 64], wao_s[h * 256:(h + 1) * 256, :].rearrange("(g d) o -> d g o", d=64),
                    writes=["Wao"], sem="w3")
            dma("sp", Wfo, wfo_s.rearrange("(g c) o -> c g o", c=128), writes=["Wfo"], sem="w4")
            dma("sp", Wo, wo_s.rearrange("(c p) o -> p c o", p=128), writes=["Wo"], sem="w5")
            NB = 4
            T2 = AR.alloc([2, N2], BF16)
            NZ = 4
            zin = [AR.alloc([NB, 1024], BF16) for _ in range(NZ)]
            bst = [AR.alloc([NB, 2, 512], BF16) for _ in range(3)]
            dma("sp", T2[:N2], t2_d[n][:, :, :], writes=["T2"], sem="c8")
            zf3 = zf.rearrange("(a b) c -> a b c", b=N2)
            grp = [(a, min(NB, N2 - a)) for a in range(0, N2, NB)]

            def load_z(gi):
                a, nb = grp[gi]
                dma("sp", zin[gi % NZ][:N1, 0:nb, :], zf3[:, a:a + nb, :], writes=["zin%d" % (gi % NZ)], sem="zin%d" % (gi % NZ))

            for gi in range(min(NZ - 1, len(grp))):
                load_z(gi)
            ctr = 0
            for gi, (a, nb) in enumerate(grp):
                if gi + NZ - 1 < len(grp):
                    load_z(gi + NZ - 1)
                zs = gi % NZ
                bs_ = gi % 3
                for j in range(nb):
                    n2 = a + j
                    br = 2 * (ctr % 2)
                    ctr += 1
                    zr = zin[zs][:N1, j, 0:512]
                    zi = zin[zs][:N1, j, 512:1024]
                    for (bank, m0, r0, m1, r1) in [(br, 0, zr, 1, zi), (br + 1, 2, zr, 0, zi)]:
                        S.op("pe", lambda e, bank=bank, m0=m0, r0=r0, n2=n2: e.matmul(
                            ps[:N1, bank, :], lhsT=T1[:N1, n2, m0, :], rhs=r0, start=True, stop=False),
                            reads=["T1", "zin%d" % zs], writes=["ps%d" % bank])
                        S.op("pe", lambda e, bank=bank, m1=m1, r1=r1, n2=n2: e.matmul(
                            ps[:N1, bank, :], lhsT=T1[:N1, n2, m1, :], rhs=r1, start=False, stop=True),
                            reads=["T1", "zin%d" % zs], writes=["ps%d" % bank])
                    if ctr % 2 == 0:
                        S.op("dve", lambda e: e.tensor_copy(out=bst[bs_][:N1, j, :, :], in_=ps[:N1, br:br + 2, :]),
                             reads=["ps%d" % br, "ps%d" % (br + 1)], writes=["bst%d_%d" % (bs_, j)])
                    else:
                        S.op("act", lambda e: e.activation(out=bst[bs_][:N1, j, :, :], in_=ps[:N1, br:br + 2, :], func=AF.Copy),
                             reads=["ps%d" % br, "ps%d" % (br + 1)], writes=["bst%d_%d" % (bs_, j)])
                dma("sp", bs[:, a:a + nb, :, :], bst[bs_][:N1, 0:nb, :, :], reads=["bst%d_%d" % (bs_, j) for j in range(nb)],
                    sem="bst%d" % bs_)
            S.wait_dma("sp", [sem_cache["bst%d" % q] for q in range(3) if ("bst%d" % q) in sem_cache])
            f_top = AR.top
            AR.top = P_MARK
            oin = [AR.alloc([4, 512], BF16) for _ in range(2)]
            gins = [AR.alloc([16, 512], BF16) for _ in range(2)]
            assert AR.top <= P_MARK + T1_WORDS
            AR.top = f_top

            def load_c1(m, extra=()):
                dma("sp", oin[m % 2], ot_s[4 * m:4 * m + 4].rearrange("q p f -> p q f"), writes=["oin%d" % (m % 2)] + list(extra),
                    sem="oin%d" % (m % 2))
                dma("sp", gins[m % 2], g_s[:, :, m * 512:(m + 1) * 512], writes=["gin%d" % (m % 2)] + list(extra), sem="gin%d" % (m % 2))

            load_c1(0, extra=["T1"])
            bin_ = zin
            grp2 = [(a, min(NB, N1 - a)) for a in range(0, N1, NB)]

            def load_b(gi):
                a, nb = grp2[gi]
                dma("sp", bin_[gi % NZ][:N2, 0:nb, :], bs[a:a + nb].rearrange("k n r c -> n k (r c)"),
                    writes=["zin%d" % (gi % NZ)], sem="zin%d" % (gi % NZ))

            for gi in range(min(NZ - 1, len(grp2))):
                load_b(gi)
            ctr = 0
            for gi, (a, nb) in enumerate(grp2):
                if gi + NZ - 1 < len(grp2):
                    load_b(gi + NZ - 1)
                zs = gi % NZ
                for j in range(nb):
                    k1 = a + j
                    bank = ctr % 2
                    ctr += 1
                    for g in range(4):
                        for r in range(2):
                            S.op("pe", lambda e, g=g, r=r, j=j, zs=zs, bank=bank: e.matmul(
                                ps[:, bank, g * N2:(g + 1) * N2], lhsT=bin_[zs][:N2, j, r * 512 + g * 128:r * 512 + (g + 1) * 128],
                                rhs=T2[:N2, r, :], start=(r == 0), stop=(r == 1)),
                                reads=["T2", "zin%d" % zs], writes=["ps%d" % bank])
                    if ctr % 2 == 0:
                        S.op("dve", lambda e: e.tensor_copy(
                            out=YT[:, :, k1:k1 + N1 * (N2 - 1) + 1:N1],
                            in_=ps[:, bank, 0:4 * N2].rearrange("p (g k) -> p g k", g=4)),
                            reads=["ps%d" % bank], writes=["YT%d" % (ctr % 2)])
                    else:
                        S.op("act", lambda e: e.activation(
                            out=YT[:, :, k1:k1 + N1 * (N2 - 1) + 1:N1],
                            in_=ps[:, bank, 0:4 * N2].rearrange("p (g k) -> p g k", g=4), func=AF.Copy),
                            reads=["ps%d" % bank], writes=["YT%d" % (ctr % 2)])
            S.fence()

            AR.top = C_MARK
            ta = [AR.alloc([512], F32) for _ in range(2)]
            tb = [AR.alloc([512], F32) for _ in range(2)]
            mg = [AR.alloc([8, 512], BF16) for _ in range(2)]
            xin = [AR.alloc([D], F32) for _ in range(3)]
            x1o = [AR.alloc([D], F32) for _ in range(2)]
            xctr = 0

            def c1_merge(m):
                ms = m % 2
                t0 = m * 512
                gin = gins[ms]
                gk = "gin%d" % ms
                if m + 1 < nmt:
                    load_c1(m + 1)
                for oc in range(8):
                    ba = (oc % 2)
                    bf = 2 + (oc % 2)
                    for g in range(4):
                        S.op("pe", lambda e: e.matmul(
                            ps[:, ba, :], lhsT=Wao[:, g, oc * 128:(oc + 1) * 128],
                            rhs=oin[ms].rearrange("p q (g t) -> p q g t", g=4)[:, :, g, :],
                            start=(g == 0), stop=(g == 3)), reads=["Wao", "oin%d" % ms], writes=["ps%d" % ba])
                    for g in range(4):
                        S.op("pe", lambda e: e.matmul(
                            ps[:, bf, :], lhsT=Wfo[:, g, oc * 128:(oc + 1) * 128], rhs=YT[:, g, 16 + t0:16 + t0 + 512],
                            start=(g == 0), stop=(g == 3)), reads=["Wfo", "YT"], writes=["ps%d" % bf])
                    o2 = oc % 2
                    S.op("dve", lambda e: e.tensor_tensor(out=ta[o2], in0=ps[:, ba, :], in1=gin[:, oc, :], op=ALU.mult),
                         reads=["ps%d" % ba, gk], writes=["ta%d" % o2])
                    S.op("dve", lambda e: e.tensor_tensor(out=tb[o2], in0=ps[:, bf, :], in1=gin[:, 8 + oc, :], op=ALU.mult),
                         reads=["ps%d" % bf, gk], writes=["tb%d" % o2])
                    S.op("pool", lambda e: e.tensor_tensor(out=mg[ms][:, oc, :], in0=ta[o2], in1=tb[o2], op=ALU.add),
                         reads=["ta%d" % o2, "tb%d" % o2], writes=["mg%d_%d" % (ms, oc)])

            c1_merge(0)
            for m in range(nmt):
                ms = m % 2
                t0 = m * 512
                if m + 1 < nmt:
                    c1_merge(m + 1)
                for s in range(4):
                    xsl = xctr % 3
                    osl = xctr % 2
                    xctr += 1
                    tt = t0 + s * 128
                    dma("sp", xin[xsl], x_d[tt:tt + 128, :], writes=["xin%d" % xsl], sem="xin%d" % xsl)
                    pb = 4 + 2 * (s % 2)
                    for hh in range(2):
                        for oc in range(8):
                            S.op("pe", lambda e: e.matmul(
                                ps[:, pb + hh, :], lhsT=mg[ms][:, oc, s * 128:(s + 1) * 128], rhs=Wo[:, oc, hh * 512:(hh + 1) * 512],
                                start=(oc == 0), stop=(oc == 7)), reads=["mg%d_%d" % (ms, oc), "Wo"], writes=["ps%d" % (pb + hh)])
                    S.op("dve", lambda e: e.tensor_tensor(
                        out=x1o[osl], in0=xin[xsl], in1=psflat[:, pb * 512:(pb + 2) * 512], op=ALU.add),
                        reads=["xin%d" % xsl, "ps%d" % pb, "ps%d" % (pb + 1)], writes=["x1o%d" % osl])
                    dma("sp", x1_s[si][tt:tt + 128, :], x1o[osl], reads=["x1o%d" % osl], sem="x1o%d" % osl)
            S.fence()

        AR.top = P_MARK
        W1 = AR.alloc([8, 2 * DFF], BF16)
        W2 = AR.alloc([NJ, D], BF16)
        gfin = AR.alloc([D], F32)
        gffn = AR.alloc([D], F32)
        xn = [AR.alloc([D], F32) for _ in range(4)]
        xr = [AR.alloc([D], F32) for _ in range(2)]
        ssq = AR.alloc([8], F32)
        hb2 = [AR.alloc([D], BF16) for _ in range(2)]
        hT2 = AR.alloc([8, 512], BF16)
        sg = [AR.alloc([512], BF16) for _ in range(2)]
        actT = AR.alloc([NJ, 512], BF16)
        yo = AR.alloc([D], F32)
        dma("sp", gffn, bcast_row(g_ffn_t, D), writes=["gffn"], sem="c6")
        dma("sp", gfin, bcast_row(g_fin_t, D), writes=["gfin"], sem="c7")

        tiles = [(si, m) for si, n in enumerate(seq_ns) for m in range(n // 512)]

        def load_xn(ti):
            si, m = tiles[ti]
            for s in range(4):
                tt = m * 512 + s * 128
                dma("sp", xn[s], x1_s[si][tt:tt + 128, :], writes=["xn%d" % s], sem="xn%d" % s)

        def front2_dve(s):
            hs = s % 2
            rms_rstd(xn[s][:, :], 128, hb2[hs], ssq, s, "xn%d" % s, "hb2_%d" % hs, "ssqn%d" % s)
            sqrt_recip(ssq[:, s:s + 1], 1.0 / D, ["ssqn%d" % s])
            S.op("dve", lambda e: e.scalar_tensor_tensor(out=hb2[hs], in0=xn[s], scalar=ssq[:, s:s + 1], in1=gffn, op0=ALU.mult, op1=ALU.mult),
                 reads=["xn%d" % s, "ssqn%d" % s, "gffn"], writes=["hb2_%d" % hs])

        def front2_pe(s):
            hs = s % 2
            for c in range(8):
                S.op("pe", lambda e: e.transpose(out=psb(7)[:, c * 128:(c + 1) * 128], in_=hb2[hs][:, c * 128:(c + 1) * 128],
                                                 identity=ident), reads=["hb2_%d" % hs, "ident"], writes=["ps7"])
            S.op("dve", lambda e: e.tensor_copy(out=hT2[:, :, s * 128:(s + 1) * 128], in_=psb(7).rearrange("p (c t) -> p c t", c=8)),
                 reads=["ps7"], writes=["hT2_%d" % s])

        load_xn(0)
        NBLK = (NJ + 3) // 4
        w1v = w1_s.rearrange("(c p) f -> p c f", p=128)
        for bk in range(NBLK):
            j0, j1 = bk * 4, min(NJ, bk * 4 + 4)
            for half in range(2):
                c0 = half * DFF + j0 * 128
                c1 = half * DFF + j1 * 128
                dma("sp", W1[:, :, c0:c1], w1v[:, :, c0:c1], writes=["W1b_%d" % bk], sem="w6_%d" % bk)
        for jq in range(0, NJ, 2):
            dma("sp", W2[:, jq:jq + 2, :], w2_s[jq * 128:(jq + 2) * 128, :].rearrange("(c p) o -> p c o", p=128),
                writes=["W2"], sem="w7")
        for s in range(4):
            front2_dve(s)
            front2_pe(s)
        fctr = 0
        octr = 0
        rctr = 0
        hkeys = ["hT2_%d" % s for s in range(4)]
        for ti, (si, m) in enumerate(tiles):
            t0 = m * 512
            x1d = x1_s[si]
            if ti + 1 < len(tiles):
                load_xn(ti + 1)
            for j in range(NJ):
                bg = 2 * (fctr % 2)
                ss_ = fctr % 2
                fctr += 1
                for (bank, col0) in [(bg, j * 128), (bg + 1, DFF + j * 128)]:
                    for c in range(8):
                        S.op("pe", lambda e: e.matmul(ps[:, bank, :], lhsT=W1[:, c, col0:col0 + 128], rhs=hT2[:, c, :],
                                                      start=(c == 0), stop=(c == 7)),
                             reads=hkeys + ["W1b_%d" % (j // 4)], writes=["ps%d" % bank])
                S.op("act", lambda e: e.activation(out=sg[ss_], in_=ps[:, bg, :], func=AF.Silu), reads=["ps%d" % bg], writes=["sg%d" % ss_])
                S.op("dve", lambda e: e.tensor_tensor(out=actT[:, j, :], in0=ps[:, bg + 1, :], in1=sg[ss_], op=ALU.mult),
                     reads=["ps%d" % (bg + 1), "sg%d" % ss_], writes=["actT%d" % j])
            for s in range(4):
                tt = t0 + s * 128
                rs = rctr % 2
                rctr += 1
                dma("sp", xr[rs], x1d[tt:tt + 128, :], writes=["xr%d" % rs], sem="xr%d" % rs)
                if ti + 1 < len(tiles):
                    front2_dve(s)
                banks = []
                for hh in range(2):
                    bank = 4 + (octr % 3)
                    octr += 1
                    banks.append(bank)
                    for j in range(NJ):
                        S.op("pe", lambda e: e.matmul(ps[:, bank, :], lhsT=actT[:, j, s * 128:(s + 1) * 128],
                                                      rhs=W2[:, j, hh * 512:(hh + 1) * 512], start=(j == 0), stop=(j == NJ - 1)),
                             reads=["actT%d" % j, "W2"], writes=["ps%d" % bank])
                if ti + 1 < len(tiles):
                    front2_pe(s)
                for hh in range(2):
                    S.op("dve", lambda e: e.tensor_tensor(out=xr[rs][:, hh * 512:(hh + 1) * 512], in0=xr[rs][:, hh * 512:(hh + 1) * 512],
                                                          in1=ps[:, banks[hh], :], op=ALU.add),
                         reads=["xr%d" % rs, "ps%d" % banks[hh]], writes=["xr%d" % rs])
                rms_rstd(xr[rs][:, :], 128, yo, ssq, 4, "xr%d" % rs, "yo", "ssq4")
                sqrt_recip(ssq[:, 4:5], 1.0 / D, ["ssq4"])
                S.op("dve", lambda e: e.scalar_tensor_tensor(out=yo, in0=xr[rs], scalar=ssq[:, 4:5], in1=gfin, op0=ALU.mult, op1=ALU.mult),
                     reads=["xr%d" % rs, "ssq4", "gfin"], writes=["yo"])
                dma("sp", ys[si][tt:tt + 128, :], yo, reads=["yo"], sem="yo")
        S.fence()
        S.emit(block)
    return nc


SEQ_NS = [2048, 2048, 8192]
_CACHE = {}


def _get_program(seq_ns):
    key = tuple(seq_ns)
    if key not in _CACHE:
        _CACHE[key] = (build(list(seq_ns)), host_consts(list(seq_ns)))
    return _CACHE[key]


def run_cores(seq_ns, per_core_x, weights):
    nc, consts = _get_program(seq_ns)
    in_maps = []
    for xl in per_core_x:
        m = dict(consts)
        m.update(weights)
        for i, x in enumerate(xl):
            m["x%d" % i] = np.ascontiguousarray(x, dtype=np.float32)
        in_maps.append(m)
    res = run_bass_kernel_spmd(nc, in_maps, core_ids=list(range(len(in_maps))))
    return [[r["y%d" % i] for i in range(len(seq_ns))] for r in res.results]


def kernel(x_prompt, x_sample, meta_tokens, norm_mix_g, w_in, q_norm_g, k_norm_g, w_attn_o, w_four_o,
           w_out, norm_ffn_g, w_ffn_in, w_ffn_out, final_norm_g):
    f = lambda a: np.ascontiguousarray(np.asarray(a), dtype=np.float32)
    weights = {
        "meta_tokens": f(meta_tokens), "norm_mix_g": f(norm_mix_g).reshape(-1), "q_norm_g": f(q_norm_g).reshape(-1),
        "k_norm_g": f(k_norm_g).reshape(-1), "norm_ffn_g": f(norm_ffn_g).reshape(-1), "final_norm_g": f(final_norm_g).reshape(-1),
        "w_in": f(w_in)[0], "w_attn_o": f(w_attn_o)[0], "w_four_o": f(w_four_o)[0], "w_out": f(w_out)[0],
        "w_ffn_in": f(w_ffn_in)[0], "w_ffn_out": f(w_ffn_out)[0],
    }
    xp = np.asarray(x_prompt)
    xsm = np.asarray(x_sample)
    per_core = [[xp[2 * c], xp[2 * c + 1], xsm[c]] for c in range(8)]
    outs = run_cores(SEQ_NS, per_core, weights)
    y_prompt = np.stack([outs[c][k] for c in range(8) for k in range(2)], 0).astype(np.float32)
    y_sample = np.stack([outs[c][2] for c in range(8)], 0).astype(np.float32)
    return (y_prompt, y_sample)
```

```python
import bisect
from contextlib import ExitStack
import numpy as np
import ml_dtypes
import concourse.bass as bass
import concourse.mybir as mybir
from concourse.bass_utils import run_bass_kernel_spmd

F32 = mybir.dt.float32
BF16 = mybir.dt.bfloat16
AF = mybir.ActivationFunctionType
ALU = mybir.AluOpType
AX = mybir.AxisListType

D = 1024
DFF = 2816
NJ = DFF // 128
WIN_COLS = 3840
C_K, C_V, C_F, C_G = 512, 640, 768, 1792
FACT = {528: (24, 22), 1040: (40, 26), 2064: (86, 24), 8208: (114, 72)}
EPS = 1e-6
EXP_SHIFT = -10.0

ENGS = ["pe", "act", "dve", "pool"]
EIDX = {e: i for i, e in enumerate(ENGS)}
EPOCH = 30000


class DSem:
    def __init__(self, h, name):
        self.h = h
        self.count = 0
        self.name = name


class _Rec:
    def __init__(self):
        self.call = None

    def __getattr__(self, name):
        def f(*a, **k):
            self.call = (name, a, k)
            return self
        return f


class Sched:
    def __init__(self, nc, stack):
        self.nc = nc
        self.stack = stack
        self.streams = {e: [] for e in ENGS + ["sp"]}
        self.nops = {e: 0 for e in ENGS}
        self.snap = {e: [] for e in ENGS}
        self.signaled = {e: [] for e in ENGS}
        self.clock = {e: [-1, -1, -1, -1] for e in ENGS + ["sp"]}
        self.known_sem = {e: {} for e in ENGS + ["sp"]}
        self.res = {}
        self.dsems = []

    def dsem(self, name):
        h = self.stack.enter_context(self.nc.semaphore(name))
        s = DSem(h, name)
        self.dsems.append(s)
        return s

    def _need(self, eng, tok):
        if tok is None:
            return
        if tok[0] == "eng":
            _, E, idx = tok
            if E == "pe" and eng == "pe":
                return
            if self.clock[eng][EIDX[E]] >= idx:
                return
            sl = self.signaled[E]
            j = bisect.bisect_left(sl, idx)
            if j < len(sl) and sl[j] - idx <= (3 if E == "pe" else 0):
                tgt = sl[j]
            else:
                bisect.insort(sl, idx)
                tgt = idx
            self.streams[eng].append(("we", E, tgt))
            sn = self.snap[E][tgt]
            c = self.clock[eng]
            for i in range(4):
                if sn[i] > c[i]:
                    c[i] = sn[i]
            if c[EIDX[E]] < tgt:
                c[EIDX[E]] = tgt
        else:
            _, s, cnt = tok
            if self.known_sem[eng].get(s, 0) >= cnt:
                return
            self.streams[eng].append(("ws", s, cnt))
            self.known_sem[eng][s] = cnt

    def op(self, eng, fn, reads=(), writes=(), dsem=None):
        rec = _Rec()
        fn(rec)
        fn = rec.call
        deps = []
        for r in reads:
            st = self.res.get(r)
            if st is not None:
                deps.append(st[0])
        for w in writes:
            st = self.res.get(w)
            if st is not None:
                deps.append(st[0])
                deps.extend(st[1])
        for d in deps:
            self._need(eng, d)
        if dsem is not None:
            dsem.count += 16
            tok = ("dma", dsem, dsem.count)
            self.streams[eng].append(("dma", fn, dsem))
        else:
            idx = self.nops[eng]
            self.nops[eng] = idx + 1
            tok = ("eng", eng, idx)
            self.snap[eng].append(tuple(self.clock[eng]))
            if eng == "pe":
                self.clock[eng][0] = idx
            self.streams[eng].append(("op", fn, idx))
        for w in writes:
            self.res[w] = [tok, []]
        for r in reads:
            st = self.res.get(r)
            if st is None:
                st = self.res[r] = [None, []]
            rl = st[1]
            for i, t in enumerate(rl):
                if t[0] == tok[0] and t[1] == tok[1]:
                    rl[i] = tok
                    break
            else:
                rl.append(tok)
        return tok

    def fence(self):
        toks = [("eng", E, self.nops[E] - 1) for E in ENGS if self.nops[E] > 0]
        toks += [("dma", s, s.count) for s in self.dsems if s.count]
        for eng in ENGS + ["sp"]:
            for t in toks:
                self._need(eng, t)
        self.res = {}

    def emit(self, block):
        nc = self.nc
        esem = {}
        for e in ENGS:
            self.signaled[e].sort()
            n = len(self.signaled[e])
            ne = max(1, (n + EPOCH - 1) // EPOCH)
            esem[e] = [self.stack.enter_context(nc.semaphore("es_%s_%d" % (e, k))) for k in range(ne)]
        rank = {e: {idx: i for i, idx in enumerate(self.signaled[e])} for e in ENGS}

        def run(eng, eo):
            for ent in self.streams[eng]:
                k = ent[0]
                if k == "we":
                    r = rank[ent[1]][ent[2]]
                    eo.wait_ge(esem[ent[1]][r // EPOCH], r % EPOCH + 1)
                elif k == "ws":
                    eo.wait_ge(ent[1].h, ent[2])
                elif k == "dma":
                    nm, a, kw = ent[1]
                    getattr(eo, nm)(*a, **kw).then_inc(ent[2].h, 16)
                else:
                    nm, a, kw = ent[1]
                    ins = getattr(eo, nm)(*a, **kw)
                    r = rank[eng].get(ent[2])
                    if r is not None:
                        ins.then_inc(esem[eng][r // EPOCH], 1)

        @block.tensor
        def _(t):
            run("pe", t)

        @block.scalar
        def _(t):
            run("act", t)

        @block.vector
        def _(t):
            run("dve", t)

        @block.gpsimd
        def _(t):
            run("pool", t)

        @block.sync
        def _(t):
            run("sp", t)


class Arena:
    def __init__(self, base, nwords):
        self.base = base
        self.n = nwords
        self.top = 0

    def alloc(self, shape, dtype):
        nel = int(np.prod(shape))
        words = nel if dtype == F32 else (nel + 1) // 2
        a = self.base[:, self.top:self.top + words]
        self.top += words
        assert self.top <= self.n, ("SBUF arena overflow", self.top, self.n)
        if dtype != F32:
            a = a.bitcast(dtype)[:, 0:nel]
        if len(shape) == 2:
            a = a.rearrange("p (a b) -> p a b", a=shape[0], b=shape[1])
        elif len(shape) == 3:
            a = a.rearrange("p (a b c) -> p a b c", a=shape[0], b=shape[1], c=shape[2])
        return a


def host_consts(seq_ns):
    c = {}
    c["ident"] = np.eye(128, dtype=np.float32).astype(ml_dtypes.bfloat16)
    c["identf"] = np.eye(128, dtype=np.float32)
    k = np.arange(128, dtype=np.float64)
    ang = 2 * np.pi * np.outer(k, k) / 128.0
    c["cs128"] = (np.concatenate([np.cos(ang), -np.sin(ang)], 1) / np.sqrt(128.0)).astype(np.float32)
    inv_freq = 1.0 / (10000.0 ** (np.arange(0, 32, 2, dtype=np.float64) / 32.0))
    for n in sorted(set(seq_ns)):
        L = n + 16
        N1, N2 = FACT[L]
        nkb = n // 128 + 1
        t = np.arange(n)
        row = np.concatenate([t // 64, np.full(16, -1)]).astype(np.float64)
        col = np.concatenate([t % 64, np.arange(16)]).astype(np.float64)
        ar = row[:, None] * inv_freq[None]
        ac = col[:, None] * inv_freq[None]
        tab = np.zeros((nkb * 128, 128), np.float32)
        tab[:L, 0:64] = np.concatenate([np.cos(ar), np.cos(ar), np.cos(ac), np.cos(ac)], 1)
        tab[:L, 64:128] = np.concatenate([-np.sin(ar), np.sin(ar), -np.sin(ac), np.sin(ac)], 1)
        c["rope%d" % n] = tab
        n1 = np.arange(N1, dtype=np.float64)
        n2 = np.arange(N2, dtype=np.float64)
        a1 = 2 * np.pi * (n1[:, None, None] * n1[None, None, :] / N1 + n2[None, :, None] * n1[None, None, :] / L)
        t1 = np.stack([np.cos(a1), np.sin(a1), -np.sin(a1)], 2)
        c["t1_%d" % n] = t1.astype(np.float32).astype(ml_dtypes.bfloat16)
        a2 = 2 * np.pi * np.outer(n2, n2) / N2
        t2 = np.stack([np.cos(a2), np.sin(a2)], 1) / np.sqrt(float(L))
        c["t2_%d" % n] = t2.astype(np.float32).astype(ml_dtypes.bfloat16)
    return c


def build(seq_ns):
    nc = bass.Bass("TRN2", target_bir_lowering=False)
    nseq = len(seq_ns)

    def din(name, shape, dt=F32):
        return nc.dram_tensor(name, list(shape), dt, kind="ExternalInput")

    xs = [din("x%d" % i, [n, D]).ap() for i, n in enumerate(seq_ns)]
    ys = [nc.dram_tensor("y%d" % i, [n, D], F32, kind="ExternalOutput").ap() for i, n in enumerate(seq_ns)]
    meta = din("meta_tokens", [16, D]).ap()
    g_mix_t = din("norm_mix_g", [D])
    g_q_t = din("q_norm_g", [64])
    g_k_t = din("k_norm_g", [64])
    g_ffn_t = din("norm_ffn_g", [D])
    g_fin_t = din("final_norm_g", [D])
    w_in = din("w_in", [D, 3328]).ap()
    w_ao = din("w_attn_o", [512, D]).ap()
    w_fo = din("w_four_o", [512, D]).ap()
    w_out = din("w_out", [D, D]).ap()
    w_fi = din("w_ffn_in", [D, 2 * DFF]).ap()
    w_f2 = din("w_ffn_out", [DFF, D]).ap()
    ident_d = din("ident", [128, 128], BF16).ap()
    identf_d = din("identf", [128, 128]).ap()
    cs128_d = din("cs128", [128, 256]).ap()
    rope_d, t1_d, t2_d = {}, {}, {}
    for n in sorted(set(seq_ns)):
        N1, N2 = FACT[n + 16]
        rope_d[n] = din("rope%d" % n, [(n // 128 + 1) * 128, 128]).ap()
        t1_d[n] = din("t1_%d" % n, [N1, N2, 3, N1], BF16).ap()
        t2_d[n] = din("t2_%d" % n, [N2, 2, N2], BF16).ap()

    nmax = max(seq_ns)
    win_s = nc.dram_tensor("win_s", [128, 8 * WIN_COLS], BF16).ap()
    zf_s = nc.dram_tensor("zf_s", [nmax + 16, 1024], BF16).ap()
    N1m, N2m = FACT[nmax + 16]
    bs_s = nc.dram_tensor("bs_s", [N1m * N2m * 1024], BF16).ap()
    qt_s = nc.dram_tensor("qt_s", [nmax // 128, 128, 512], BF16).ap()
    ot_s = nc.dram_tensor("ot_s", [nmax // 128, 128, 512], BF16).ap()
    g_s = nc.dram_tensor("g_s", [128, 16, nmax], BF16).ap()
    x1_s = [nc.dram_tensor("x1_s%d" % i, [n, D], F32).ap() for i, n in enumerate(seq_ns)]
    w1_s = nc.dram_tensor("w1_s", [D, 2 * DFF], BF16).ap()
    w2_s = nc.dram_tensor("w2_s", [DFF, D], BF16).ap()

    ARW = 53100
    with ExitStack() as st:
        arena_t = st.enter_context(nc.sbuf_tensor("arena", [128, ARW], F32))
        ps = st.enter_context(nc.psum_tensor("ps", [128, 8, 512], F32))
        block = st.enter_context(nc.Block())
        S = Sched(nc, st)
        AR = Arena(arena_t, ARW)
        psflat = ps[:, :, :].rearrange("p b f -> p (b f)")

        def psb(b):
            return ps[:, b, :].bitcast(BF16)

        sem_cache = {}
        WINK = ["win_q", "win_kv", "win_g", "win_f"]

        def dsem(name):
            if name not in sem_cache:
                sem_cache[name] = S.dsem(name)
            return sem_cache[name]

        def dma(eng, out, in_, reads=(), writes=(), sem="misc", **kw):
            S.op(eng, lambda e: e.dma_start(out=out, in_=in_, **kw), reads=reads, writes=writes, dsem=dsem(sem))

        ident = AR.alloc([128], BF16)
        ones64 = AR.alloc([64], BF16)
        eps_t = AR.alloc([1], F32)
        ebias = AR.alloc([1], F32)
        gqk = AR.alloc([640], F32)
        tmp2 = AR.alloc([2], F32)
        P_MARK = AR.top

        dma("sp", ident, ident_d[:, :], writes=["ident"], sem="c0")
        S.op("pool", lambda e: e.memset(ones64, 1.0), writes=["ones64"])
        S.op("pool", lambda e: e.memset(eps_t, EPS), writes=["eps"])
        dma("sp", gqk[:, 0:512].rearrange("p (h d) -> p h d", d=64), bass.AP(g_q_t, 0, [[0, 128], [0, 8], [1, 64]]),
            writes=["gqk"], sem="c1")
        dma("sp", gqk[:, 512:640].rearrange("p (h d) -> p h d", d=64), bass.AP(g_k_t, 0, [[0, 128], [0, 2], [1, 64]]),
            writes=["gqk2"], sem="c2")
        S.op("dve", lambda e: e.tensor_reduce(out=tmp2[:, 0:1], in_=gqk[:, 0:64], axis=AX.X, op=ALU.max,
                                              apply_absolute_value=True), reads=["gqk"], writes=["tmp2a"])
        S.op("dve", lambda e: e.tensor_reduce(out=tmp2[:, 1:2], in_=gqk[:, 512:576], axis=AX.X, op=ALU.max,
                                              apply_absolute_value=True), reads=["gqk2"], writes=["tmp2b"])
        S.op("dve", lambda e: e.tensor_tensor(out=ebias, in0=tmp2[:, 0:1], in1=tmp2[:, 1:2], op=ALU.mult),
             reads=["tmp2a", "tmp2b"], writes=["ebias"])
        S.op("dve", lambda e: e.tensor_scalar(out=ebias, in0=ebias, scalar1=-8.0, scalar2=None, op0=ALU.mult),
             reads=["ebias"], writes=["ebias"])

        def bcast_row(t, nel):
            return bass.AP(t, 0, [[0, 128], [1, nel]])

        def rms_rstd(x_ap, ntok, junk, ssq, col, key_x, key_junk, key_ssq):
            S.op("dve", lambda e: e.scalar_tensor_tensor(out=junk[:ntok], in0=x_ap, scalar=1.0, in1=x_ap,
                                                         op0=ALU.mult, op1=ALU.mult, accum_out=ssq[:ntok, col:col + 1]),
                 reads=[key_x], writes=[key_junk, key_ssq])

        def sqrt_recip(ssq_ap, scale, keys):
            S.op("act", lambda e: e.activation(out=ssq_ap, in_=ssq_ap, func=AF.Sqrt, bias=eps_t[:ssq_ap.shape[0], 0:1],
                                               scale=scale), reads=keys + ["eps"], writes=keys)
            S.op("dve", lambda e: e.reciprocal(out=ssq_ap, in_=ssq_ap), reads=keys, writes=keys)

        AR.top = P_MARK
        KT_W = (nmax // 128 + 1) * 128
        T1_WORDS = max(max(FACT[n_ + 16][1] * 3 * FACT[n_ + 16][0] for n_ in seq_ns) // 2 + 2, 10240)
        Win = AR.alloc([8, WIN_COLS], BF16)
        A_MARK = AR.top
        cs128 = AR.alloc([256], F32)
        identf = AR.alloc([128], F32)
        wfT = [AR.alloc([128], F32) for _ in range(2)]
        stg = [AR.alloc([3328], F32) for _ in range(3)]
        dma("sp", identf, identf_d[:, :], writes=["identf"], sem="c3")
        dma("sp", cs128, cs128_d[:, :], writes=["cs128"], sem="c4")

        def load_stg(c):
            dma("sp", stg[c % 3], w_in[c * 128:(c + 1) * 128, :], writes=["stg%d" % (c % 3)], sem="stg%d" % (c % 3))

        load_stg(0)
        load_stg(1)
        it = 0
        for c in range(8):
            if c + 2 < 8:
                load_stg(c + 2)
            st_ = stg[c % 3]
            sk = "stg%d" % (c % 3)
            S.op("dve", lambda e: e.tensor_copy(out=Win[:, c, 0:512].rearrange("p (g k d) -> p g k d", g=4, k=2),
                                                in_=st_[:, 0:512].rearrange("p (k g d) -> p g k d", k=2, g=4)),
                 reads=[sk], writes=["win_q"])
            S.op("pool", lambda e: e.tensor_copy(out=Win[:, c, 512:768], in_=st_[:, 512:768]), reads=[sk], writes=["win_kv"])
            S.op("act", lambda e: e.activation(out=Win[:, c, C_G:WIN_COLS], in_=st_[:, 1280:3328], func=AF.Copy),
                 reads=[sk], writes=["win_g"])
            for g in range(4):
                b = it % 2
                it += 1
                S.op("pe", lambda e: e.transpose(out=ps[:, b, 0:128], in_=st_[:, 768 + g * 128:768 + (g + 1) * 128], identity=identf),
                     reads=[sk, "identf"], writes=["ps%d" % b])
                S.op("dve", lambda e: e.tensor_copy(out=wfT[b], in_=ps[:, b, 0:128]), reads=["ps%d" % b], writes=["wfT%d" % b])
                S.op("pe", lambda e: e.matmul(ps[:, 2 + b, 0:256], lhsT=wfT[b], rhs=cs128, start=True, stop=True),
                     reads=["wfT%d" % b, "cs128"], writes=["ps%d" % (2 + b)])
                S.op("dve", lambda e: e.tensor_copy(
                    out=Win[:, c, C_F:C_G].rearrange("p (r g k) -> p r g k", r=2, g=4)[:, :, g, :],
                    in_=ps[:, 2 + b, 0:256].rearrange("p (r k) -> p r k", r=2)),
                    reads=["ps%d" % (2 + b)], writes=["win_f"])
        dma("sp", win_s[:, :], Win[:, :, :].rearrange("p c f -> p (c f)"), reads=WINK, sem="ws")
        S.fence()

        for si, n in enumerate(seq_ns):
            L = n + 16
            N1, N2 = FACT[L]
            nqb = n // 128
            nkb = nqb + 1
            nmt = n // 512
            x_d = xs[si]
            zf = zf_s[0:L, :]
            bs = bs_s[0:N1 * N2 * 1024].rearrange("(k n r c) -> k n r c", k=N1, n=N2, r=2)

            AR.top = A_MARK
            KT = AR.alloc([KT_W], BF16)
            Vt = AR.alloc([KT_W // 128, 128], BF16)
            B_MARK = AR.top
            gmix = AR.alloc([D], F32)
            xbuf = [AR.alloc([4, D], F32) for _ in range(2)]
            junk = AR.alloc([D], BF16)
            ssq = AR.alloc([8], F32)
            hb = AR.alloc([4, D], BF16)
            hT = [AR.alloc([8, 512], BF16) for _ in range(2)]
            zc = [AR.alloc([640], F32) for _ in range(2)]
            sq = AR.alloc([640], F32)
            ssh = AR.alloc([16], F32)
            qn = AR.alloc([640], F32)
            t1b = AR.alloc([640], F32)
            t2b = AR.alloc([640], F32)
            qko = [AR.alloc([640], BF16) for _ in range(2)]
            zfs = [AR.alloc([1024], BF16) for _ in range(2)]
            qTs = [AR.alloc([512], BF16) for _ in range(2)]
            gst = [AR.alloc([4, 512], BF16) for _ in range(2)]
            rt = [AR.alloc([128], F32) for _ in range(2)]

            dma("sp", gmix, bcast_row(g_mix_t, D), writes=["gmix"], sem="c6")

            mts = [(m * 512, [128] * 4, False) for m in range(nmt)] + [(n, [16], True)]

            def load_x(mi):
                t0, subs, is_meta = mts[mi]
                sl = mi % 2
                if is_meta:
                    dma("sp", xbuf[sl][0:16, 0, :], meta[:, :], writes=["x%d_0" % sl], sem="x%d" % sl)
                else:
                    dma("sp", xbuf[sl][:, :, :], x_d[t0:t0 + 512, :].rearrange("(s p) d -> p s d", p=128),
                        writes=["x%d_%d" % (sl, s) for s in range(4)], sem="x%d" % sl)

            def front_a(mi, s):
                t0, subs, is_meta = mts[mi]
                sl = mi % 2
                ntok = subs[s]
                rms_rstd(xbuf[sl][:ntok, s, :], ntok, junk, ssq, s, "x%d_%d" % (sl, s), "junk", "ssq%d" % s)
                S.op("act", lambda e: e.activation(out=ssq[:ntok, s:s + 1], in_=ssq[:ntok, s:s + 1], func=AF.Sqrt,
                                                   bias=eps_t[:ntok, 0:1], scale=1.0 / D), reads=["ssq%d" % s, "eps"], writes=["ssq%d" % s])

            def front_b(mi, s):
                t0, subs, is_meta = mts[mi]
                sl = mi % 2
                ntok = subs[s]
                S.op("dve", lambda e: e.reciprocal(out=ssq[:ntok, s:s + 1], in_=ssq[:ntok, s:s + 1]), reads=["ssq%d" % s], writes=["ssq%d" % s])
                S.op("dve", lambda e: e.scalar_tensor_tensor(
                    out=hb[:ntok, s, :], in0=xbuf[sl][:ntok, s, :], scalar=ssq[:ntok, s:s + 1], in1=gmix[:ntok],
                    op0=ALU.mult, op1=ALU.mult), reads=["x%d_%d" % (sl, s), "ssq%d" % s, "gmix"], writes=["hb%d" % s])

            def front_dve(mi, s):
                front_a(mi, s)
                front_b(mi, s)

            def front_pe(mi, s):
                t0, subs, is_meta = mts[mi]
                sl = mi % 2
                ntok = subs[s]
                for c in range(8):
                    S.op("pe", lambda e: e.transpose(
                        out=psb(4)[:, c * 128:c * 128 + ntok], in_=hb[:ntok, s, c * 128:(c + 1) * 128],
                        identity=ident[:ntok, :ntok]), reads=["hb%d" % s, "ident"], writes=["ps4"])
                S.op("dve", lambda e: e.tensor_copy(
                    out=hT[sl][:, :, s * 128:s * 128 + ntok],
                    in_=psb(4).rearrange("p (c t) -> p c t", c=8)[:, :, 0:ntok]),
                    reads=["ps4"], writes=["hT%d_%d" % (sl, s)])

            def front(mi):
                for s in range(len(mts[mi][1])):
                    front_dve(mi, s)
                    front_pe(mi, s)

            def chain_a(ntok, so, kb, is_meta):
                z = zc[so]
                S.op("dve", lambda e: e.tensor_tensor(out=sq[:ntok], in0=z[:ntok], in1=z[:ntok], op=ALU.mult),
                     reads=["zc%d" % so], writes=["sq"])
                S.op("dve", lambda e: e.tensor_reduce(out=ssh[:ntok, 0:10], in_=sq[:ntok].rearrange("p (h d) -> p h d", d=64),
                                                      axis=AX.X, op=ALU.add), reads=["sq"], writes=["ssh"])
                S.op("act", lambda e: e.activation(out=ssh[:ntok, 0:10], in_=ssh[:ntok, 0:10], func=AF.Sqrt,
                                                   bias=eps_t[:ntok, 0:1], scale=1.0 / 64), reads=["ssh", "eps"], writes=["ssh"])

            def chain_b(ntok, so, kb, is_meta):
                z = zc[so]
                S.op("dve", lambda e: e.reciprocal(out=ssh[:ntok, 0:10], in_=ssh[:ntok, 0:10]), reads=["ssh"], writes=["ssh"])
                S.op("dve", lambda e: e.tensor_tensor(
                    out=qn[:ntok].rearrange("p (h d) -> p h d", d=64), in0=z[:ntok].rearrange("p (h d) -> p h d", d=64),
                    in1=ssh[:ntok, 0:10].unsqueeze(2).broadcast_to([ntok, 10, 64]), op=ALU.mult),
                    reads=["zc%d" % so, "ssh"], writes=["qn"])
                S.op("dve", lambda e: e.tensor_tensor(out=qn[:ntok], in0=qn[:ntok], in1=gqk[:ntok], op=ALU.mult),
                     reads=["qn", "gqk", "gqk2"], writes=["qn"])
                S.op("dve", lambda e: e.tensor_tensor(
                    out=t1b[:ntok].rearrange("p (h d) -> p h d", d=64), in0=qn[:ntok].rearrange("p (h d) -> p h d", d=64),
                    in1=rt[so][:ntok, 0:64].unsqueeze(1).broadcast_to([ntok, 10, 64]), op=ALU.mult),
                    reads=["qn", "rt%d" % so], writes=["t1b"])
                for hf in range(2):
                    S.op("dve", lambda e: e.tensor_tensor(
                        out=t2b[:ntok].rearrange("p (h a b) -> p h a b", a=2, b=32)[:, :, :, hf * 16:hf * 16 + 16],
                        in0=qn[:ntok].rearrange("p (h a b) -> p h a b", a=2, b=32)[:, :, :, (1 - hf) * 16:(1 - hf) * 16 + 16],
                        in1=rt[so][:ntok, 64:128].rearrange("p (a b) -> p a b", a=2)[:, :, hf * 16:hf * 16 + 16]
                        .unsqueeze(1).broadcast_to([ntok, 10, 2, 16]), op=ALU.mult),
                        reads=["qn", "rt%d" % so], writes=["t2b%d" % hf])
                S.op("dve", lambda e: e.tensor_tensor(out=qko[so][:ntok], in0=t1b[:ntok], in1=t2b[:ntok], op=ALU.add),
                     reads=["t1b", "t2b0", "t2b1"], writes=["qko%d" % so])

            def chain(ntok, so, kb, is_meta):
                chain_a(ntok, so, kb, is_meta)
                chain_b(ntok, so, kb, is_meta)

            def chain_pe(ntok, so, kb, is_meta):
                for j in range(5):
                    S.op("pe", lambda e: e.transpose(
                        out=psb(5)[:, j * 128:j * 128 + ntok], in_=qko[so][:ntok, j * 128:(j + 1) * 128],
                        identity=ident[:ntok, :ntok]), reads=["qko%d" % so, "ident"], writes=["ps5"])
                S.op("dve", lambda e: e.tensor_copy(out=KT[:, kb * 128:kb * 128 + ntok], in_=psb(5)[:, 512:512 + ntok]),
                     reads=["ps5"], writes=["KT%d" % kb])
                if not is_meta:
                    S.op("dve", lambda e: e.tensor_copy(out=qTs[so], in_=psb(5)[:, 0:512]), reads=["ps5"], writes=["qTs%d" % so])
                    dma("sp", qt_s[kb, :, :], qTs[so], reads=["qTs%d" % so], sem="qTs%d" % so)

            load_x(0)
            if si > 0:
                for c in range(8):
                    dma("sp", Win[:, c, :], win_s[:, c * WIN_COLS:(c + 1) * WIN_COLS], writes=["winr%d" % c], sem="wl%d" % c)
            front(0)
            sub_ctr = 0
            gate_ctr = 0
            pending = None
            pending2 = None
            for mi, (t0, subs, is_meta) in enumerate(mts):
                sl = mi % 2
                if mi + 1 < len(mts):
                    load_x(mi + 1)
                ns = len(subs)
                for s, ntok in enumerate(subs):
                    kb = (t0 // 128) + s
                    so = sub_ctr % 2
                    sub_ctr += 1
                    dma("sp", rt[so][:, :], rope_d[n][kb * 128:(kb + 1) * 128, :], writes=["rt%d" % so], sem="rt%d" % so)
                    for j, (c0, c1) in enumerate([(0, 512), (512, 1024), (1024, 1536), (1536, 1792)]):
                        for c in range(8):
                            S.op("pe", lambda e: e.matmul(
                                ps[:ntok, j, 0:c1 - c0], lhsT=hT[sl][:, c, s * 128:s * 128 + ntok], rhs=Win[:, c, c0:c1],
                                start=(c == 0), stop=(c == 7)),
                                reads=["hT%d_%d" % (sl, s), "winr%d" % c] + WINK, writes=["ps%d" % j])
                    S.op("dve", lambda e: e.tensor_copy(out=zc[so][:ntok], in_=psflat[:ntok, 0:640]),
                         reads=["ps0", "ps1"], writes=["zc%d" % so])
                    S.op("dve", lambda e: e.tensor_copy(out=Vt[:ntok, kb, :], in_=psflat[:ntok, 640:768]),
                         reads=["ps1"], writes=["Vt%d" % kb])
                    S.op("act", lambda e: e.activation(out=zfs[so][:ntok], in_=psflat[:ntok, 768:1792], func=AF.Copy),
                         reads=["ps1", "ps2", "ps3"], writes=["zfs%d" % so])
                    p0 = 0 if is_meta else 16 + t0 + s * 128
                    dma("sp", zf[p0:p0 + ntok, :], zfs[so][:ntok], reads=["zfs%d" % so], sem="zfs%d" % so)
                    has_next = mi + 1 < len(mts) and s < len(mts[mi + 1][1])
                    if pending is not None:
                        chain_a(*pending)
                    if has_next:
                        front_a(mi + 1, s)
                    if not is_meta:
                        for jj in range(4):
                            j = s * 4 + jj
                            gb = 6 + (gate_ctr % 2)
                            gsl = (gate_ctr // 4) % 2
                            gate_ctr += 1
                            for c in range(8):
                                S.op("pe", lambda e: e.matmul(
                                    ps[:, gb, :], lhsT=Win[:, c, C_G + j * 128:C_G + (j + 1) * 128], rhs=hT[sl][:, c, :],
                                    start=(c == 0), stop=(c == 7)),
                                    reads=["hT%d_%d" % (sl, q) for q in range(4)] + ["winr%d" % c] + WINK, writes=["ps%d" % gb])
                            S.op("act", lambda e: e.activation(out=gst[gsl][:, jj, :], in_=ps[:, gb, :], func=AF.Sigmoid),
                                 reads=["ps%d" % gb], writes=["gst%d_%d" % (gsl, jj)])
                            if jj == 3:
                                dma("sp", g_s[:, j - 3:j + 1, t0:t0 + 512], gst[gsl][:, :, :],
                                    reads=["gst%d_%d" % (gsl, q) for q in range(4)], sem="gst%d" % gsl)
                    if has_next:
                        front_b(mi + 1, s)
                    if pending is not None:
                        chain_b(*pending)
                    if pending2 is not None:
                        chain_pe(*pending2)
                    pending2 = pending
                    pending = (ntok, so, kb, is_meta)
                    if has_next:
                        front_pe(mi + 1, s)
            if pending2 is not None:
                chain_pe(*pending2)
            chain(*pending)
            chain_pe(*pending)
            S.fence()

            AR.top = B_MARK
            qT = [AR.alloc([512], BF16) for _ in range(2)]
            pT = [AR.alloc([2, 512], BF16) for _ in range(4)]
            psum2 = [AR.alloc([2, 512], BF16) for _ in range(2)]
            rec = AR.alloc([512], F32)
            osb = AR.alloc([512], F32)
            oT = [AR.alloc([512], BF16) for _ in range(2)]
            dma("sp", qT[0], qt_s[0, :, :], writes=["qT0"], sem="qT0")
            b_top = AR.top
            AR.top = P_MARK
            T1 = AR.alloc([N2, 3, N1], BF16)
            assert AR.top <= P_MARK + T1_WORDS
            AR.top = b_top
            dma("sp", T1[:N1], t1_d[n][:, :, :, :], sem="c7")
            if si == nseq - 1:
                for c in range(8):
                    dma("pool", w1_s[c * 128:(c + 1) * 128, :], w_fi[c * 128:(c + 1) * 128, :], sem="wc1")
                for jq in range(NJ):
                    dma("pool", w2_s[jq * 128:(jq + 1) * 128, :], w_f2[jq * 128:(jq + 1) * 128, :], sem="wc2")
            its = [(qb, kb) for qb in range(nqb) for kb in range(nkb)]

            def emit_qk(i):
                qb, kb = its[i]
                nk = 128 if kb < nkb - 1 else 16
                sb = 2 * (i % 3)
                qs = qb % 2
                for h in range(2):
                    S.op("pe", lambda e, h=h, nk=nk, sb=sb, qs=qs, kb=kb: e.matmul(
                        ps[:nk, sb + h, :], lhsT=KT[h * 64:(h + 1) * 64, kb * 128:kb * 128 + nk],
                        rhs=qT[qs][h * 64:(h + 1) * 64, :], start=True, stop=True),
                        reads=["qT%d" % qs], writes=["ps%d" % (sb + h)])

            norm_q = []

            def run_norm(cnt):
                for _ in range(cnt):
                    o_, qb_, qq = norm_q.pop(0)
                    cs = slice(qq * 128, (qq + 1) * 128)
                    S.op("dve", lambda e: e.reciprocal(out=rec[:, cs], in_=rec[:, cs]), reads=["rec"], writes=["rec%d" % qq])
                    S.op("dve", lambda e: e.tensor_tensor(out=oT[o_][:, cs], in0=osb[:, cs], in1=rec[:, cs], op=ALU.mult),
                         reads=["osb", "rec%d" % qq], writes=["oT%d_%d" % (o_, qq)])
                    if qq == 3:
                        dma("sp", ot_s[qb_, :, :], oT[o_], reads=["oT%d_%d" % (o_, q) for q in range(4)], sem="oT%d" % o_)

            emit_qk(0)
            if len(its) > 1:
                emit_qk(1)
            pend_sum = None
            for i, (qb, kb) in enumerate(its):
                nk = 128 if kb < nkb - 1 else 16
                sb = 2 * (i % 3)
                pslot = i % 4
                ob = 6
                if kb == 0 and qb + 1 < nqb:
                    dma("sp", qT[(qb + 1) % 2], qt_s[qb + 1, :, :], writes=["qT%d" % ((qb + 1) % 2)], sem="qT%d" % ((qb + 1) % 2))
                if i + 2 < len(its):
                    emit_qk(i + 2)
                S.op("act", lambda e: e.activation(
                    out=pT[pslot][:nk], in_=ps[:nk, sb:sb + 2, :], func=AF.Exp, bias=EXP_SHIFT, scale=0.125),
                    reads=["ps%d" % sb, "ps%d" % (sb + 1)], writes=["pT%d" % pslot])
                first, last = (kb == 0), (kb == nkb - 1)
                for h in range(2):
                    S.op("pe", lambda e: e.matmul(
                        ps[h * 64:(h + 1) * 64, ob, :], lhsT=Vt[:nk, kb, h * 64:(h + 1) * 64], rhs=pT[pslot][:nk, h, :],
                        start=first, stop=last, tile_position=(0, h * 64)),
                        reads=["pT%d" % pslot], writes=["ps%d_%d" % (ob, h)])
                if pend_sum is not None:
                    pp2, pst = pend_sum
                    pend_sum = None
                    for h in range(2):
                        S.op("pe", lambda e: e.matmul(
                            ps[h * 64:(h + 1) * 64, ob + 1, :], lhsT=ones64[:, :], rhs=psum2[pp2][:, h, :],
                            start=pst, stop=False, tile_position=(0, h * 64)),
                            reads=["psum2_%d" % pp2], writes=["ps%d_%d" % (ob + 1, h)])
                if last:
                    for h in range(2):
                        S.op("pe", lambda e: e.matmul(
                            ps[h * 64:(h + 1) * 64, ob + 1, :], lhsT=ones64[:nk, :], rhs=pT[pslot][:nk, h, :],
                            start=False, stop=True, tile_position=(0, h * 64)),
                            reads=["pT%d" % pslot], writes=["ps%d_%d" % (ob + 1, h)])
                elif kb % 2 == 1:
                    p2 = (kb // 2) % 2
                    pprev = (i - 1) % 4
                    S.op("dve", lambda e: e.tensor_tensor(out=psum2[p2], in0=pT[pprev], in1=pT[pslot], op=ALU.add),
                         reads=["pT%d" % pprev, "pT%d" % pslot], writes=["psum2_%d" % p2])
                    pend_sum = (p2, kb == 1)
                if norm_q and (last or (kb >= 2 and kb % 2 == 0)):
                    run_norm(len(norm_q) if last else 1)
                if last:
                    os_ = qb % 2
                    S.op("dve", lambda e: e.tensor_copy(out=rec, in_=ps[:, ob + 1, :]),
                         reads=["ps%d_0" % (ob + 1), "ps%d_1" % (ob + 1)], writes=["rec"])
                    S.op("dve", lambda e: e.tensor_copy(out=osb, in_=ps[:, ob, :]),
                         reads=["ps%d_0" % ob, "ps%d_1" % ob], writes=["osb"])
                    for qq in range(4):
                        norm_q.append((os_, qb, qq))
            run_norm(len(norm_q))
            S.fence()

            AR.top = P_MARK + T1_WORDS
            YT = AR.alloc([4, L], BF16)
            Wao = AR.alloc([4, D], BF16)
            Wfo = AR.alloc([4, D], BF16)
            Wo = AR.alloc([8, D], BF16)
            C_MARK = AR.top
            for h in range(2):
                dma("pool", Wao[h * 64:(h + 1) * 64], w_ao[h * 256:(h + 1) * 256, :].rearrange("(g d) o -> d g o", d=64),
                    writes=["Wao"], sem="w3")
            dma("pool", Wfo, w_fo.rearrange("(g c) o -> c g o", c=128), writes=["Wfo"], sem="w4")
            dma("pool", Wo, w_out.rearrange("(c p) o -> p c o", p=128), writes=["Wo"], sem="w5")
            NB = 4
            T2 = AR.alloc([2, N2], BF16)
            NZ = 4
            zin = [AR.alloc([NB, 1024], BF16) for _ in range(NZ)]
            bst = [AR.alloc([NB, 2, 512], BF16) for _ in range(3)]
            dma("sp", T2[:N2], t2_d[n][:, :, :], writes=["T2"], sem="c8")
            zf3 = zf.rearrange("(a b) c -> a b c", b=N2)
            grp = [(a, min(NB, N2 - a)) for a in range(0, N2, NB)]

            def load_z(gi):
                a, nb = grp[gi]
                dma("sp", zin[gi % NZ][:N1, 0:nb, :], zf3[:, a:a + nb, :], writes=["zin%d" % (gi % NZ)], sem="zin%d" % (gi % NZ))

            for gi in range(min(NZ - 1, len(grp))):
                load_z(gi)
            ctr = 0
            for gi, (a, nb) in enumerate(grp):
                if gi + NZ - 1 < len(grp):
                    load_z(gi + NZ - 1)
                zs = gi % NZ
                bs_ = gi % 3
                for j in range(nb):
                    n2 = a + j
                    br = 2 * (ctr % 2)
                    ctr += 1
                    zr = zin[zs][:N1, j, 0:512]
                    zi = zin[zs][:N1, j, 512:1024]
                    for (bank, m0, r0, m1, r1) in [(br, 0, zr, 1, zi), (br + 1, 2, zr, 0, zi)]:
                        S.op("pe", lambda e, bank=bank, m0=m0, r0=r0, n2=n2: e.matmul(
                            ps[:N1, bank, :], lhsT=T1[:N1, n2, m0, :], rhs=r0, start=True, stop=False),
                            reads=["T1", "zin%d" % zs], writes=["ps%d" % bank])
                        S.op("pe", lambda e, bank=bank, m1=m1, r1=r1, n2=n2: e.matmul(
                            ps[:N1, bank, :], lhsT=T1[:N1, n2, m1, :], rhs=r1, start=False, stop=True),
                            reads=["T1", "zin%d" % zs], writes=["ps%d" % bank])
                    if ctr % 2 == 0:
                        S.op("dve", lambda e: e.tensor_copy(out=bst[bs_][:N1, j, :, :], in_=ps[:N1, br:br + 2, :]),
                             reads=["ps%d" % br, "ps%d" % (br + 1)], writes=["bst%d_%d" % (bs_, j)])
                    else:
                        S.op("act", lambda e: e.activation(out=bst[bs_][:N1, j, :, :], in_=ps[:N1, br:br + 2, :], func=AF.Copy),
                             reads=["ps%d" % br, "ps%d" % (br + 1)], writes=["bst%d_%d" % (bs_, j)])
                dma("sp", bs[:, a:a + nb, :, :], bst[bs_][:N1, 0:nb, :, :], reads=["bst%d_%d" % (bs_, j) for j in range(nb)],
                    sem="bst%d" % bs_)
            S.fence()
            f_top = AR.top
            AR.top = P_MARK
            oin = [AR.alloc([4, 512], BF16) for _ in range(2)]
            gins = [AR.alloc([16, 512], BF16) for _ in range(2)]
            assert AR.top <= P_MARK + T1_WORDS
            AR.top = f_top

            def load_c1(m):
                dma("sp", oin[m % 2], ot_s[4 * m:4 * m + 4].rearrange("q p f -> p q f"), writes=["oin%d" % (m % 2)], sem="oin%d" % (m % 2))
                dma("sp", gins[m % 2], g_s[:, :, m * 512:(m + 1) * 512], writes=["gin%d" % (m % 2)], sem="gin%d" % (m % 2))

            load_c1(0)
            bin_ = zin
            grp2 = [(a, min(NB, N1 - a)) for a in range(0, N1, NB)]

            def load_b(gi):
                a, nb = grp2[gi]
                dma("sp", bin_[gi % NZ][:N2, 0:nb, :], bs[a:a + nb].rearrange("k n r c -> n k (r c)"),
                    writes=["bin%d" % (gi % NZ)], sem="zin%d" % (gi % NZ))

            for gi in range(min(NZ - 1, len(grp2))):
                load_b(gi)
            ctr = 0
            for gi, (a, nb) in enumerate(grp2):
                if gi + NZ - 1 < len(grp2):
                    load_b(gi + NZ - 1)
                zs = gi % NZ
                for j in range(nb):
                    k1 = a + j
                    bank = ctr % 2
                    ctr += 1
                    for g in range(4):
                        for r in range(2):
                            S.op("pe", lambda e, g=g, r=r, j=j, zs=zs, bank=bank: e.matmul(
                                ps[:, bank, g * N2:(g + 1) * N2], lhsT=bin_[zs][:N2, j, r * 512 + g * 128:r * 512 + (g + 1) * 128],
                                rhs=T2[:N2, r, :], start=(r == 0), stop=(r == 1)),
                                reads=["T2", "bin%d" % zs], writes=["ps%d" % bank])
                    if ctr % 2 == 0:
                        S.op("dve", lambda e: e.tensor_copy(
                            out=YT[:, :, k1:k1 + N1 * (N2 - 1) + 1:N1],
                            in_=ps[:, bank, 0:4 * N2].rearrange("p (g k) -> p g k", g=4)),
                            reads=["ps%d" % bank], writes=["YT%d" % (ctr % 2)])
                    else:
                        S.op("act", lambda e: e.activation(
                            out=YT[:, :, k1:k1 + N1 * (N2 - 1) + 1:N1],
                            in_=ps[:, bank, 0:4 * N2].rearrange("p (g k) -> p g k", g=4), func=AF.Copy),
                            reads=["ps%d" % bank], writes=["YT%d" % (ctr % 2)])
            S.fence()

            AR.top = C_MARK
            ta = [AR.alloc([512], F32) for _ in range(2)]
            tb = [AR.alloc([512], F32) for _ in range(2)]
            mg = [AR.alloc([8, 512], BF16) for _ in range(2)]
            xin = [AR.alloc([D], F32) for _ in range(3)]
            x1o = [AR.alloc([D], F32) for _ in range(2)]
            xctr = 0

            def c1_merge(m):
                ms = m % 2
                t0 = m * 512
                gin = gins[ms]
                gk = "gin%d" % ms
                if m + 1 < nmt:
                    load_c1(m + 1)
                for oc in range(8):
                    ba = (oc % 2)
                    bf = 2 + (oc % 2)
                    for g in range(4):
                        S.op("pe", lambda e: e.matmul(
                            ps[:, ba, :], lhsT=Wao[:, g, oc * 128:(oc + 1) * 128],
                            rhs=oin[ms].rearrange("p q (g t) -> p q g t", g=4)[:, :, g, :],
                            start=(g == 0), stop=(g == 3)), reads=["Wao", "oin%d" % ms], writes=["ps%d" % ba])
                    for g in range(4):
                        S.op("pe", lambda e: e.matmul(
                            ps[:, bf, :], lhsT=Wfo[:, g, oc * 128:(oc + 1) * 128], rhs=YT[:, g, 16 + t0:16 + t0 + 512],
                            start=(g == 0), stop=(g == 3)), reads=["Wfo", "YT"], writes=["ps%d" % bf])
                    o2 = oc % 2
                    S.op("dve", lambda e: e.tensor_tensor(out=ta[o2], in0=ps[:, ba, :], in1=gin[:, oc, :], op=ALU.mult),
                         reads=["ps%d" % ba, gk], writes=["ta%d" % o2])
                    S.op("dve", lambda e: e.tensor_tensor(out=tb[o2], in0=ps[:, bf, :], in1=gin[:, 8 + oc, :], op=ALU.mult),
                         reads=["ps%d" % bf, gk], writes=["tb%d" % o2])
                    S.op("pool", lambda e: e.tensor_tensor(out=mg[ms][:, oc, :], in0=ta[o2], in1=tb[o2], op=ALU.add),
                         reads=["ta%d" % o2, "tb%d" % o2], writes=["mg%d_%d" % (ms, oc)])

            c1_merge(0)
            for m in range(nmt):
                ms = m % 2
                t0 = m * 512
                if m + 1 < nmt:
                    c1_merge(m + 1)
                for s in range(4):
                    xsl = xctr % 3
                    osl = xctr % 2
                    xctr += 1
                    tt = t0 + s * 128
                    dma("sp", xin[xsl], x_d[tt:tt + 128, :], writes=["xin%d" % xsl], sem="xin%d" % xsl)
                    pb = 4 + 2 * (s % 2)
                    for hh in range(2):
                        for oc in range(8):
                            S.op("pe", lambda e: e.matmul(
                                ps[:, pb + hh, :], lhsT=mg[ms][:, oc, s * 128:(s + 1) * 128], rhs=Wo[:, oc, hh * 512:(hh + 1) * 512],
                                start=(oc == 0), stop=(oc == 7)), reads=["mg%d_%d" % (ms, oc), "Wo"], writes=["ps%d" % (pb + hh)])
                    S.op("dve", lambda e: e.tensor_tensor(
                        out=x1o[osl], in0=xin[xsl], in1=psflat[:, pb * 512:(pb + 2) * 512], op=ALU.add),
                        reads=["xin%d" % xsl, "ps%d" % pb, "ps%d" % (pb + 1)], writes=["x1o%d" % osl])
                    dma("sp", x1_s[si][tt:tt + 128, :], x1o[osl], reads=["x1o%d" % osl], sem="x1o%d" % osl)
            S.fence()

        AR.top = P_MARK
        W1 = AR.alloc([8, 2 * DFF], BF16)
        W2 = AR.alloc([NJ, D], BF16)
        gfin = AR.alloc([D], F32)
        gffn = AR.alloc([D], F32)
        xn = [AR.alloc([D], F32) for _ in range(4)]
        xr = [AR.alloc([D], F32) for _ in range(2)]
        ssq = AR.alloc([8], F32)
        hb2 = [AR.alloc([D], BF16) for _ in range(2)]
        hT2 = AR.alloc([8, 512], BF16)
        sg = [AR.alloc([512], BF16) for _ in range(2)]
        actT = AR.alloc([NJ, 512], BF16)
        yo = AR.alloc([D], F32)
        dma("sp", gffn, bcast_row(g_ffn_t, D), writes=["gffn"], sem="c6")
        dma("sp", gfin, bcast_row(g_fin_t, D), writes=["gfin"], sem="c7")

        tiles = [(si, m) for si, n in enumerate(seq_ns) for m in range(n // 512)]

        def load_xn(ti):
            si, m = tiles[ti]
            for s in range(4):
                tt = m * 512 + s * 128
                dma("sp", xn[s], x1_s[si][tt:tt + 128, :], writes=["xn%d" % s], sem="xn%d" % s)

        def front2_dve(s):
            hs = s % 2
            rms_rstd(xn[s][:, :], 128, hb2[hs], ssq, s, "xn%d" % s, "hb2_%d" % hs, "ssqn%d" % s)
            sqrt_recip(ssq[:, s:s + 1], 1.0 / D, ["ssqn%d" % s])
            S.op("dve", lambda e: e.scalar_tensor_tensor(out=hb2[hs], in0=xn[s], scalar=ssq[:, s:s + 1], in1=gffn, op0=ALU.mult, op1=ALU.mult),
                 reads=["xn%d" % s, "ssqn%d" % s, "gffn"], writes=["hb2_%d" % hs])

        def front2_pe(s):
            hs = s % 2
            for c in range(8):
                S.op("pe", lambda e: e.transpose(out=psb(7)[:, c * 128:(c + 1) * 128], in_=hb2[hs][:, c * 128:(c + 1) * 128],
                                                 identity=ident), reads=["hb2_%d" % hs, "ident"], writes=["ps7"])
            S.op("dve", lambda e: e.tensor_copy(out=hT2[:, :, s * 128:(s + 1) * 128], in_=psb(7).rearrange("p (c t) -> p c t", c=8)),
                 reads=["ps7"], writes=["hT2_%d" % s])

        load_xn(0)
        NBLK = (NJ + 3) // 4
        w1v = w1_s.rearrange("(c p) f -> p c f", p=128)
        for bk in range(NBLK):
            j0, j1 = bk * 4, min(NJ, bk * 4 + 4)
            for half in range(2):
                c0 = half * DFF + j0 * 128
                c1 = half * DFF + j1 * 128
                dma("sp", W1[:, :, c0:c1], w1v[:, :, c0:c1], writes=["W1b_%d" % bk], sem="w6_%d" % bk)
        for jq in range(0, NJ, 2):
            dma("sp", W2[:, jq:jq + 2, :], w2_s[jq * 128:(jq + 2) * 128, :].rearrange("(c p) o -> p c o", p=128),
                writes=["W2"], sem="w7")
        for s in range(4):
            front2_dve(s)
            front2_pe(s)
        fctr = 0
        octr = 0
        rctr = 0
        hkeys = ["hT2_%d" % s for s in range(4)]
        for ti, (si, m) in enumerate(tiles):
            t0 = m * 512
            x1d = x1_s[si]
            if ti + 1 < len(tiles):
                load_xn(ti + 1)
            for j in range(NJ):
                bg = 2 * (fctr % 2)
                ss_ = fctr % 2
                fctr += 1
                for (bank, col0) in [(bg, j * 128), (bg + 1, DFF + j * 128)]:
                    for c in range(8):
                        S.op("pe", lambda e: e.matmul(ps[:, bank, :], lhsT=W1[:, c, col0:col0 + 128], rhs=hT2[:, c, :],
                                                      start=(c == 0), stop=(c == 7)),
                             reads=hkeys + ["W1b_%d" % (j // 4)], writes=["ps%d" % bank])
                S.op("act", lambda e: e.activation(out=sg[ss_], in_=ps[:, bg, :], func=AF.Silu), reads=["ps%d" % bg], writes=["sg%d" % ss_])
                S.op("dve", lambda e: e.tensor_tensor(out=actT[:, j, :], in0=ps[:, bg + 1, :], in1=sg[ss_], op=ALU.mult),
                     reads=["ps%d" % (bg + 1), "sg%d" % ss_], writes=["actT%d" % j])
            for s in range(4):
                tt = t0 + s * 128
                rs = rctr % 2
                rctr += 1
                dma("sp", xr[rs], x1d[tt:tt + 128, :], writes=["xr%d" % rs], sem="xr%d" % rs)
                if ti + 1 < len(tiles):
                    front2_dve(s)
                banks = []
                for hh in range(2):
                    bank = 4 + (octr % 3)
                    octr += 1
                    banks.append(bank)
                    for j in range(NJ):
                        S.op("pe", lambda e: e.matmul(ps[:, bank, :], lhsT=actT[:, j, s * 128:(s + 1) * 128],
                                                      rhs=W2[:, j, hh * 512:(hh + 1) * 512], start=(j == 0), stop=(j == NJ - 1)),
                             reads=["actT%d" % j, "W2"], writes=["ps%d" % bank])
                if ti + 1 < len(tiles):
                    front2_pe(s)
                for hh in range(2):
                    S.op("dve", lambda e: e.tensor_tensor(out=xr[rs][:, hh * 512:(hh + 1) * 512], in0=xr[rs][:, hh * 512:(hh + 1) * 512],
                                                          in1=ps[:, banks[hh], :], op=ALU.add),
                         reads=["xr%d" % rs, "ps%d" % banks[hh]], writes=["xr%d" % rs])
                rms_rstd(xr[rs][:, :], 128, yo, ssq, 4, "xr%d" % rs, "yo", "ssq4")
                sqrt_recip(ssq[:, 4:5], 1.0 / D, ["ssq4"])
                S.op("dve", lambda e: e.scalar_tensor_tensor(out=yo, in0=xr[rs], scalar=ssq[:, 4:5], in1=gfin, op0=ALU.mult, op1=ALU.mult),
                     reads=["xr%d" % rs, "ssq4", "gfin"], writes=["yo"])
                dma("sp", ys[si][tt:tt + 128, :], yo, reads=["yo"], sem="yo")
        S.fence()
        S.emit(block)
    return nc


SEQ_NS = [2048, 2048, 8192]
_CACHE = {}


def _get_program(seq_ns):
    key = tuple(seq_ns)
    if key not in _CACHE:
        _CACHE[key] = (build(list(seq_ns)), host_consts(list(seq_ns)))
    return _CACHE[key]


def run_cores(seq_ns, per_core_x, weights):
    nc, consts = _get_program(seq_ns)
    in_maps = []
    for xl in per_core_x:
        m = dict(consts)
        m.update(weights)
        for i, x in enumerate(xl):
            m["x%d" % i] = np.ascontiguousarray(x, dtype=np.float32)
        in_maps.append(m)
    res = run_bass_kernel_spmd(nc, in_maps, core_ids=list(range(len(in_maps))))
    return [[r["y%d" % i] for i in range(len(seq_ns))] for r in res.results]


def kernel(x_prompt, x_sample, meta_tokens, norm_mix_g, w_in, q_norm_g, k_norm_g, w_attn_o, w_four_o,
           w_out, norm_ffn_g, w_ffn_in, w_ffn_out, final_norm_g):
    f = lambda a: np.ascontiguousarray(np.asarray(a), dtype=np.float32)
    weights = {
        "meta_tokens": f(meta_tokens), "norm_mix_g": f(norm_mix_g).reshape(-1), "q_norm_g": f(q_norm_g).reshape(-1),
        "k_norm_g": f(k_norm_g).reshape(-1), "norm_ffn_g": f(norm_ffn_g).reshape(-1), "final_norm_g": f(final_norm_g).reshape(-1),
        "w_in": f(w_in)[0], "w_attn_o": f(w_attn_o)[0], "w_four_o": f(w_four_o)[0], "w_out": f(w_out)[0],
        "w_ffn_in": f(w_ffn_in)[0], "w_ffn_out": f(w_ffn_out)[0],
    }
    xp = np.asarray(x_prompt)
    xsm = np.asarray(x_sample)
    per_core = [[xp[2 * c], xp[2 * c + 1], xsm[c]] for c in range(8)]
    outs = run_cores(SEQ_NS, per_core, weights)
    y_prompt = np.stack([outs[c][k] for c in range(8) for k in range(2)], 0).astype(np.float32)
    y_sample = np.stack([outs[c][2] for c in range(8)], 0).astype(np.float32)
    return (y_prompt, y_sample)
```
